# Optimizing a Trainium2 kernel written in Bass

```python
import math
import jax, jax.numpy as jnp
from jax import lax
import numpy as np

D_MODEL = 1024
BATCH = 4
SEQ = 4096
DEPTH = 2

N_A_LAYERS = DEPTH // 2
N_B_LAYERS = DEPTH - N_A_LAYERS
D_FF = 2816
MOBA_HEADS = 16
MOBA_HEAD_DIM = D_MODEL // MOBA_HEADS
MOBA_BLOCK = 256
MOBA_TOPK = 3
MOBA_Q_CHUNK = 64
REL_BUCKETS = 32
REL_MAX_DIST = 128
MLA_HEADS = 16
MLA_Q_LORA = 384
MLA_KV_LORA = 256
MLA_NOPE = 64
MLA_ROPE = 32
MLA_V = 64
MLA_QK = MLA_NOPE + MLA_ROPE
ATTN_Q_BLOCK = 128
ROPE_BASE = 10000.0
EPS = 1e-6
NEG = -1e30

kernel_name = "yoco_moba_mla_macaron_adaln"


def rms_norm(x, g):
    xf = x.astype(jnp.float32)
    y = xf * lax.rsqrt(jnp.mean(xf * xf, axis=-1, keepdims=True) + EPS)
    return (y * g.astype(jnp.float32)).astype(x.dtype)


def modulate(h, shift, scale):
    return h * (1 + scale[:, None, :]) + shift[:, None, :]


def swiglu(h, w_gate, w_up, w_down):
    return (jax.nn.silu(h @ w_gate) * (h @ w_up)) @ w_down


def t5_bucket(rel):
    n = jnp.maximum(rel, 0)
    max_exact = REL_BUCKETS // 2
    nf = jnp.maximum(n, 1).astype(jnp.float32)
    large = max_exact + (jnp.log(nf / max_exact) / math.log(REL_MAX_DIST / max_exact)
                         * (REL_BUCKETS - max_exact)).astype(jnp.int32)
    large = jnp.minimum(large, REL_BUCKETS - 1)
    return jnp.where(n < max_exact, n, large)


def rope(x, positions):
    half = x.shape[-1] // 2
    inv = ROPE_BASE ** (-jnp.arange(half, dtype=jnp.float32) / half)
    ang = positions.astype(jnp.float32)[..., None] * inv
    cos = jnp.cos(ang)[:, :, None, :]
    sin = jnp.sin(ang)[:, :, None, :]
    x1 = x[..., :half].astype(jnp.float32)
    x2 = x[..., half:].astype(jnp.float32)
    out = jnp.concatenate([x1 * cos - x2 * sin, x2 * cos + x1 * sin], axis=-1)
    return out.astype(x.dtype)


def moba_attention(h, w_qkv, q_g, k_g, w_o, rel_bias, positions):
    B, S, _ = h.shape
    H, Dh, L = MOBA_HEADS, MOBA_HEAD_DIM, MOBA_BLOCK
    qkv = (h @ w_qkv).reshape(B, S, 3, H, Dh)
    q = rms_norm(qkv[:, :, 0], q_g).transpose(0, 2, 1, 3)
    k = rms_norm(qkv[:, :, 1], k_g)
    v = qkv[:, :, 2]
    nb = -(-S // L)
    pad = nb * L - S
    kb = jnp.pad(k, ((0, 0), (0, pad), (0, 0), (0, 0))).reshape(B, nb, L, H, Dh).transpose(0, 3, 1, 2, 4)
    vb = jnp.pad(v, ((0, 0), (0, pad), (0, 0), (0, 0))).reshape(B, nb, L, H, Dh).transpose(0, 3, 1, 2, 4)
    pos_p = jnp.pad(positions, ((0, 0), (0, pad)), mode="edge")
    kmean = jnp.mean(kb.astype(jnp.float32), axis=3)
    gate = jnp.einsum("bhsd,bhnd->bhsn", q.astype(jnp.float32), kmean)
    q_index = jnp.arange(S)
    q_blk = q_index // L
    past = jnp.arange(nb)[None, :] < q_blk[:, None]
    gate = jnp.where(past, gate, -jnp.inf)
    k_sel = min(MOBA_TOPK, nb)
    _, sel = lax.top_k(gate, k_sel)
    own = jnp.broadcast_to(q_blk[None, None, :, None], (B, H, S, 1)).astype(sel.dtype)
    idx = jnp.concatenate([sel, own], axis=-1)
    nslot = k_sel + 1
    slot_valid = jnp.concatenate([jnp.arange(k_sel)[None, :] < q_blk[:, None],
                                  jnp.ones((S, 1), dtype=bool)], axis=-1)
    QC = MOBA_Q_CHUNK
    nc = S // QC
    qs = q.reshape(B, H, nc, QC, Dh).transpose(2, 0, 1, 3, 4)
    idxs = idx.reshape(B, H, nc, QC, nslot).transpose(2, 0, 1, 3, 4)
    qis = q_index.reshape(nc, QC)
    qps = positions.reshape(B, nc, QC).transpose(1, 0, 2)
    svs = slot_valid.reshape(nc, QC, nslot)
    b4 = jnp.arange(B)[:, None, None, None]
    h4 = jnp.arange(H)[None, :, None, None]
    b5 = b4[..., None]
    h5 = h4[..., None]
    bias_t = rel_bias.T
    scale = Dh ** -0.5

    def chunk(args):
        qc, ic, qi, qp, sv = args
        kg = kb[b4, h4, ic]
        vg = vb[b4, h4, ic]
        kidx = ic[..., None] * L + jnp.arange(L)
        kpos = pos_p[b5, kidx]
        bias = bias_t[h5, t5_bucket(qp[:, None, :, None, None] - kpos)]
        s = jnp.einsum("bhqd,bhqnkd->bhqnk", qc, kg).astype(jnp.float32) * scale + bias.astype(jnp.float32)
        mask = (kidx <= qi[None, None, :, None, None]) & sv[None, None, :, :, None]
        s = jnp.where(mask, s, NEG)
        p = jax.nn.softmax(s.reshape(B, H, QC, nslot * L), axis=-1).reshape(s.shape).astype(vg.dtype)
        return jnp.einsum("bhqnk,bhqnkd->bhqd", p, vg)

    o = lax.map(chunk, (qs, idxs, qis, qps, svs))
    o = o.transpose(1, 0, 3, 2, 4).reshape(B, S, H * Dh)
    return o @ w_o


def causal_block_attention(q, k, v, scale):
    B, S, H, D = q.shape
    nq = S // ATTN_Q_BLOCK
    qb = q.reshape(B, nq, ATTN_Q_BLOCK, H, D).transpose(1, 0, 2, 3, 4)
    kpos = jnp.arange(S)

    def one(args):
        qi, start = args
        s = jnp.einsum("bqhd,bkhd->bhqk", qi, k).astype(jnp.float32) * scale
        mask = kpos[None, :] <= (start + jnp.arange(ATTN_Q_BLOCK))[:, None]
        s = jnp.where(mask, s, NEG)
        p = jax.nn.softmax(s, axis=-1).astype(v.dtype)
        return jnp.einsum("bhqk,bkhd->bqhd", p, v)

    o = lax.map(one, (qb, jnp.arange(nq) * ATTN_Q_BLOCK))
    return o.transpose(1, 0, 2, 3, 4).reshape(B, S, H, v.shape[-1])


def mla_shared_kv(x, c_act, kv_ada_w, kv_ada_b, kv_norm_g, w_dkv, kv_a_norm_g, w_uk, w_uv, k_g, positions):
    B, S, _ = x.shape
    shift, scale = jnp.split(c_act @ kv_ada_w + kv_ada_b, 2, axis=-1)
    hs = modulate(rms_norm(x, kv_norm_g), shift, scale)
    ckv_kpe = hs @ w_dkv
    ckv = rms_norm(ckv_kpe[..., :MLA_KV_LORA], kv_a_norm_g)
    kpe = ckv_kpe[..., MLA_KV_LORA:]
    k_nope = (ckv @ w_uk).reshape(B, S, MLA_HEADS, MLA_NOPE)
    v = (ckv @ w_uv).reshape(B, S, MLA_HEADS, MLA_V)
    k = jnp.concatenate([k_nope, jnp.broadcast_to(kpe[:, :, None, :], (B, S, MLA_HEADS, MLA_ROPE))], axis=-1)
    k = rms_norm(k, k_g)
    k = jnp.concatenate([k[..., :MLA_NOPE], rope(k[..., MLA_NOPE:], positions)], axis=-1)
    return k, v


def mla_attention(h, w_dq, q_a_norm_g, w_uq, q_g, w_o, k, v, positions):
    B, S, _ = h.shape
    q = (rms_norm(h @ w_dq, q_a_norm_g) @ w_uq).reshape(B, S, MLA_HEADS, MLA_QK)
    q = rms_norm(q, q_g)
    q = jnp.concatenate([q[..., :MLA_NOPE], rope(q[..., MLA_NOPE:], positions)], axis=-1)
    o = causal_block_attention(q, k, v, MLA_QK ** -0.5)
    return o.reshape(B, S, MLA_HEADS * MLA_V) @ w_o


def setup_inputs(seed: int = 0) -> dict:
    key = jax.random.key(seed)
    ks = iter(jax.random.split(key, 32))
    D, F = D_MODEL, D_FF
    NA, NB = N_A_LAYERS, N_B_LAYERS

    def nrm(shape, fan_in, mult=1.0):
        return jax.random.normal(next(ks), shape, jnp.float32) * (mult * fan_in ** -0.5)

    def gain(shape):
        return 1.0 + 0.02 * jax.random.normal(next(ks), shape, jnp.float32)

    def small(shape, s=0.02):
        return s * jax.random.normal(next(ks), shape, jnp.float32)

    x = jax.random.normal(next(ks), (BATCH, SEQ, D), jnp.float32)
    c = jax.random.normal(next(ks), (BATCH, D), jnp.float32)
    offsets = jax.random.randint(next(ks), (BATCH,), 0, 1024, dtype=jnp.int32)
    positions = offsets[:, None] + jnp.arange(SEQ, dtype=jnp.int32)[None, :]
    return {
        "x": x,
        "c": c,
        "positions": positions,
        "ada_w": nrm((DEPTH, D, 9 * D), D, 0.5),
        "ada_b": small((DEPTH, 9 * D)),
        "norm_g": gain((DEPTH, 3, D)),
        "ffn_w_gate": nrm((DEPTH, 2, D, F), D),
        "ffn_w_up": nrm((DEPTH, 2, D, F), D),
        "ffn_w_down": nrm((DEPTH, 2, F, D), F),
        "rel_bias": small((REL_BUCKETS, MOBA_HEADS), 0.5),
        "moba_w_qkv": nrm((NA, D, 3 * MOBA_HEADS * MOBA_HEAD_DIM), D),
        "moba_q_g": gain((NA, MOBA_HEAD_DIM)),
        "moba_k_g": gain((NA, MOBA_HEAD_DIM)),
        "moba_w_o": nrm((NA, MOBA_HEADS * MOBA_HEAD_DIM, D), MOBA_HEADS * MOBA_HEAD_DIM),
        "kv_ada_w": nrm((D, 2 * D), D, 0.5),
        "kv_ada_b": small((2 * D,)),
        "kv_norm_g": gain((D,)),
        "w_dkv": nrm((D, MLA_KV_LORA + MLA_ROPE), D),
        "kv_a_norm_g": gain((MLA_KV_LORA,)),
        "w_uk": nrm((MLA_KV_LORA, MLA_HEADS * MLA_NOPE), MLA_KV_LORA),
        "w_uv": nrm((MLA_KV_LORA, MLA_HEADS * MLA_V), MLA_KV_LORA),
        "mla_k_g": gain((MLA_QK,)),
        "mla_w_dq": nrm((NB, D, MLA_Q_LORA), D),
        "mla_q_a_norm_g": gain((NB, MLA_Q_LORA)),
        "mla_w_uq": nrm((NB, MLA_Q_LORA, MLA_HEADS * MLA_QK), MLA_Q_LORA),
        "mla_q_g": gain((NB, MLA_QK)),
        "mla_w_o": nrm((NB, MLA_HEADS * MLA_V, D), MLA_HEADS * MLA_V),
    }


def reference(x, c, positions, ada_w, ada_b, norm_g, ffn_w_gate, ffn_w_up, ffn_w_down, rel_bias,
              moba_w_qkv, moba_q_g, moba_k_g, moba_w_o, kv_ada_w, kv_ada_b, kv_norm_g, w_dkv,
              kv_a_norm_g, w_uk, w_uv, mla_k_g, mla_w_dq, mla_q_a_norm_g, mla_w_uq, mla_q_g, mla_w_o):
    c_act = jax.nn.silu(c)
    shared_k = None
    shared_v = None
    for layer in range(DEPTH):
        mods = c_act @ ada_w[layer] + ada_b[layer]
        sh1, sc1, g1, sh2, sc2, g2, sh3, sc3, g3 = jnp.split(mods, 9, axis=-1)
        h = modulate(rms_norm(x, norm_g[layer, 0]), sh1, sc1)
        x = x + 0.5 * g1[:, None, :] * swiglu(h, ffn_w_gate[layer, 0], ffn_w_up[layer, 0], ffn_w_down[layer, 0])
        h = modulate(rms_norm(x, norm_g[layer, 1]), sh2, sc2)
        if layer < N_A_LAYERS:
            mix = moba_attention(h, moba_w_qkv[layer], moba_q_g[layer], moba_k_g[layer],
                                 moba_w_o[layer], rel_bias, positions)
        else:
            if layer == N_A_LAYERS:
                shared_k, shared_v = mla_shared_kv(x, c_act, kv_ada_w, kv_ada_b, kv_norm_g, w_dkv,
                                                   kv_a_norm_g, w_uk, w_uv, mla_k_g, positions)
            j = layer - N_A_LAYERS
            mix = mla_attention(h, mla_w_dq[j], mla_q_a_norm_g[j], mla_w_uq[j], mla_q_g[j],
                                mla_w_o[j], shared_k, shared_v, positions)
        x = x + g2[:, None, :] * mix
        h = modulate(rms_norm(x, norm_g[layer, 2]), sh3, sc3)
        x = x + 0.5 * g3[:, None, :] * swiglu(h, ffn_w_gate[layer, 1], ffn_w_up[layer, 1], ffn_w_down[layer, 1])
    return x
```

```python
from contextlib import ExitStack
import numpy as np
import concourse.bass as bass
import concourse.mybir as mybir

F32 = mybir.dt.float32
BF16 = mybir.dt.bfloat16
I32 = mybir.dt.int32
ALU = mybir.AluOpType
AF = mybir.ActivationFunctionType
AX = mybir.AxisListType

ENGS = ("pe", "act", "dve", "pool", "sp")


class Reg:
    registry = {}

    def __new__(cls, name=""):
        r = cls.registry.get(name)
        if r is None:
            r = object.__new__(cls)
            r.name = name
            r.writers = []
            r.readers = []
            r.sem = None
            r.dcount = 0
            cls.registry[name] = r
        return r


class Op:
    __slots__ = ("eng", "fn", "deps", "needs_inc", "val", "sem", "is_dma", "pos", "default_inc", "dbg")

    def __init__(self, eng, fn):
        self.eng = eng
        self.fn = fn
        self.deps = {}
        self.needs_inc = False
        self.val = None
        self.sem = None
        self.is_dma = False
        self.pos = 0
        self.default_inc = False


class Prog:
    def __init__(self, nc, es):
        Reg.registry.clear()
        self.nc = nc
        self.es = es
        self.ops = {e: [] for e in ENGS}
        self.n = 0
        self.esem = {}
        self.nsem = 0
        for e in ENGS:
            self.esem[e] = self.newsem("e_" + e)

    def newsem(self, name):
        self.nsem += 1
        return self.es.enter_context(self.nc.semaphore(name + "_%d" % self.nsem))

    def _key(self, op):
        return op.sem if op.is_dma else op.eng

    def _same(self, a, b):
        if a.is_dma != b.is_dma:
            return False
        if a.is_dma:
            return a.sem is b.sem
        return a.eng == b.eng

    def _adddep(self, op, d):
        if d is op:
            return
        if (not d.is_dma) and d.eng == op.eng and not op.is_dma:
            if op.eng == "pe":
                return
        k = id(d.sem) if d.is_dma else d.eng
        cur = op.deps.get(k)
        if cur is None or cur.pos < d.pos:
            op.deps[k] = d

    def _track(self, op, r, w, pw, same_eng_war=False):
        for x in r:
            for d in x.writers:
                self._adddep(op, d)
        for x in w:
            for d in x.writers:
                self._adddep(op, d)
            for d in x.readers:
                if d.eng == op.eng and not d.is_dma and not op.is_dma:
                    continue
                self._adddep(op, d)
        for x in pw:
            if x.readers:
                for d in x.readers:
                    if d.eng == op.eng and not d.is_dma and not op.is_dma:
                        continue
                    self._adddep(op, d)
        for d in op.deps.values():
            d.needs_inc = True
        for x in r:
            x.readers = [q for q in x.readers if not self._same(q, op)] + [op]
        for x in w:
            x.writers = [op]
            x.readers = []
        for x in pw:
            if x.readers:
                x.writers = []
                x.readers = []
            x.writers = [q for q in x.writers if not self._same(q, op)] + [op]

    def op(self, eng, fn, r=(), w=(), pw=()):
        o = Op(eng, fn)
        self.n += 1
        o.pos = self.n
        o.sem = self.esem[eng]
        self._track(o, r, w, pw)
        self.ops[eng].append(o)
        return o

    def wait(self, eng, r=()):
        o = Op(eng, lambda e: None)
        self.n += 1
        o.pos = self.n
        o.sem = self.esem[eng]
        for x in r:
            for d in x.writers:
                self._adddep(o, d)
        for d in o.deps.values():
            d.needs_inc = True
        self.ops[eng].append(o)
        return o

    def dma(self, q, out, in_, r=(), w=(), pw=()):
        dst = (list(w) + list(pw))[0]
        if dst.sem is None:
            dst.sem = self.newsem("d_" + dst.name)
        def _ap(a, e):
            return a(e) if callable(a) else a
        o = Op(q, lambda e: e.dma_start(out=_ap(out, e), in_=_ap(in_, e)))
        o.is_dma = True
        o.dbg = dst.name
        self.n += 1
        o.pos = self.n
        o.sem = dst.sem
        self._track(o, r, w, pw)
        dst.dcount += 16
        o.val = dst.dcount
        o.needs_inc = True
        self.ops[q].append(o)
        return o

    def cc(self, kind, groups, in_ap, out_ap, r=(), w=(), pw=()):
        dst = (list(w) + list(pw))[0]
        sem = self.newsem("cc_" + dst.name)
        o = Op("pool", lambda e: e.collective_compute(kind, ALU.bypass, replica_groups=groups, ins=[in_ap], outs=[out_ap]))
        o.is_dma = True
        o.default_inc = True
        self.n += 1
        o.pos = self.n
        o.sem = sem
        for x in r:
            for d in x.writers:
                self._adddep(o, d)
        self._track(o, (), w, pw)
        o.val = 1
        o.needs_inc = True
        self.ops["pool"].append(o)
        return o

    def final_waits(self):
        out = []
        seen = set()
        for e in ENGS:
            for o in self.ops[e]:
                if o.is_dma:
                    seen.add(id(o.sem))
                    out = [x for x in out if x[0] is not o.sem] + [(o.sem, o.val)]
        return out

    def emit(self):
        nc = self.nc
        for e in ENGS:
            c = 0
            for o in self.ops[e]:
                if o.is_dma:
                    continue
                if o.needs_inc:
                    c += 1
                    o.val = c
        with nc.Block() as block:
            def run(eng_name, eh):
                waited = {}
                for o in self.ops[eng_name]:
                    for d in o.deps.values():
                        k = id(d.sem)
                        if waited.get(k, 0) < d.val:
                            eh.wait_ge(d.sem, d.val)
                            waited[k] = d.val
                    try:
                        ins = o.fn(eh)
                    except Exception:
                        print("EMIT FAIL", eng_name, getattr(o, "dbg", None), o.pos)
                        raise
                    if o.needs_inc:
                        if ins is None:
                            raise RuntimeError("op without instruction needs inc")
                        if o.default_inc:
                            ins.then_inc(o.sem)
                        else:
                            ins.then_inc(o.sem, 16 if o.is_dma else 1)

            @block.tensor
            def _(eh):
                run("pe", eh)

            @block.scalar
            def _(eh):
                run("act", eh)

            @block.vector
            def _(eh):
                run("dve", eh)

            @block.gpsimd
            def _(eh):
                run("pool", eh)

            @block.sync
            def _(eh):
                run("sp", eh)
                for sem, cnt in self.final_waits():
                    eh.wait_ge(sem, cnt)

NTOK = 2048
NT = 4
D = 1024
DC = 8
FF = 2816
EPS = 1e-6
FGROUPS = [(0, 512), (512, 512), (1024, 512), (1536, 512), (2048, 512), (2560, 256)]
NV = 20


import itertools
_UID = itertools.count()


class Env:
    pass


def alloc_common(nc, es, P):
    E = Env()
    E.nc, E.es, E.P = nc, es, P
    sb = lambda n, s, d: es.enter_context(nc.sbuf_tensor("s%d_" % next(_UID) + n, s, d))
    E.sb = sb
    E.ps = [es.enter_context(nc.psum_tensor("ps%d" % i, [128, 512], F32)) for i in range(8)]
    E.psR = [Reg("ps%d" % i) for i in range(8)]
    E.xT = sb("xT", [128, DC, NTOK], F32)
    E.xR = [Reg("xT%d" % t) for t in range(NT)]
    E.dv = sb("dv", [128, 256], F32)
    E.dvR = Reg("dv")
    E.ones = sb("ones", [128, 128], BF16)
    E.onesR = Reg("ones")
    E.one1 = sb("one1", [128, 128], BF16)
    E.one1R = Reg("one1")
    E.epsc = sb("epsc", [128, 1], F32)
    E.epsR = Reg("epsc")
    P.op("pool", lambda e: e.memset(E.ones[:], 1.0 / 1024), w=[E.onesR])
    P.op("pool", lambda e: e.memset(E.one1[:], 1.0), w=[E.one1R])
    P.op("pool", lambda e: e.memset(E.epsc[:], EPS), w=[E.epsR])
    return E


def dvcol(i):
    return slice(8 * i, 8 * i + 8)


def phase0_mods(E, cT, ada_w, ada_bT, norm_gT, kv_ada_w, kv_ada_bT, kv_norm_gT):
    nc, P, sb = E.nc, E.P, E.sb
    with ExitStack() as es2:
        sb2 = lambda n, s, d: es2.enter_context(nc.sbuf_tensor("s%d_" % next(_UID) + n, s, d))
        cs = sb2("p0_c", [128, 8], F32)
        cact = sb2("p0_cact", [128, 8], BF16)
        bT = sb2("p0_b", [128, 160], F32)
        gT = sb2("p0_g", [128, 56], F32)
        raw = sb2("p0_raw", [128, 160], F32)
        wp = [sb2("p0_w%d" % i, [128, 8, 1024], BF16) for i in range(2)]
        wR = [Reg("p0w%d" % i) for i in range(2)]
        cR, caR, bR, gR, rawR = Reg("p0c"), Reg("p0ca"), Reg("p0b"), Reg("p0g"), Reg("p0raw")
        P.dma("sp", cs[:], cT, w=[cR])
        P.dma("sp", bT[:, 0:72], ada_bT[0], pw=[bR])
        P.dma("sp", bT[:, 72:144], ada_bT[1], pw=[bR])
        P.dma("sp", bT[:, 144:160], kv_ada_bT, pw=[bR])
        P.dma("sp", gT[:, 0:24], norm_gT[0], pw=[gR])
        P.dma("sp", gT[:, 24:48], norm_gT[1], pw=[gR])
        P.dma("sp", gT[:, 48:56], kv_norm_gT, pw=[gR])
        P.op("act", lambda e: e.activation(cact[:], cs[:], AF.Silu), r=[cR], w=[caR])
        pieces = []
        for L in range(2):
            for v in range(9):
                pieces.append((ada_w[L].rearrange("(k p) f -> p k f", p=128)[:, :, v * 1024:(v + 1) * 1024], 9 * L + v))
        for v in range(2):
            pieces.append((kv_ada_w.rearrange("(k p) f -> p k f", p=128)[:, :, v * 1024:(v + 1) * 1024], 18 + v))
        psb = E.ps[7]
        for n, (src, vi) in enumerate(pieces):
            b = n % 2
            P.dma("pool", wp[b][:], src, w=[wR[b]])
            pst = E.ps[6 + (n % 2)]
            psr = E.psR[6 + (n % 2)]
            for kc in range(8):
                for ic in range(8):
                    P.op("pe", (lambda e, b=b, kc=kc, ic=ic, pst=pst: e.matmul(
                        pst[:, kc:kc + 1], wp[b][:, ic, kc * 128:(kc + 1) * 128], cact[:, ic:ic + 1],
                        start=(ic == 0), stop=(ic == 7))), r=[wR[b], caR], pw=[psr])
            P.op("dve", (lambda e, vi=vi, pst=pst: e.tensor_tensor(
                raw[:, dvcol(vi)], pst[:, 0:8], bT[:, dvcol(vi)], ALU.add)), r=[psr, bR], pw=[rawR])
        dv = E.dv
        for L in range(2):
            for j in range(3):
                base = 9 * L + 3 * j
                sh, sc, gg = base, base + 1, base + 2
                gcol = slice(24 * L + 8 * j, 24 * L + 8 * j + 8)
                P.op("dve", (lambda e, sc=sc, gcol=gcol, base=base: e.scalar_tensor_tensor(
                    dv[:, dvcol(base)], raw[:, dvcol(sc)], 1.0, gT[:, gcol], ALU.add, ALU.mult)),
                    r=[rawR, gR], pw=[E.dvR])
                P.op("dve", (lambda e, sh=sh, base=base: e.tensor_copy(dv[:, dvcol(base + 1)], raw[:, dvcol(sh)])),
                     r=[rawR], pw=[E.dvR])
                mul = 1.0 if j == 1 else 0.5
                P.op("dve", (lambda e, gg=gg, base=base, mul=mul: e.tensor_scalar(
                    dv[:, dvcol(base + 2)], raw[:, dvcol(gg)], mul, None, ALU.mult)),
                    r=[rawR], pw=[E.dvR])
        P.op("dve", lambda e: e.scalar_tensor_tensor(dv[:, dvcol(18)], raw[:, dvcol(19)], 1.0, gT[:, 48:56], ALU.add, ALU.mult),
             r=[rawR, gR], pw=[E.dvR])
        P.op("dve", lambda e: e.tensor_copy(dv[:, dvcol(19)], raw[:, dvcol(18)]), r=[rawR], pw=[E.dvR])
        fence(E, [[rawR, gR, bR, caR, cR, wR[0], wR[1]]])


def norm_mod(E, Acol, Bcol, hT, hR, tmpb, tmpR, sqb, sqR, rsb, rsR, psi=6):
    P = E.P
    dv = E.dv
    pst, psr = E.ps[psi], E.psR[psi]
    for tt in range(NT):
        ts = slice(tt * 512, (tt + 1) * 512)
        for k in range(DC):
            b = k % 2
            P.op("act", (lambda e, k=k, b=b, ts=ts: e.activation(sqb[b][:], E.xT[:, k, ts], AF.Square)),
                 r=[E.xR[tt]], w=[sqR[b]])
            P.op("pe", (lambda e, k=k, b=b: e.matmul(pst[:], E.ones[:], sqb[b][:], start=(k == 0), stop=(k == DC - 1))),
                 r=[sqR[b], E.onesR], pw=[psr])
        P.op("act", lambda e: e.activation(rsb[:], pst[:], AF.Sqrt, bias=E.epsc[:, 0:1], scale=1.0),
             r=[psr, E.epsR], w=[rsR])
        P.op("dve", lambda e: e.reciprocal(rsb[:], rsb[:]), r=[rsR], w=[rsR])
        for k in range(DC):
            b = k % 2
            P.op("dve", (lambda e, k=k, b=b, ts=ts: e.tensor_tensor(tmpb[b][:], E.xT[:, k, ts], rsb[:], ALU.mult)),
                 r=[E.xR[tt], rsR], w=[tmpR[b]])
            P.op("act", (lambda e, k=k, b=b, ts=ts: e.activation(
                hT[:, k, ts], tmpb[b][:], AF.Identity, bias=dv[:, 8 * Bcol + k:8 * Bcol + k + 1],
                scale=dv[:, 8 * Acol + k:8 * Acol + k + 1])), r=[tmpR[b], E.dvR], pw=[hR[tt]])


def ffn_phase(E, wg, wu, wd, Acol, Bcol, Gcol):
    nc, P = E.nc, E.P
    dv = E.dv
    with ExitStack() as es2:
        sb2 = lambda n, s, d: es2.enter_context(nc.sbuf_tensor("s%d_" % next(_UID) + n, s, d))
        hT = sb2("f_hT", [128, DC, NTOK], BF16)
        hR = [Reg("f_hT%d" % t) for t in range(NT)]
        wgb = [sb2("f_wg%d" % i, [128, DC, 512], BF16) for i in range(2)]
        wub = [sb2("f_wu%d" % i, [128, DC, 512], BF16) for i in range(2)]
        wdb = [sb2("f_wd%d" % i, [128, 4, D], BF16) for i in range(2)]
        wgR = [Reg("f_wg%d" % i) for i in range(2)]
        wuR = [Reg("f_wu%d" % i) for i in range(2)]
        wdR = [Reg("f_wd%d" % i) for i in range(2)]
        act = [sb2("f_act%d" % i, [128, 4, NTOK], BF16) for i in range(2)]
        actR = [[Reg("f_act%d_%d" % (i, t)) for t in range(NT)] for i in range(2)]
        tmpb = [sb2("f_tmp%d" % i, [128, 512], F32) for i in range(2)]
        tmpR = [Reg("f_tmp%d" % i) for i in range(2)]
        sqb = [sb2("f_sq%d" % i, [128, 512], BF16) for i in range(2)]
        sqR = [Reg("f_sq%d" % i) for i in range(2)]
        rsb = sb2("f_rs", [128, 512], F32)
        rsR = Reg("f_rs")

        def load_group(gi):
            f0, fw = FGROUPS[gi]
            b = gi % 2
            nch = fw // 128
            P.dma("pool", wgb[b][:, :, 0:fw], wg.rearrange("(k p) f -> p k f", p=128)[:, :, f0:f0 + fw], w=[wgR[b]])
            P.dma("pool", wub[b][:, :, 0:fw], wu.rearrange("(k p) f -> p k f", p=128)[:, :, f0:f0 + fw], w=[wuR[b]])
            P.dma("pool", wdb[b][:, 0:nch, :], wd[f0:f0 + fw, :].rearrange("(k p) d -> p k d", p=128), w=[wdR[b]])

        P.wait("dve", r=E.xR)
        load_group(0)
        norm_mod(E, Acol, Bcol, hT, hR, tmpb, tmpR, sqb, sqR, rsb, rsR)
        cnt = 0
        for gi, (f0, fw) in enumerate(FGROUPS):
            b = gi % 2
            nch = fw // 128
            if gi + 1 < len(FGROUPS):
                load_group(gi + 1)
            for tt in range(NT):
                ts = slice(tt * 512, (tt + 1) * 512)
                for fc in range(nch):
                    pb = cnt % 2
                    cnt += 1
                    pg, pgR = E.ps[pb], E.psR[pb]
                    pu, puR = E.ps[2 + pb], E.psR[2 + pb]
                    for k in range(DC):
                        P.op("pe", (lambda e, k=k, fc=fc, b=b, pg=pg, ts=ts: e.matmul(
                            pg[:], wgb[b][:, k, fc * 128:(fc + 1) * 128], hT[:, k, ts],
                            start=(k == 0), stop=(k == DC - 1))), r=[wgR[b], hR[tt]], pw=[pgR])
                    for k in range(DC):
                        P.op("pe", (lambda e, k=k, fc=fc, b=b, pu=pu, ts=ts: e.matmul(
                            pu[:], wub[b][:, k, fc * 128:(fc + 1) * 128], hT[:, k, ts],
                            start=(k == 0), stop=(k == DC - 1))), r=[wuR[b], hR[tt]], pw=[puR])
                    P.op("act", (lambda e, pb=pb, pg=pg: e.activation(tmpb[pb][:], pg[:], AF.Silu)),
                         r=[pgR], w=[tmpR[pb]])
                    P.op("dve", (lambda e, pb=pb, pu=pu, b=b, fc=fc, ts=ts: e.tensor_tensor(
                        act[b][:, fc, ts], tmpb[pb][:], pu[:], ALU.mult)), r=[tmpR[pb], puR], pw=[actR[b][tt]])
            for tt in range(NT):
                ts = slice(tt * 512, (tt + 1) * 512)
                for oc in range(DC):
                    pb = cnt % 2
                    cnt += 1
                    pd, pdR = E.ps[4 + pb], E.psR[4 + pb]
                    for fc in range(nch):
                        P.op("pe", (lambda e, fc=fc, oc=oc, b=b, pd=pd, ts=ts: e.matmul(
                            pd[:], wdb[b][:, fc, oc * 128:(oc + 1) * 128], act[b][:, fc, ts],
                            start=(fc == 0), stop=(fc == nch - 1))), r=[wdR[b], actR[b][tt]], pw=[pdR])
                    P.op("dve", (lambda e, oc=oc, pd=pd, ts=ts: e.scalar_tensor_tensor(
                        E.xT[:, oc, ts], pd[:], dv[:, 8 * Gcol + oc:8 * Gcol + oc + 1], E.xT[:, oc, ts],
                        ALU.mult, ALU.add)), r=[pdR, E.dvR], pw=[E.xR[tt]])
        fence(E, [hR, wgR, wuR, wdR, actR[0], actR[1], tmpR, sqR, [rsR]])


def fence(E, reglists):
    regs = [x for l in reglists for x in l]
    P = E.P
    if not hasattr(E, "fenceR"):
        E.fenceR = Reg("fence")
    o = P.op("dve", lambda e: e.tensor_copy(E.dv[:, 255:256], E.dv[:, 254:255]), r=[E.dvR], w=regs + [E.fenceR])
    for eng in ("pe", "act", "pool", "sp"):
        P.wait(eng, r=[E.fenceR])


def load_xT(E, xT_d):
    P = E.P
    for tt in range(NT):
        ts = slice(tt * 512, (tt + 1) * 512)
        P.dma("sp", E.xT[:, :, ts], xT_d.rearrange("(k p) t -> p k t", p=128)[:, :, ts], w=[E.xR[tt]])


def store_xT(E, out_d, outR):
    P = E.P
    for tt in range(NT):
        ts = slice(tt * 512, (tt + 1) * 512)
        P.dma("sp", out_d.rearrange("(k p) t -> p k t", p=128)[:, :, ts], E.xT[:, :, ts], r=[E.xR[tt]], pw=[outR])


def load_w(E, sb2, name, view, shape, q="pool"):
    t = sb2(name, shape, BF16)
    R = Reg(name)
    E.P.dma(q, t[:], view, w=[R])
    return t, R


def rms_chunks(E, zps_list, gcols, outT, outR, tt, nfeat, tmpb, tmpR, sqb, sqR, rsb, rsR, sspsi=6):
    P = E.P
    ts = slice(tt * 512, (tt + 1) * 512)
    pss, pssR = E.ps[sspsi], E.psR[sspsi]
    n = len(zps_list)
    for ci, (pz, pzR, nr) in enumerate(zps_list):
        b = ci % 2
        P.op("act", (lambda e, pz=pz, nr=nr, b=b: e.activation(sqb[b][0:nr, :], pz[0:nr, :], AF.Square)),
             r=[pzR], w=[sqR[b]])
        P.op("pe", (lambda e, nr=nr, b=b, ci=ci: e.matmul(pss[:], E.one1[0:nr, :], sqb[b][0:nr, :],
                                                          start=(ci == 0), stop=(ci == n - 1))),
             r=[sqR[b], E.one1R], pw=[pssR])
    P.op("act", lambda e: e.activation(rsb[:], pss[:], AF.Sqrt, bias=E.epsn[:, nfeat:nfeat + 1], scale=1.0),
         r=[pssR, E.epsR], w=[rsR])
    P.op("dve", lambda e: e.reciprocal(rsb[:], rsb[:]), r=[rsR], w=[rsR])
    for ci, (pz, pzR, nr) in enumerate(zps_list):
        P.op("dve", (lambda e, pz=pz, nr=nr, ci=ci: e.scalar_tensor_tensor(
            outT[0:nr, ci, ts], pz[0:nr, :], gcols[ci], rsb[0:nr, :], ALU.mult, ALU.mult)),
            r=[pzR, rsR, E.gvR], pw=[outR[tt]])


def head_norm_proj(E, Wsb, wR, kparts, zT, zR, dk, gcol, out_d, outR, stg, stgR, sqb, sqR, rsb, rsR, nfeat_idx, hook=None):
    P = E.P
    cnt = 0
    for h in range(16):
        for tt in range(NT):
            ts = slice(tt * 512, (tt + 1) * 512)
            pb = cnt % 2
            cnt += 1
            pq, pqR = E.ps[pb], E.psR[pb]
            pss, pssR = E.ps[2 + pb], E.psR[2 + pb]
            nk = len(kparts)
            for ci, ksz in enumerate(kparts):
                P.op("pe", (lambda e, ci=ci, ksz=ksz, h=h, pq=pq, ts=ts: e.matmul(
                    pq[0:dk, :], Wsb[0:ksz, ci, h * dk:(h + 1) * dk], zT[0:ksz, ci, ts],
                    start=(ci == 0), stop=(ci == nk - 1))), r=[wR, zR[tt]], pw=[pqR])
            P.op("act", (lambda e, pq=pq, pb=pb: e.activation(sqb[pb][0:dk, :], pq[0:dk, :], AF.Square)),
                 r=[pqR], w=[sqR[pb]])
            P.op("pe", (lambda e, pss=pss, pb=pb: e.matmul(pss[0:dk, :], E.one1[0:dk, 0:dk], sqb[pb][0:dk, :],
                                                          start=True, stop=True)),
                 r=[sqR[pb], E.one1R], pw=[pssR])
            P.op("act", (lambda e, pss=pss, pb=pb: e.activation(
                rsb[pb][0:dk, :], pss[0:dk, :], AF.Sqrt, bias=E.epsn[0:dk, nfeat_idx:nfeat_idx + 1], scale=1.0)),
                r=[pssR, E.epsR], w=[rsR[pb]])
            P.op("dve", (lambda e, pb=pb: e.reciprocal(rsb[pb][0:dk, :], rsb[pb][0:dk, :])), r=[rsR[pb]], w=[rsR[pb]])
            P.op("dve", (lambda e, pq=pq, pb=pb: e.scalar_tensor_tensor(
                stg[pb][0:dk, :], pq[0:dk, :], gcol[0:dk, :], rsb[pb][0:dk, :], ALU.mult, ALU.mult)),
                r=[pqR, rsR[pb], E.gvR], w=[stgR[pb]])
            P.dma("sp", out_d[h, :, ts], stg[pb][0:dk, :], r=[stgR[pb]], pw=[outR(h) if callable(outR) else outR])
        if hook:
            hook(h)


def v_proj(E, Wv, wvR, kparts, zT, zR, v_d, vR, stg, stgR, hook=None):
    P = E.P
    cnt = 0
    nk = len(kparts)
    for t16 in range(16):
        tt = t16 // 4
        tk = slice(t16 * 128, (t16 + 1) * 128)
        sb_ = t16 % 2
        for vc in range(2):
            pb = cnt % 2
            cnt += 1
            pv, pvR = E.ps[4 + pb], E.psR[4 + pb]
            for ci, ksz in enumerate(kparts):
                P.op("pe", (lambda e, ci=ci, ksz=ksz, vc=vc, pv=pv, tk=tk: e.matmul(
                    pv[:], zT[0:ksz, ci, tk], Wv[0:ksz, ci, vc * 512:(vc + 1) * 512],
                    start=(ci == 0), stop=(ci == nk - 1))), r=[wvR, zR[tt]], pw=[pvR])
            P.op("act", (lambda e, pv=pv, vc=vc, sb_=sb_: e.activation(stg[sb_][:, vc * 512:(vc + 1) * 512], pv[:], AF.Copy)),
                 r=[pvR], pw=[stgR[sb_]])
        P.dma("sp", v_d[tk, :], stg[sb_][:], r=[stgR[sb_]], pw=[vR(t16) if callable(vR) else vR])
        if hook:
            hook(t16)


def setup_eps(E):
    E.epsn = E.sb("epsn", [128, 4], F32)
    for i, n in enumerate((64, 96, 256, 384)):
        E.P.op("pool", (lambda e, i=i, n=n: e.memset(E.epsn[:, i:i + 1], n * EPS)), pw=[E.epsR])


def moba_prep(E, w_qkv, gq_d, gk_d, q_d, k_d, v_d, Acol, Bcol, hooks=None):
    nc, P = E.nc, E.P
    with ExitStack() as es2:
        sb2 = lambda n, s, d: es2.enter_context(nc.sbuf_tensor("s%d_" % next(_UID) + n, s, d))
        hT = sb2("m_hT", [128, DC, NTOK], BF16)
        hR = [Reg("m_hT%d" % t) for t in range(NT)]
        tmpb = [sb2("m_tmp%d" % i, [128, 512], F32) for i in range(2)]
        tmpR = [Reg("m_tmp%d" % i) for i in range(2)]
        sqb = [sb2("m_sq%d" % i, [128, 512], BF16) for i in range(2)]
        sqR = [Reg("m_sq%d" % i) for i in range(2)]
        rs1 = sb2("m_rs", [128, 512], F32)
        rs1R = Reg("m_rs")
        rsb = [sb2("m_rsb%d" % i, [128, 512], F32) for i in range(2)]
        rsR = [Reg("m_rsb%d" % i) for i in range(2)]
        stg = [sb2("m_stg%d" % i, [128, 512], BF16) for i in range(2)]
        stgR = [Reg("m_stg%d" % i) for i in range(2)]
        vst = [sb2("m_vst%d" % i, [128, 1024], BF16) for i in range(2)]
        vstR = [Reg("m_vst%d" % i) for i in range(2)]
        gv = sb2("m_gv", [128, 2], F32)
        E.gvR = Reg("m_gv")
        P.dma("sp", gv[:, 0:1], gq_d, pw=[E.gvR])
        P.dma("sp", gv[:, 1:2], gk_d, pw=[E.gvR])
        wv3 = w_qkv.rearrange("(k p) f -> p k f", p=128)
        Wq, wqR = load_w(E, sb2, "m_wq", wv3[:, :, 0:1024], [128, 8, 1024])
        Wk, wkR = load_w(E, sb2, "m_wk", wv3[:, :, 1024:2048], [128, 8, 1024])
        Wv, wvR = load_w(E, sb2, "m_wv", wv3[:, :, 2048:3072], [128, 8, 1024])
        norm_mod(E, Acol, Bcol, hT, hR, tmpb, tmpR, sqb, sqR, rs1, rs1R)
        qR, kR, vR = Reg("q_d"), Reg("k_d"), Reg("v_d")
        kp = [128] * 8
        H = hooks or {}
        head_norm_proj(E, Wk, wkR, kp, hT, hR, 64, gv[:, 1:2], k_d, H.get("kR", kR), stg, stgR, sqb, sqR, rsb, rsR, 0, hook=H.get("k"))
        v_proj(E, Wv, wvR, kp, hT, hR, v_d, H.get("vR", vR), vst, vstR, hook=H.get("v"))
        head_norm_proj(E, Wq, wqR, kp, hT, hR, 64, gv[:, 0:1], q_d, H.get("qR", qR), stg, stgR, sqb, sqR, rsb, rsR, 0, hook=H.get("q"))
        fence(E, [hR, tmpR, sqR, [rs1R], rsR, stgR, vstR, [E.gvR, wqR, wkR, wvR]])
        E.outRs = getattr(E, "outRs", []) + [qR, kR, vR]


def wo_phase(E, w_o, ld_otok, ident_d, Gcol):
    nc, P = E.nc, E.P
    dv = E.dv
    with ExitStack() as es2:
        sb2 = lambda n, s, d: es2.enter_context(nc.sbuf_tensor("s%d_" % next(_UID) + n, s, d))
        oT = sb2("w_oT", [128, DC, NTOK], BF16)
        oR = [Reg("w_oT%d" % t) for t in range(NT)]
        otok = [sb2("w_otok%d" % i, [128, 1024], BF16) for i in range(2)]
        otR = [Reg("w_otok%d" % i) for i in range(2)]
        ident = sb2("w_id", [128, 128], BF16)
        identR = Reg("w_id")
        P.dma("sp", ident[:], ident_d, w=[identR])
        Wo, woR = load_w(E, sb2, "w_wo", w_o.rearrange("(k p) f -> p k f", p=128), [128, 8, 1024])
        for t16 in range(16):
            tt = t16 // 4
            b = t16 % 2
            ld_otok(t16, otok[b], otR[b])
            for half in range(2):
                pz, pzR = E.ps[2 * b + half], E.psR[2 * b + half]
                for kq in range(4):
                    k = half * 4 + kq
                    P.op("pe", (lambda e, k=k, kq=kq, b=b, pz=pz: e.matmul(
                        pz[:, kq * 128:(kq + 1) * 128], otok[b][:, k * 128:(k + 1) * 128], ident[:],
                        start=True, stop=True)), r=[otR[b], identR], pw=[pzR])
                P.op("act", (lambda e, half=half, pz=pz, t16=t16: e.activation(
                    oT[:, half * 4:(half + 1) * 4, t16 * 128:(t16 + 1) * 128],
                    pz[:].rearrange("p (k t) -> p k t", t=128), AF.Copy)), r=[pzR], pw=[oR[tt]])
        P.wait("dve", r=E.xR)
        cnt = 0
        for tt in range(NT):
            ts = slice(tt * 512, (tt + 1) * 512)
            for oc in range(DC):
                pb = cnt % 2
                cnt += 1
                pd, pdR = E.ps[4 + pb], E.psR[4 + pb]
                for k in range(DC):
                    P.op("pe", (lambda e, k=k, oc=oc, pd=pd, ts=ts: e.matmul(
                        pd[:], Wo[:, k, oc * 128:(oc + 1) * 128], oT[:, k, ts],
                        start=(k == 0), stop=(k == DC - 1))), r=[woR, oR[tt]], pw=[pdR])
                P.op("dve", (lambda e, oc=oc, pd=pd, ts=ts: e.scalar_tensor_tensor(
                    E.xT[:, oc, ts], pd[:], dv[:, 8 * Gcol + oc:8 * Gcol + oc + 1], E.xT[:, oc, ts],
                    ALU.mult, ALU.add)), r=[pdR, E.dvR], pw=[E.xR[tt]])
        fence(E, [oR, otR, [woR, identR]])


def proj_chunks(E, Wsb, wR, zT, zR, mparts, tt, psbase=0):
    P = E.P
    ts = slice(tt * 512, (tt + 1) * 512)
    res = []
    m0 = 0
    for mi, msz in enumerate(mparts):
        pz, pzR = E.ps[psbase + mi], E.psR[psbase + mi]
        for k in range(DC):
            P.op("pe", (lambda e, k=k, m0=m0, msz=msz, pz=pz: e.matmul(
                pz[0:msz, :], Wsb[:, k, m0:m0 + msz], zT[:, k, ts], start=(k == 0), stop=(k == DC - 1))),
                r=[wR, zR[tt]], pw=[pzR])
        res.append((pz, pzR, msz))
        m0 += msz
    return res


def mla_prep(E, w_dq, gqa_d, w_uq, gq_d, w_dkv, gkva_d, wk_full, gk_d, w_uv, q_d, k_d, v_d, A2, B2, Akv, Bkv, hooks=None):
    nc, P = E.nc, E.P
    with ExitStack() as es2:
        sb2 = lambda n, s, d: es2.enter_context(nc.sbuf_tensor("s%d_" % next(_UID) + n, s, d))
        hT = sb2("l_hT", [128, DC, NTOK], BF16)
        hR = [Reg("l_hT%d" % t) for t in range(NT)]
        zT = sb2("l_zT", [128, 3, NTOK], BF16)
        zR = [Reg("l_zT%d" % t) for t in range(NT)]
        tmpb = [sb2("l_tmp%d" % i, [128, 512], F32) for i in range(2)]
        tmpR = [Reg("l_tmp%d" % i) for i in range(2)]
        sqb = [sb2("l_sq%d" % i, [128, 512], BF16) for i in range(2)]
        sqR = [Reg("l_sq%d" % i) for i in range(2)]
        rs1 = sb2("l_rs", [128, 512], F32)
        rs1R = Reg("l_rs")
        rsb = [sb2("l_rsb%d" % i, [128, 512], F32) for i in range(2)]
        rsR = [Reg("l_rsb%d" % i) for i in range(2)]
        stg = [sb2("l_stg%d" % i, [128, 512], BF16) for i in range(2)]
        stgR = [Reg("l_stg%d" % i) for i in range(2)]
        vst = [sb2("l_vst%d" % i, [128, 1024], BF16) for i in range(2)]
        vstR = [Reg("l_vst%d" % i) for i in range(2)]
        gv = sb2("l_gv", [128, 16], F32)
        E.gvR = Reg("l_gv")
        P.dma("sp", gv[:, 0:3], gqa_d, pw=[E.gvR])
        P.dma("sp", gv[:, 3:5], gkva_d, pw=[E.gvR])
        P.dma("sp", gv[0:96, 5:6], gq_d, pw=[E.gvR])
        P.dma("sp", gv[0:96, 6:7], gk_d, pw=[E.gvR])
        P.op("dve", lambda e: e.tensor_scalar(gv[:, 8:11], gv[:, 0:3], float(np.sqrt(384.0)), None, ALU.mult), r=[E.gvR], pw=[E.gvR])
        P.op("dve", lambda e: e.tensor_scalar(gv[:, 11:13], gv[:, 3:5], float(np.sqrt(256.0)), None, ALU.mult), r=[E.gvR], pw=[E.gvR])
        qR, kR, vR = Reg("q_d2"), Reg("k_d2"), Reg("v_d2")
        H = hooks or {}
        Wdq, wdqR = load_w(E, sb2, "l_wdq", w_dq.rearrange("(k p) f -> p k f", p=128), [128, 8, 384])
        Wuq, wuqR = load_w(E, sb2, "l_wuq", w_uq.rearrange("(k p) f -> p k f", p=128), [128, 3, 1536])
        Wdkv, wdkvR = load_w(E, sb2, "l_wdkv", w_dkv.rearrange("(k p) f -> p k f", p=128), [128, 8, 288])
        Wkf = sb2("l_wkf", [128, 3, 1536], BF16)
        wkfR = Reg("l_wkf")
        P.dma("pool", Wkf[:, 0:2, :], wk_full[0:256, :].rearrange("(k p) f -> p k f", p=128), pw=[wkfR])
        P.dma("pool", Wkf[0:32, 2, :], wk_full[256:288, :], pw=[wkfR])
        Wuv, wuvR = load_w(E, sb2, "l_wuv", w_uv.rearrange("(k p) f -> p k f", p=128), [128, 2, 1024])
        norm_mod(E, Akv, Bkv, hT, hR, tmpb, tmpR, sqb, sqR, rs1, rs1R)
        for tt in range(NT):
            ts = slice(tt * 512, (tt + 1) * 512)
            zps = proj_chunks(E, Wdkv, wdkvR, hT, hR, [128, 128, 32], tt, psbase=0)
            rms_chunks(E, zps[0:2], [gv[:, 11:12], gv[:, 12:13]], zT, zR, tt, 2, tmpb, tmpR, sqb, sqR, rs1, rs1R)
            pz, pzR, _ = zps[2]
            P.op("act", (lambda e, pz=pz, ts=ts: e.activation(zT[0:32, 2, ts], pz[0:32, :], AF.Copy)), r=[pzR], pw=[zR[tt]])
        head_norm_proj(E, Wkf, wkfR, [128, 128, 32], zT, zR, 96, gv[:, 6:7], k_d, H.get("kR", kR), stg, stgR, sqb, sqR, rsb, rsR, 1, hook=H.get("k"))
        v_proj(E, Wuv, wuvR, [128, 128], zT, zR, v_d, H.get("vR", vR), vst, vstR, hook=H.get("v"))
        norm_mod(E, A2, B2, hT, hR, tmpb, tmpR, sqb, sqR, rs1, rs1R)
        for tt in range(NT):
            zps = proj_chunks(E, Wdq, wdqR, hT, hR, [128, 128, 128], tt, psbase=0)
            rms_chunks(E, zps, [gv[:, 8:9], gv[:, 9:10], gv[:, 10:11]], zT, zR, tt, 3, tmpb, tmpR, sqb, sqR, rs1, rs1R)
        head_norm_proj(E, Wuq, wuqR, [128, 128, 128], zT, zR, 96, gv[:, 5:6], q_d, H.get("qR", qR), stg, stgR, sqb, sqR, rsb, rsR, 1, hook=H.get("q"))
        fence(E, [hR, zR, tmpR, sqR, [rs1R], rsR, stgR, vstR, [E.gvR, wdqR, wuqR, wdkvR, wkfR, wuvR]])
        E.outRs = getattr(E, "outRs", []) + [qR, kR, vR]


SEQ = 4096
NQT = 8
NKT = 32
BIGM = 30000.0


def attn_phase(E, ld, o_d, dk, scale, tri_d, moba=None, rope=None, side=None, pre=None):
    return _attn_phase(E, ld, o_d, dk, scale, tri_d, moba, rope, side, pre)


def _attn_phase(E, ld, o_d, dk, scale, tri_d, moba, rope, side_factory, pre):
    nc, P = E.nc, E.P
    with ExitStack() as es2:
        sb2 = lambda n, s, d: es2.enter_context(nc.sbuf_tensor("s%d_" % next(_UID) + n, s, d))
        Kb = [sb2("a_K%d" % i, [128, SEQ], BF16) for i in range(2)]
        Qb = [sb2("a_Q%d" % i, [128, SEQ], BF16) for i in range(2)]
        Vb = [sb2("a_V%d" % i, [128, NKT, 65], BF16) for i in range(2)]
        KR = [Reg("a_K%d" % i) for i in range(2)]
        QR = [Reg("a_Q%d" % i) for i in range(2)]
        VR = [Reg("a_V%d" % i) for i in range(2)]
        QmR = [[Reg("a_Qm%d_%d" % (i, j)) for j in range(NQT)] for i in range(2)]
        Pb = [sb2("a_P%d" % i, [128, 512], BF16) for i in range(4)]
        PR = [Reg("a_P%d" % i) for i in range(4)]
        osb = [sb2("a_o%d" % i, [128, 4, 64], BF16) for i in range(2)]
        osR = [Reg("a_o%d" % i) for i in range(2)]
        rec = [sb2("a_rec%d" % i, [128, 4], F32) for i in range(2)]
        recR = [Reg("a_rec%d" % i) for i in range(2)]
        tri = sb2("a_tri", [128, 128], BF16)
        triR = Reg("a_tri")
        P.dma("sp", tri[:], tri_d, w=[triR])
        oR = Reg("a_od")
        E.outRs = getattr(E, "outRs", []) + [oR]
        for i in range(2):
            P.op("pool", (lambda e, i=i: e.memset(Vb[i][:, :, 64:65], 1.0)), pw=[VR[i]])
        Dt = Et = None
        if moba is not None:
            for i in range(2):
                P.dma("sp", Kb[i][64:80, :], moba["onehot_d"], pw=[KR[i]])
            cst = sb2("a_cst", [128, 3, 512], F32)
            cstR = Reg("a_cst")
            P.dma("sp", cst[:, 0, :], moba["eligadd_d"], pw=[cstR])
            P.dma("sp", cst[:, 1, :], moba["elig01_d"], pw=[cstR])
            P.dma("sp", cst[:, 2, :], moba["own01_d"], pw=[cstR])
            ident = sb2("a_id", [128, 128], BF16)
            identR = Reg("a_id")
            P.dma("sp", ident[:], moba["ident_d"], w=[identR])
            nb31 = sb2("a_nb31", [128, 8], F32)
            nbR = Reg("a_nb31")
            P.dma("sp", nb31[:], moba["b31_d"], w=[nbR])
            P.op("dve", lambda e: e.tensor_scalar(nb31[:], nb31[:], -1.0, None, ALU.mult), r=[nbR], w=[nbR])
            rawt = [sb2("a_raw%d" % i, [128, 128], F32) for i in range(2)]
            rawR = [Reg("a_raw%d" % i) for i in range(2)]
            Dt = sb2("a_D", [128, 8, 128], BF16)
            Et = sb2("a_E", [128, 8, 128], BF16)
            DR, ER = Reg("a_D"), Reg("a_E")
            for h in range(8):
                P.dma("sp", rawt[0][:], moba["rawD_d"][h], w=[rawR[0]])
                P.op("act", (lambda e, h=h: e.activation(rawt[0][:], rawt[0][:], AF.Exp, bias=nb31[:, h:h + 1], scale=1.0)),
                     r=[nbR], w=[rawR[0]])
                P.op("dve", (lambda e, h=h: e.tensor_tensor(Dt[:, h, :], rawt[0][:], tri[:], ALU.mult)),
                     r=[rawR[0], triR], pw=[DR])
                P.dma("sp", rawt[1][:], moba["rawE_d"][h], w=[rawR[1]])
                P.op("act", (lambda e, h=h: e.activation(Et[:, h, :], rawt[1][:], AF.Exp, bias=nb31[:, h:h + 1], scale=1.0)),
                     r=[rawR[1], nbR], pw=[ER])
            ks32 = [sb2("a_ks32_%d" % i, [64, 16], F32) for i in range(2)]
            ksb = [sb2("a_ksb_%d" % i, [64, 16], BF16) for i in range(2)]
            ks32R = [Reg("a_ks32_%d" % i) for i in range(2)]
            ksbR = [Reg("a_ksb_%d" % i) for i in range(2)]
            gm = sb2("a_gm", [128, 64], F32)
            sel = sb2("a_sel", [128, 64], F32)
            mx8 = sb2("a_mx8", [128, 4, 8], F32)
            gmR, selR, mxR = Reg("a_gm"), Reg("a_sel"), Reg("a_mx8")
            Z = sb2("a_Z", [128, 4, 80], BF16)
            ZR = Reg("a_Z")
            P.op("pool", lambda e: e.memset(Z[:], 0.0), w=[ZR])
        if rope is not None:
            Ct = sb2("a_C", [128, SEQ], F32)
            St = sb2("a_S", [128, SEQ], F32)
            CR, SR = Reg("a_C"), Reg("a_S")
            pr = slice(64, 96)
            P.dma("sp", Ct[pr, :], rope["C_d"], r=[Reg("ropetab_d")], w=[CR])
            P.dma("sp", St[pr, :], rope["S_d"], r=[Reg("ropetab_d")], w=[SR])
            Qs = [sb2("a_Qs%d" % i, [128, SEQ], BF16) for i in range(2)]
            Ks = [sb2("a_Ks%d" % i, [128, SEQ], BF16) for i in range(2)]
            QsR = [Reg("a_Qs%d" % i) for i in range(2)]
            KsR = [Reg("a_Ks%d" % i) for i in range(2)]
            rt1 = sb2("a_rt1", [128, 2048], F32)
            rt2 = sb2("a_rt2", [128, 2048], F32)
            rt1R, rt2R = Reg("a_rt1"), Reg("a_rt2")

        LOOK = 3
        side = side_factory(sb2) if side_factory else None

        def head_prologue(h):
            hb = h % 2
            K, Q, V = Kb[hb], Qb[hb], Vb[hb]
            ld["K"](h, K, KR[hb])
            ld["Q"](h, Q, QR[hb])
            ld["V"](h, V, VR[hb])
            if rope is not None:
                pr = slice(64, 96)
                ld["Qs"](h, Qs[hb], QsR[hb])
                ld["Ks"](h, Ks[hb], KsR[hb])
                for (X, XR, Xs, XsR) in ((Q, QR[hb], Qs[hb], QsR[hb]), (K, KR[hb], Ks[hb], KsR[hb])):
                    for hf in range(2):
                        cs_ = slice(hf * 2048, (hf + 1) * 2048)
                        P.op("dve", (lambda e, X=X, cs_=cs_: e.tensor_tensor(rt1[pr, :], X[pr, cs_], Ct[pr, cs_], ALU.mult)),
                             r=[XR, CR], w=[rt1R])
                        P.op("pool", (lambda e, Xs=Xs, cs_=cs_: e.tensor_tensor(rt2[pr, :], Xs[pr, cs_], St[pr, cs_], ALU.mult)),
                             r=[XsR, SR], w=[rt2R])
                        P.op("dve", (lambda e, X=X, cs_=cs_: e.tensor_tensor(X[pr, cs_], rt1[pr, :], rt2[pr, :], ALU.add)),
                             r=[rt1R, rt2R, XR], pw=[XR])
            if moba is not None:
                P.op("dve", (lambda e, K=K, hb=hb: e.tensor_reduce(
                    ks32[hb][:], K[0:64, :].rearrange("p (n t) -> p n t", t=256), AX.X, ALU.add)), r=[KR[hb]], w=[ks32R[hb]])
                P.op("act", (lambda e, hb=hb: e.activation(ksb[hb][:], ks32[hb][:], AF.Copy)), r=[ks32R[hb]], w=[ksbR[hb]])

        def gate_prologue(h, j):
            hb = h % 2
            Q = Qb[hb]
            qs0 = j * 512
            pg, pgR = E.ps[6], E.psR[6]
            for ip in range(4):
                P.op("pe", (lambda e, ip=ip, Q=Q, qs0=qs0, hb=hb: e.matmul(
                    pg[:, ip * 16:(ip + 1) * 16], Q[0:64, qs0 + ip * 128:qs0 + (ip + 1) * 128], ksb[hb][:],
                    start=True, stop=True)), r=[QR[hb], ksbR[hb]], pw=[pgR])
            cs = slice(j * 64, (j + 1) * 64)
            P.op("dve", (lambda e, cs=cs: e.tensor_tensor(gm[:], pg[:, 0:64], cst[:, 0, cs], ALU.add)),
                 r=[pgR, cstR], w=[gmR])
            for ip in range(4):
                P.op("dve", (lambda e, ip=ip: e.max(mx8[:, ip, :], gm[:, ip * 16:(ip + 1) * 16])), r=[gmR], pw=[mxR])
            for ip in range(4):
                P.op("dve", (lambda e, ip=ip: e.tensor_scalar(
                    sel[:, ip * 16:(ip + 1) * 16], gm[:, ip * 16:(ip + 1) * 16], mx8[:, ip, 2:3], None, ALU.is_ge)),
                    r=[gmR, mxR], pw=[selR])
            P.op("dve", (lambda e, cs=cs: e.tensor_tensor(sel[:], sel[:], cst[:, 1, cs], ALU.mult)), r=[selR, cstR], w=[selR])
            P.op("dve", (lambda e, cs=cs: e.tensor_tensor(sel[:], sel[:], cst[:, 2, cs], ALU.add)), r=[selR, cstR], w=[selR])
            P.op("dve", lambda e: e.tensor_scalar(
                Z[:, :, 64:80], sel[:].rearrange("p (i n) -> p i n", n=16), -1.0, BIGM, ALU.add, ALU.mult),
                r=[selR], w=[ZR])
            pz, pzR = E.ps[7], E.psR[7]
            for ip in range(4):
                P.op("pe", (lambda e, ip=ip: e.matmul(pz[0:80, ip * 128:(ip + 1) * 128], Z[:, ip, :], ident[:],
                                                     start=True, stop=True)), r=[ZR, identR], pw=[pzR])
            P.op("act", (lambda e, Q=Q, qs0=qs0: e.activation(Q[64:80, qs0:qs0 + 512], pz[64:80, :], AF.Copy)),
                 r=[pzR], w=[QmR[hb][j]])

        units = [(h, j, g) for h in range(8) for j in range(NQT) for g in range(4 * j + 4)]
        first_of_head = {}
        for idx, (h, j, g) in enumerate(units):
            if (j, g) == (0, 0):
                first_of_head[h] = idx
        NU = len(units)

        def front(idx):
            h, j, g = units[idx]
            hb = h % 2
            K, Q = Kb[hb], Qb[hb]
            qs0 = j * 512
            imin = max(0, g - 4 * j)
            c0 = imin * 128
            pb = idx % 4
            pS, pSR = E.ps[pb], E.psR[pb]
            rds = [KR[hb], QR[hb]] + ([QmR[hb][j]] if moba is not None else [])
            kk = 80 if moba is not None else dk
            P.op("pe", (lambda e: e.matmul(
                pS[:, c0:512], K[0:kk, g * 128:(g + 1) * 128], Q[0:kk, qs0 + c0:qs0 + 512],
                start=True, stop=True)), r=rds, pw=[pSR])
            P.op("act", (lambda e: e.activation(Pb[pb][:, c0:512], pS[:, c0:512], AF.Exp, scale=scale)),
                 r=[pSR], w=[PR[pb]])
            for ip in range(imin, 4):
                G = 4 * j + ip
                tab = None
                if g == G:
                    tab = (Dt[:, h, :], DR) if moba is not None else (tri[:], triR)
                elif g == G - 1 and moba is not None:
                    tab = (Et[:, h, :], ER)
                if tab is not None:
                    P.op("dve", (lambda e, ip=ip, tab=tab: e.tensor_tensor(
                        Pb[pb][:, ip * 128:(ip + 1) * 128], Pb[pb][:, ip * 128:(ip + 1) * 128], tab[0], ALU.mult)),
                        r=[tab[1], PR[pb]], pw=[PR[pb]])

        def back(idx):
            h, j, g = units[idx]
            hb = h % 2
            V = Vb[hb]
            qs0 = j * 512
            imin = max(0, g - 4 * j)
            pb = idx % 4
            ob = (h * NQT + j) % 2
            po, poR = E.ps[4 + ob], E.psR[4 + ob]
            for ip in range(imin, 4):
                G = 4 * j + ip
                P.op("pe", (lambda e, ip=ip, G=G: e.matmul(
                    po[:, ip * 65:(ip + 1) * 65], Pb[pb][:, ip * 128:(ip + 1) * 128], V[:, g, :],
                    start=(g == 0 and ip == 0), stop=(g == G), skip_group_check=True)), r=[PR[pb], VR[hb]], pw=[poR])
            if g == 4 * j + 3:
                P.op("dve", (lambda e: e.reciprocal(
                    rec[ob][:], po[:, 0:260].rearrange("p (i c) -> p i c", c=65)[:, :, 64])), r=[poR], w=[recR[ob]])
                for ip in range(4):
                    P.op("dve", (lambda e, ip=ip: e.tensor_scalar(
                        osb[ob][:, ip, :], po[:, ip * 65:ip * 65 + 64], rec[ob][:, ip:ip + 1], None, ALU.mult)),
                        r=[poR, recR[ob]], pw=[osR[ob]])
                P.dma("sp", o_d[qs0:qs0 + 512, h * 64:(h + 1) * 64].rearrange("(i p) d -> p i d", p=128), osb[ob][:],
                      r=[osR[ob]], pw=[oR])

        if pre:
            pre()
        head_prologue(0)
        if moba is not None:
            gate_prologue(0, 0)
        side_jobs = list(side) if side else []
        for idx in range(NU + LOOK):
            if idx < NU:
                h, j, g = units[idx]
                if g == 0 and moba is not None and j + 1 < NQT:
                    gate_prologue(h, j + 1)
                if side_jobs and g == 0 and j >= 2:
                    side_jobs.pop(0)()
                if h + 1 < 8 and idx == first_of_head[h] + LOOK + 1:
                    head_prologue(h + 1)
                    if moba is not None:
                        gate_prologue(h + 1, 0)
                front(idx)
            if idx - LOOK >= 0:
                back(idx - LOOK)
        while side_jobs:
            side_jobs.pop(0)()


def rope_table_jobs(E, pos_d, inv_d, C_d, S_d):
    P = E.P
    TWO_PI = float(2 * np.pi)

    def factory(sb2):
        CW = 1024
        posi = sb2("r_posi", [128, CW], I32)
        rr = sb2("r_rr", [128, CW], F32)
        ni = sb2("r_ni", [128, CW], I32)
        nf = sb2("r_nf", [128, CW], F32)
        dst = sb2("r_dst", [128, CW], F32)
        inv = sb2("r_inv", [128, 2], F32)
        posR, rrR, niR, nfR, dstR, invR = Reg("r_posi"), Reg("r_rr"), Reg("r_ni"), Reg("r_nf"), Reg("r_dst"), Reg("r_inv")
        tabR = Reg("ropetab_d")
        pr = slice(64, 96)
        P.dma("sp", inv[:], inv_d, w=[invR])
        jobs = []

        def job(ck, which):
            cs = slice(ck * CW, (ck + 1) * CW)
            shift = 0.0 if which == "S" else 0.25
            if which == "S":
                P.dma("sp", posi[pr, :], pos_d[:, cs], w=[posR])
                P.op("dve", lambda e: e.tensor_copy(rr[pr, :], posi[pr, :]), r=[posR], w=[rrR])
                P.op("dve", lambda e: e.tensor_scalar(rr[pr, :], rr[pr, :], inv[pr, 0:1], 1.0 / TWO_PI, ALU.mult, ALU.mult),
                     r=[rrR, invR], w=[rrR])
            P.op("dve", lambda e: e.tensor_scalar(nf[pr, :], rr[pr, :], shift, None, ALU.add), r=[rrR], w=[nfR])
            P.op("dve", lambda e: e.tensor_copy(ni[pr, :], nf[pr, :]), r=[nfR], w=[niR])
            P.op("dve", lambda e: e.tensor_copy(dst[pr, :], ni[pr, :]), r=[niR], w=[dstR])
            P.op("dve", lambda e: e.tensor_tensor(nf[pr, :], nf[pr, :], dst[pr, :], ALU.subtract), r=[nfR, dstR], w=[nfR])
            P.op("dve", lambda e: e.tensor_scalar(dst[pr, :], nf[pr, :], 0.5, None, ALU.is_gt), r=[nfR], w=[dstR])
            P.op("dve", lambda e: e.tensor_tensor(nf[pr, :], nf[pr, :], dst[pr, :], ALU.subtract), r=[nfR, dstR], w=[nfR])
            P.op("dve", lambda e: e.tensor_scalar(dst[pr, :], nf[pr, :], -0.5, None, ALU.is_lt), r=[nfR], w=[dstR])
            P.op("dve", lambda e: e.tensor_tensor(nf[pr, :], nf[pr, :], dst[pr, :], ALU.add), r=[nfR, dstR], w=[nfR])
            P.op("act", lambda e: e.activation(dst[pr, :], nf[pr, :], AF.Sin, scale=TWO_PI), r=[nfR], w=[dstR])
            if which == "S":
                P.op("dve", lambda e: e.tensor_scalar(dst[pr, :], dst[pr, :], inv[pr, 1:2], None, ALU.mult), r=[dstR, invR], w=[dstR])
            P.dma("sp", (S_d if which == "S" else C_d)[:, cs], dst[pr, :], r=[dstR], pw=[tabR])

        for ck in range(SEQ // CW):
            for which in ("S", "C"):
                jobs.append(lambda ck=ck, which=which: job(ck, which))
        return jobs
    return factory


import math
import ml_dtypes
from concourse.bass_utils import run_bass_kernel_spmd

BF = ml_dtypes.bfloat16


def _pl(v):
    return np.ascontiguousarray(np.asarray(v).reshape(-1, 128).T)


def _dt(nc, n, s, d=F32, k="ExternalInput"):
    return nc.dram_tensor(n, list(s), d, kind=k).ap()


def _finish(P, E):
    P.emit()


def _common_inputs(nc):
    a = {}
    a["wg"] = _dt(nc, "wg", [2, 2, 1024, 2816])
    a["wu"] = _dt(nc, "wu", [2, 2, 1024, 2816])
    a["wd"] = _dt(nc, "wd", [2, 2, 2816, 1024])
    return a


def _t5_bucket_np(n):
    n = np.maximum(np.asarray(n, np.int32), 0)
    nf = np.maximum(n, 1).astype(np.float32)
    large = 16 + (np.log(nf / np.float32(16)) / np.float32(math.log(128 / 16)) * np.float32(16)).astype(np.int32)
    large = np.minimum(large, 31)
    return np.where(n < 16, n, large)


PAIRS = [[0, 1], [2, 3], [4, 5], [6, 7]]


def build_fused():
    nc = bass.Bass("TRN2", target_bir_lowering=False)
    a = _common_inputs(nc)
    xT_d = _dt(nc, "xT", [1024, 2048]); cT = _dt(nc, "cT", [128, 8])
    ada_w = _dt(nc, "ada_w", [2, 1024, 9216]); ada_bT = _dt(nc, "ada_bT", [2, 128, 72]); norm_gT = _dt(nc, "norm_gT", [2, 128, 24])
    kv_ada_w = _dt(nc, "kv_ada_w", [1024, 2048]); kv_ada_bT = _dt(nc, "kv_ada_bT", [128, 16]); kv_norm_gT = _dt(nc, "kv_norm_gT", [128, 8])
    w_qkv = _dt(nc, "w_qkv", [1024, 3072]); mgq = _dt(nc, "mgq", [128, 1]); mgk = _dt(nc, "mgk", [128, 1])
    w_o1 = _dt(nc, "w_o1", [1024, 1024]); w_o2 = _dt(nc, "w_o2", [1024, 1024])
    w_dq = _dt(nc, "w_dq", [1024, 384]); gqa = _dt(nc, "gqa", [128, 3]); w_uq = _dt(nc, "w_uq", [384, 1536]); gq = _dt(nc, "gq", [96, 1])
    w_dkv = _dt(nc, "w_dkv", [1024, 288]); gkva = _dt(nc, "gkva", [128, 2]); wkf = _dt(nc, "wkf", [288, 1536]); gk = _dt(nc, "gk", [96, 1])
    w_uv = _dt(nc, "w_uv", [256, 1024])
    tri = _dt(nc, "tri", [128, 128], BF16); ident = _dt(nc, "ident", [128, 128], BF16)
    mo = dict(onehot_d=_dt(nc, "onehot", [16, 4096], BF16), eligadd_d=_dt(nc, "eligadd", [128, 512]),
              elig01_d=_dt(nc, "elig01", [128, 512]), own01_d=_dt(nc, "own01", [128, 512]),
              rawD_d=_dt(nc, "rawD", [8, 128, 128]), rawE_d=_dt(nc, "rawE", [8, 128, 128]),
              b31_d=_dt(nc, "b31", [128, 8]), ident_d=ident)
    pos_d = _dt(nc, "pos", [32, 4096], I32)
    inv_d = _dt(nc, "inv", [128, 2])
    C_d = nc.dram_tensor("i_ropeC", [32, 4096], F32).ap()
    S_d = nc.dram_tensor("i_ropeS", [32, 4096], F32).ap()
    ro = dict(C_d=C_d, S_d=S_d)
    out = _dt(nc, "outT", [1024, 2048], F32, "ExternalOutput")
    it = lambda n, s: nc.dram_tensor(n, list(s), BF16)
    q1, k1, v1 = it("i_q1", [1024, 2048]), it("i_k1", [1024, 2048]), it("i_v1", [2048, 1024])
    q1g, k1g, v1g = it("i_q1g", [2048, 2048]), it("i_k1g", [2048, 2048]), it("i_v1g", [4096, 1024])
    o1, o1g = it("i_o1", [4096, 512]), it("i_o1g", [8192, 512])
    q2, k2, v2 = it("i_q2", [1536, 2048]), it("i_k2", [1536, 2048]), it("i_v2", [2048, 1024])
    q2g, k2g, v2g = it("i_q2g", [3072, 2048]), it("i_k2g", [3072, 2048]), it("i_v2g", [4096, 1024])
    o2, o2g = it("i_o2", [4096, 512]), it("i_o2g", [8192, 512])

    with ExitStack() as es:
        P = Prog(nc, es)
        E = alloc_common(nc, es, P)
        setup_eps(E)
        parc = {}

        def par(e):
            k = id(e)
            if k not in parc:
                parc[k] = e.snap(e.partition_id() % 2)
            return parc[k]

        def gather(src, dst, names, nch):
            srcR, dstR = Reg(names[0]), Reg(names[1])
            rows = src.shape[0] // nch
            for j in range(nch):
                P.cc("AllGather", PAIRS, src.ap()[j * rows:(j + 1) * rows, :], dst.ap()[j * 2 * rows:(j + 1) * 2 * rows, :],
                     r=[srcR], pw=[dstR])
            return dstR

        def make_exchange(q, k, v, qg, kg, vg, dk, tag, nq, names):
            jj = 1 if dk == 64 else 2
            qs_ = nc.dram_tensor("i_qs" + tag, [2, 8 * dk, 2048], BF16)
            ks_ = nc.dram_tensor("i_ks" + tag, [2, 8 * dk, 2048], BF16)
            vs_ = nc.dram_tensor("i_vs" + tag, [4096, 512], BF16)
            qsR, ksR, vsR = Reg("qsel"), Reg("ksel"), Reg("vsel")

            gRs = {}
            hpc = 16 // nq

            def chunk_gather(src, g_, nm, nch, j):
                rows = src.shape[0] // nch
                P.cc("AllGather", PAIRS, src.ap()[j * rows:(j + 1) * rows, :], g_.ap()[j * 2 * rows:(j + 1) * 2 * rows, :],
                     r=[Reg(nm[0])], pw=[Reg(nm[1])])

            def hqk(src, g_, nm, key):
                def f(h):
                    if (h + 1) % hpc == 0:
                        chunk_gather(src, g_, nm, nq, h // hpc)
                        gRs[key] = Reg(nm[1])
                return f

            def hv(t16):
                if (t16 + 1) % 8 == 0:
                    chunk_gather(v, vg, names[2], 2, t16 // 8)
                    gRs["v"] = Reg(names[2][1])

            def finish():
                for (g_, s_, sR, key) in ((qg, qs_, qsR, "q"), (kg, ks_, ksR, "k")):
                    view = g_.ap().rearrange("(g j t r) n -> g j t r n", g=2, j=jj, t=2)
                    for j in range(jj):
                        P.dma("sp", s_.ap().rearrange("t (j r) n -> j t r n", j=jj)[j],
                              (lambda e, view=view, j=j: view[bass.ds(par(e), 1), j, :, :, :].rearrange("1 t r n -> t r n")),
                              r=[gRs[key]], pw=[sR])
                vview = vg.ap().rearrange("(j t i) (a c) -> j t i a c", j=2, t=2, a=2)
                for j in range(2):
                    P.dma("sp", vs_.ap().rearrange("(t j i) c -> j t i c", t=2, j=2)[j],
                          (lambda e, j=j: vview[j, :, :, bass.ds(par(e), 1), :].rearrange("t i 1 c -> t i c")), r=[gRs["v"]], pw=[vsR])
            hooks = dict(q=hqk(q, qg, names[0], "q"), k=hqk(k, kg, names[1], "k"), v=hv)
            return hooks, (qs_, ks_, vs_, qsR, ksR, vsR), finish

        def attn_loaders(qs_, ks_, vs_, dk, qsR, ksR, vsR, rope):
            qv = qs_.ap().rearrange("t (h d) n -> t h d n", h=8)
            kv = ks_.ap().rearrange("t (h d) n -> t h d n", h=8)
            vv = vs_.ap()

            def mk(view, gR, rows=None, prow=None):
                def f(h, tile, reg):
                    r0, r1 = rows if rows is not None else (0, dk)
                    p0 = prow if prow is not None else r0
                    P.dma("sp", tile[p0:p0 + (r1 - r0), :].rearrange("d (t n) -> d t n", t=2),
                          view[:, h, r0:r1, :].rearrange("t d n -> d t n"), r=[gR(h) if callable(gR) else gR], pw=[reg])
                return f

            def fV(h, tile, reg):
                P.dma("sp", tile[:, :, 0:64], vv[:, h * 64:(h + 1) * 64].rearrange("(g p) d -> p g d", p=128),
                      r=[vsR], pw=[reg])
            ld = dict(K=mk(kv, ksR), Q=mk(qv, qsR), V=fV)
            if rope:
                def sw(view, gR):
                    f1 = mk(view, gR, rows=(80, 96), prow=64)
                    f2 = mk(view, gR, rows=(64, 80), prow=80)
                    return lambda h, tile, reg: (f1(h, tile, reg), f2(h, tile, reg))
                ld["Qs"] = sw(qv, qsR)
                ld["Ks"] = sw(kv, ksR)
            return ld

        def otok_loader(og, ogR, tag):
            os_ = nc.dram_tensor("i_os" + tag, [2, 2048, 512], BF16)
            osR_ = Reg("osel")
            ov = og.ap().rearrange("(j g n) c -> j g n c", j=2, g=2)
            P.dma("sp", os_.ap(), (lambda e: ov[bass.ds(par(e), 1), :, :, :].rearrange("1 g n c -> g n c")), r=[ogR], w=[osR_])

            def f(t16, tile, reg):
                P.dma("sp", tile[:, :].rearrange("n (g c) -> n g c", g=2),
                      os_.ap()[:, t16 * 128:(t16 + 1) * 128, :].rearrange("g n c -> n g c"), r=[osR_], w=[reg])
            return f

        load_xT(E, xT_d)
        phase0_mods(E, cT, ada_w, ada_bT, norm_gT, kv_ada_w, kv_ada_bT, kv_norm_gT)
        ffn_phase(E, a["wg"][0, 0], a["wu"][0, 0], a["wd"][0, 0], 0, 1, 2)
        hooks, sel, fin = make_exchange(q1, k1, v1, q1g, k1g, v1g, 64, "1", 2, (("q_d", "q1g"), ("k_d", "k1g"), ("v_d", "v1g")))
        moba_prep(E, w_qkv, mgq, mgk, q1.ap().rearrange("(h d) n -> h d n", h=16), k1.ap().rearrange("(h d) n -> h d n", h=16),
                  v1.ap(), 3, 4, hooks=hooks)
        attn_phase(E, attn_loaders(sel[0], sel[1], sel[2], 64, sel[3], sel[4], sel[5], False), o1.ap(), 64, 8.0, tri, moba=mo,
                   side=rope_table_jobs(E, pos_d, inv_d, C_d, S_d), pre=fin)
        ogR = gather(o1, o1g, ("a_od", "o1g"), 2)
        wo_phase(E, w_o1, otok_loader(o1g, ogR, "1"), ident, 5)
        ffn_phase(E, a["wg"][0, 1], a["wu"][0, 1], a["wd"][0, 1], 6, 7, 8)
        ffn_phase(E, a["wg"][1, 0], a["wu"][1, 0], a["wd"][1, 0], 9, 10, 11)
        hooks, sel, fin = make_exchange(q2, k2, v2, q2g, k2g, v2g, 96, "2", 4, (("q_d2", "q2g"), ("k_d2", "k2g"), ("v_d2", "v2g")))
        mla_prep(E, w_dq, gqa, w_uq, gq, w_dkv, gkva, wkf, gk, w_uv, q2.ap().rearrange("(h d) n -> h d n", h=16),
                 k2.ap().rearrange("(h d) n -> h d n", h=16), v2.ap(), 12, 13, 18, 19, hooks=hooks)
        attn_phase(E, attn_loaders(sel[0], sel[1], sel[2], 96, sel[3], sel[4], sel[5], True), o2.ap(), 96, float(math.sqrt(96)), tri, rope=ro, pre=fin)
        ogR = gather(o2, o2g, ("a_od", "o2g"), 2)
        wo_phase(E, w_o2, otok_loader(o2g, ogR, "2"), ident, 14)
        ffn_phase(E, a["wg"][1, 1], a["wu"][1, 1], a["wd"][1, 1], 15, 16, 17)
        store_xT(E, out, Reg("outo"))
        P.emit()
    return nc


def kernel(**inp):
    inp = {k: np.asarray(v) for k, v in inp.items()}
    x = inp["x"]
    ada_bT = np.stack([_pl(inp["ada_b"][L]) for L in range(2)])
    norm_gT = np.stack([_pl(inp["norm_g"][L].reshape(-1)) for L in range(2)])
    kk = np.arange(128)[:, None]
    qq = np.arange(128)[None, :]
    tri = (kk <= qq).astype(np.float32).astype(BF)
    bD = _t5_bucket_np(qq - kk)
    bE = _t5_bucket_np(qq - kk + 128)
    rb = inp["rel_bias"]
    onehot = (np.arange(4096)[None, :] // 256 == np.arange(16)[:, None]).astype(np.float32).astype(BF)
    eligadd = np.zeros((8, 4, 16), np.float32); elig01 = np.zeros((8, 4, 16), np.float32); own01 = np.zeros((8, 4, 16), np.float32)
    for j in range(8):
        for ip in range(4):
            qb = (4 * j + ip) // 2
            n = np.arange(16)
            eligadd[j, ip] = np.where(n < qb, 0.0, -1e30)
            elig01[j, ip] = (n < qb)
            own01[j, ip] = (n == qb)
    rep = lambda t: np.ascontiguousarray(np.broadcast_to(t.reshape(1, -1), (128, 512))).astype(np.float32)
    ident = np.eye(128, dtype=np.float32).astype(BF)
    wkf = np.zeros((288, 16, 96), np.float32)
    wkf[0:256, :, 0:64] = inp["w_uk"].reshape(256, 16, 64)
    wkf[256:288, :, 64:96] = np.eye(32, dtype=np.float32)[:, None, :]
    wkf = wkf.reshape(288, 1536)
    invf = (np.float32(10000.0) ** (-np.arange(16, dtype=np.float32) / np.float32(16))).astype(np.float32)
    inv = np.zeros((128, 2), np.float32)
    inv[64:96, 0] = np.tile(invf, 2)
    inv[64:80, 1] = -1.0
    inv[80:96, 1] = 1.0
    shared = dict(wg=inp["ffn_w_gate"], wu=inp["ffn_w_up"], wd=inp["ffn_w_down"], ada_w=inp["ada_w"], ada_bT=ada_bT,
                  norm_gT=norm_gT, kv_ada_w=inp["kv_ada_w"], kv_ada_bT=_pl(inp["kv_ada_b"]), kv_norm_gT=_pl(inp["kv_norm_g"]),
                  w_qkv=inp["moba_w_qkv"][0],
                  mgq=np.tile(inp["moba_q_g"][0], 2).reshape(128, 1).astype(np.float32),
                  mgk=np.tile(inp["moba_k_g"][0], 2).reshape(128, 1).astype(np.float32),
                  w_o1=inp["moba_w_o"][0], w_o2=inp["mla_w_o"][0],
                  w_dq=inp["mla_w_dq"][0], gqa=_pl(inp["mla_q_a_norm_g"][0]), w_uq=inp["mla_w_uq"][0],
                  gq=inp["mla_q_g"][0].reshape(96, 1).astype(np.float32), w_dkv=inp["w_dkv"], gkva=_pl(inp["kv_a_norm_g"]),
                  wkf=wkf, gk=inp["mla_k_g"].reshape(96, 1).astype(np.float32), w_uv=inp["w_uv"],
                  tri=tri, ident=ident, onehot=onehot, eligadd=rep(eligadd), elig01=rep(elig01), own01=rep(own01), inv=inv)
    maps = []
    for b in range(4):
        for c in range(2):
            hs = slice(c * 8, (c + 1) * 8)
            m = dict(shared)
            m.update(xT=np.ascontiguousarray(x[b, c * 2048:(c + 1) * 2048].T), cT=_pl(inp["c"][b]),
                     rawD=np.ascontiguousarray(np.transpose(rb[bD][:, :, hs], (2, 0, 1))).astype(np.float32),
                     rawE=np.ascontiguousarray(np.transpose(rb[bE][:, :, hs], (2, 0, 1))).astype(np.float32),
                     b31=np.ascontiguousarray(np.broadcast_to(rb[31, hs].reshape(1, 8), (128, 8))).astype(np.float32),
                     pos=np.ascontiguousarray(np.broadcast_to(inp["positions"][b].reshape(1, 4096), (32, 4096))).astype(np.int32))
            maps.append(m)
    res = run_bass_kernel_spmd(build_fused(), maps, core_ids=list(range(8))).results
    out = np.zeros((4, 4096, 1024), np.float32)
    for i in range(8):
        b, c = divmod(i, 2)
        out[b, c * 2048:(c + 1) * 2048] = np.asarray(res[i]["outT"]).T
    return out
```

```python
from contextlib import ExitStack
import numpy as np
import concourse.bass as bass
import concourse.mybir as mybir

F32 = mybir.dt.float32
BF16 = mybir.dt.bfloat16
I32 = mybir.dt.int32
ALU = mybir.AluOpType
AF = mybir.ActivationFunctionType
AX = mybir.AxisListType

ENGS = ("pe", "act", "dve", "pool", "sp")


class Reg:
    registry = {}

    def __new__(cls, name=""):
        r = cls.registry.get(name)
        if r is None:
            r = object.__new__(cls)
            r.name = name
            r.writers = []
            r.readers = []
            r.sem = None
            r.dcount = 0
            cls.registry[name] = r
        return r


class Op:
    __slots__ = ("eng", "fn", "deps", "needs_inc", "val", "sem", "is_dma", "pos", "default_inc", "dbg")

    def __init__(self, eng, fn):
        self.eng = eng
        self.fn = fn
        self.deps = {}
        self.needs_inc = False
        self.val = None
        self.sem = None
        self.is_dma = False
        self.pos = 0
        self.default_inc = False


class Prog:
    def __init__(self, nc, es):
        Reg.registry.clear()
        self.nc = nc
        self.es = es
        self.ops = {e: [] for e in ENGS}
        self.n = 0
        self.esem = {}
        self.nsem = 0
        for e in ENGS:
            self.esem[e] = self.newsem("e_" + e)

    def newsem(self, name):
        self.nsem += 1
        return self.es.enter_context(self.nc.semaphore(name + "_%d" % self.nsem))

    def _key(self, op):
        return op.sem if op.is_dma else op.eng

    def _same(self, a, b):
        if a.is_dma != b.is_dma:
            return False
        if a.is_dma:
            return a.sem is b.sem
        return a.eng == b.eng

    def _adddep(self, op, d):
        if d is op:
            return
        if (not d.is_dma) and d.eng == op.eng and not op.is_dma:
            if op.eng == "pe":
                return
        k = id(d.sem) if d.is_dma else d.eng
        cur = op.deps.get(k)
        if cur is None or cur.pos < d.pos:
            op.deps[k] = d

    def _track(self, op, r, w, pw, same_eng_war=False):
        for x in r:
            for d in x.writers:
                self._adddep(op, d)
        for x in w:
            for d in x.writers:
                self._adddep(op, d)
            for d in x.readers:
                if d.eng == op.eng and not d.is_dma and not op.is_dma:
                    continue
                self._adddep(op, d)
        for x in pw:
            if x.readers:
                for d in x.readers:
                    if d.eng == op.eng and not d.is_dma and not op.is_dma:
                        continue
                    self._adddep(op, d)
        for d in op.deps.values():
            d.needs_inc = True
        for x in r:
            x.readers = [q for q in x.readers if not self._same(q, op)] + [op]
        for x in w:
            x.writers = [op]
            x.readers = []
        for x in pw:
            if x.readers:
                x.writers = []
                x.readers = []
            x.writers = [q for q in x.writers if not self._same(q, op)] + [op]

    def op(self, eng, fn, r=(), w=(), pw=()):
        o = Op(eng, fn)
        self.n += 1
        o.pos = self.n
        o.sem = self.esem[eng]
        self._track(o, r, w, pw)
        self.ops[eng].append(o)
        return o

    def wait(self, eng, r=()):
        o = Op(eng, lambda e: None)
        self.n += 1
        o.pos = self.n
        o.sem = self.esem[eng]
        for x in r:
            for d in x.writers:
                self._adddep(o, d)
        for d in o.deps.values():
            d.needs_inc = True
        self.ops[eng].append(o)
        return o

    def dma(self, q, out, in_, r=(), w=(), pw=()):
        dst = (list(w) + list(pw))[0]
        if dst.sem is None:
            dst.sem = self.newsem("d_" + dst.name)
        def _ap(a, e):
            return a(e) if callable(a) else a
        o = Op(q, lambda e: e.dma_start(out=_ap(out, e), in_=_ap(in_, e)))
        o.is_dma = True
        o.dbg = dst.name
        self.n += 1
        o.pos = self.n
        o.sem = dst.sem
        self._track(o, r, w, pw)
        dst.dcount += 16
        o.val = dst.dcount
        o.needs_inc = True
        self.ops[q].append(o)
        return o

    def cc(self, kind, groups, in_ap, out_ap, r=(), w=(), pw=()):
        dst = (list(w) + list(pw))[0]
        sem = self.newsem("cc_" + dst.name)
        o = Op("pool", lambda e: e.collective_compute(kind, ALU.bypass, replica_groups=groups, ins=[in_ap], outs=[out_ap]))
        o.is_dma = True
        o.default_inc = True
        self.n += 1
        o.pos = self.n
        o.sem = sem
        for x in r:
            for d in x.writers:
                self._adddep(o, d)
        self._track(o, (), w, pw)
        o.val = 1
        o.needs_inc = True
        self.ops["pool"].append(o)
        return o

    def final_waits(self):
        out = []
        seen = set()
        for e in ENGS:
            for o in self.ops[e]:
                if o.is_dma:
                    seen.add(id(o.sem))
                    out = [x for x in out if x[0] is not o.sem] + [(o.sem, o.val)]
        return out

    def emit(self):
        nc = self.nc
        for e in ENGS:
            c = 0
            for o in self.ops[e]:
                if o.is_dma:
                    continue
                if o.needs_inc:
                    c += 1
                    o.val = c
        with nc.Block() as block:
            def run(eng_name, eh):
                waited = {}
                for o in self.ops[eng_name]:
                    for d in o.deps.values():
                        k = id(d.sem)
                        if waited.get(k, 0) < d.val:
                            eh.wait_ge(d.sem, d.val)
                            waited[k] = d.val
                    try:
                        ins = o.fn(eh)
                    except Exception:
                        print("EMIT FAIL", eng_name, getattr(o, "dbg", None), o.pos)
                        raise
                    if o.needs_inc:
                        if ins is None:
                            raise RuntimeError("op without instruction needs inc")
                        if o.default_inc:
                            ins.then_inc(o.sem)
                        else:
                            ins.then_inc(o.sem, 16 if o.is_dma else 1)

            @block.tensor
            def _(eh):
                run("pe", eh)

            @block.scalar
            def _(eh):
                run("act", eh)

            @block.vector
            def _(eh):
                run("dve", eh)

            @block.gpsimd
            def _(eh):
                run("pool", eh)

            @block.sync
            def _(eh):
                run("sp", eh)
                for sem, cnt in self.final_waits():
                    eh.wait_ge(sem, cnt)

NTOK = 2048
NT = 4
D = 1024
DC = 8
FF = 2816
EPS = 1e-6
FGROUPS = [(0, 512), (512, 512), (1024, 512), (1536, 512), (2048, 512), (2560, 256)]
NV = 20


import itertools
_UID = itertools.count()


class Env:
    pass


def alloc_common(nc, es, P):
    E = Env()
    E.nc, E.es, E.P = nc, es, P
    sb = lambda n, s, d: es.enter_context(nc.sbuf_tensor("s%d_" % next(_UID) + n, s, d))
    E.sb = sb
    E.ps = [es.enter_context(nc.psum_tensor("ps%d" % i, [128, 512], F32)) for i in range(8)]
    E.psR = [Reg("ps%d" % i) for i in range(8)]
    E.xT = sb("xT", [128, DC, NTOK], F32)
    E.xR = [Reg("xT%d" % t) for t in range(NT)]
    E.dv = sb("dv", [128, 256], F32)
    E.dvR = Reg("dv")
    E.ones = sb("ones", [128, 128], BF16)
    E.onesR = Reg("ones")
    E.one1 = sb("one1", [128, 128], BF16)
    E.one1R = Reg("one1")
    E.epsc = sb("epsc", [128, 1], F32)
    E.epsR = Reg("epsc")
    P.op("pool", lambda e: e.memset(E.ones[:], 1.0 / 1024), w=[E.onesR])
    P.op("pool", lambda e: e.memset(E.one1[:], 1.0), w=[E.one1R])
    P.op("pool", lambda e: e.memset(E.epsc[:], EPS), w=[E.epsR])
    P.op("pool", lambda e: e.memset(E.dv[:, 248:256], 0.0), pw=[E.dvR])
    return E


def dvcol(i):
    return slice(8 * i, 8 * i + 8)


def phase0_mods(E, cT, ada_w, ada_bT, norm_gT, kv_ada_w, kv_ada_bT, kv_norm_gT):
    nc, P, sb = E.nc, E.P, E.sb
    with ExitStack() as es2:
        sb2 = lambda n, s, d: es2.enter_context(nc.sbuf_tensor("s%d_" % next(_UID) + n, s, d))
        cs = sb2("p0_c", [128, 8], F32)
        cact = sb2("p0_cact", [128, 8], BF16)
        bT = sb2("p0_b", [128, 160], F32)
        gT = sb2("p0_g", [128, 56], F32)
        raw = sb2("p0_raw", [128, 160], F32)
        wp = [sb2("p0_w%d" % i, [128, 8, 1024], BF16) for i in range(2)]
        wR = [Reg("p0w%d" % i) for i in range(2)]
        cR, caR, bR, gR, rawR = Reg("p0c"), Reg("p0ca"), Reg("p0b"), Reg("p0g"), Reg("p0raw")
        P.dma("sp", cs[:], cT, w=[cR])
        P.dma("sp", bT[:, 0:72], ada_bT[0], pw=[bR])
        P.dma("sp", bT[:, 72:144], ada_bT[1], pw=[bR])
        P.dma("sp", bT[:, 144:160], kv_ada_bT, pw=[bR])
        P.dma("sp", gT[:, 0:24], norm_gT[0], pw=[gR])
        P.dma("sp", gT[:, 24:48], norm_gT[1], pw=[gR])
        P.dma("sp", gT[:, 48:56], kv_norm_gT, pw=[gR])
        P.op("act", lambda e: e.activation(cact[:], cs[:], AF.Silu), r=[cR], w=[caR])
        pieces = []
        for L in range(2):
            for v in range(9):
                pieces.append((ada_w[L].rearrange("(k p) f -> p k f", p=128)[:, :, v * 1024:(v + 1) * 1024], 9 * L + v))
        for v in range(2):
            pieces.append((kv_ada_w.rearrange("(k p) f -> p k f", p=128)[:, :, v * 1024:(v + 1) * 1024], 18 + v))
        psb = E.ps[7]
        for n, (src, vi) in enumerate(pieces):
            b = n % 2
            P.dma("pool", wp[b][:], src, w=[wR[b]])
            pst = E.ps[6 + (n % 2)]
            psr = E.psR[6 + (n % 2)]
            for kc in range(8):
                for ic in range(8):
                    P.op("pe", (lambda e, b=b, kc=kc, ic=ic, pst=pst: e.matmul(
                        pst[:, kc:kc + 1], wp[b][:, ic, kc * 128:(kc + 1) * 128], cact[:, ic:ic + 1],
                        start=(ic == 0), stop=(ic == 7))), r=[wR[b], caR], pw=[psr])
            P.op("dve", (lambda e, vi=vi, pst=pst: e.tensor_tensor(
                raw[:, dvcol(vi)], pst[:, 0:8], bT[:, dvcol(vi)], ALU.add)), r=[psr, bR], pw=[rawR])
        dv = E.dv
        for L in range(2):
            for j in range(3):
                base = 9 * L + 3 * j
                sh, sc, gg = base, base + 1, base + 2
                gcol = slice(24 * L + 8 * j, 24 * L + 8 * j + 8)
                P.op("dve", (lambda e, sc=sc, gcol=gcol, base=base: e.scalar_tensor_tensor(
                    dv[:, dvcol(base)], raw[:, dvcol(sc)], 1.0, gT[:, gcol], ALU.add, ALU.mult)),
                    r=[rawR, gR], pw=[E.dvR])
                P.op("dve", (lambda e, sh=sh, base=base: e.tensor_copy(dv[:, dvcol(base + 1)], raw[:, dvcol(sh)])),
                     r=[rawR], pw=[E.dvR])
                mul = 1.0 if j == 1 else 0.5
                P.op("dve", (lambda e, gg=gg, base=base, mul=mul: e.tensor_scalar(
                    dv[:, dvcol(base + 2)], raw[:, dvcol(gg)], mul, None, ALU.mult)),
                    r=[rawR], pw=[E.dvR])
        P.op("dve", lambda e: e.scalar_tensor_tensor(dv[:, dvcol(18)], raw[:, dvcol(19)], 1.0, gT[:, 48:56], ALU.add, ALU.mult),
             r=[rawR, gR], pw=[E.dvR])
        P.op("dve", lambda e: e.tensor_copy(dv[:, dvcol(19)], raw[:, dvcol(18)]), r=[rawR], pw=[E.dvR])
        fence(E, [[rawR, gR, bR, caR, cR, wR[0], wR[1]]])


def norm_mod(E, Acol, Bcol, hT, hR, tmpb, tmpR, sqb, sqR, rsb, rsR, psi=6):
    P = E.P
    dv = E.dv
    pst, psr = E.ps[psi], E.psR[psi]
    for tt in range(NT):
        ts = slice(tt * 512, (tt + 1) * 512)
        for k in range(DC):
            b = k % 2
            P.op("act", (lambda e, k=k, b=b, ts=ts: e.activation(sqb[b][:], E.xT[:, k, ts], AF.Square)),
                 r=[E.xR[tt]], w=[sqR[b]])
            P.op("pe", (lambda e, k=k, b=b: e.matmul(pst[:], E.ones[:], sqb[b][:], start=(k == 0), stop=(k == DC - 1))),
                 r=[sqR[b], E.onesR], pw=[psr])
        P.op("act", lambda e: e.activation(rsb[:], pst[:], AF.Sqrt, bias=E.epsc[:, 0:1], scale=1.0),
             r=[psr, E.epsR], w=[rsR])
        P.op("dve", lambda e: e.reciprocal(rsb[:], rsb[:]), r=[rsR], w=[rsR])
        for k in range(DC):
            b = k % 2
            P.op("dve", (lambda e, k=k, b=b, ts=ts: e.tensor_tensor(tmpb[b][:], E.xT[:, k, ts], rsb[:], ALU.mult)),
                 r=[E.xR[tt], rsR], w=[tmpR[b]])
            P.op("act", (lambda e, k=k, b=b, ts=ts: e.activation(
                hT[:, k, ts], tmpb[b][:], AF.Identity, bias=dv[:, 8 * Bcol + k:8 * Bcol + k + 1],
                scale=dv[:, 8 * Acol + k:8 * Acol + k + 1])), r=[tmpR[b], E.dvR], pw=[hR[tt]])


def ffn_phase(E, wg, wu, wd, Acol, Bcol, Gcol):
    nc, P = E.nc, E.P
    dv = E.dv
    with ExitStack() as es2:
        sb2 = lambda n, s, d: es2.enter_context(nc.sbuf_tensor("s%d_" % next(_UID) + n, s, d))
        hT = sb2("f_hT", [128, DC, NTOK], BF16)
        hR = [Reg("f_hT%d" % t) for t in range(NT)]
        wgb = [sb2("f_wg%d" % i, [128, DC, 512], BF16) for i in range(2)]
        wub = [sb2("f_wu%d" % i, [128, DC, 512], BF16) for i in range(2)]
        wdb = [sb2("f_wd%d" % i, [128, 4, D], BF16) for i in range(2)]
        wgR = [Reg("f_wg%d" % i) for i in range(2)]
        wuR = [Reg("f_wu%d" % i) for i in range(2)]
        wdR = [Reg("f_wd%d" % i) for i in range(2)]
        act = [sb2("f_act%d" % i, [128, 4, NTOK], BF16) for i in range(2)]
        actR = [[Reg("f_act%d_%d" % (i, t)) for t in range(NT)] for i in range(2)]
        tmpb = [sb2("f_tmp%d" % i, [128, 512], F32) for i in range(2)]
        tmpR = [Reg("f_tmp%d" % i) for i in range(2)]
        sqb = [sb2("f_sq%d" % i, [128, 512], BF16) for i in range(2)]
        sqR = [Reg("f_sq%d" % i) for i in range(2)]
        rsb = sb2("f_rs", [128, 512], F32)
        rsR = Reg("f_rs")

        def load_group(gi):
            f0, fw = FGROUPS[gi]
            b = gi % 2
            nch = fw // 128
            P.dma("pool", wgb[b][:, :, 0:fw], wg.rearrange("(k p) f -> p k f", p=128)[:, :, f0:f0 + fw], w=[wgR[b]])
            P.dma("pool", wub[b][:, :, 0:fw], wu.rearrange("(k p) f -> p k f", p=128)[:, :, f0:f0 + fw], w=[wuR[b]])
            P.dma("pool", wdb[b][:, 0:nch, :], wd[f0:f0 + fw, :].rearrange("(k p) d -> p k d", p=128), w=[wdR[b]])

        P.wait("dve", r=E.xR)
        load_group(0)
        norm_mod(E, Acol, Bcol, hT, hR, tmpb, tmpR, sqb, sqR, rsb, rsR)
        cnt = 0
        for gi, (f0, fw) in enumerate(FGROUPS):
            b = gi % 2
            nch = fw // 128
            if gi + 1 < len(FGROUPS):
                load_group(gi + 1)
            for tt in range(NT):
                ts = slice(tt * 512, (tt + 1) * 512)
                for fc in range(nch):
                    pb = cnt % 2
                    cnt += 1
                    pg, pgR = E.ps[pb], E.psR[pb]
                    pu, puR = E.ps[2 + pb], E.psR[2 + pb]
                    for k in range(DC):
                        P.op("pe", (lambda e, k=k, fc=fc, b=b, pg=pg, ts=ts: e.matmul(
                            pg[:], wgb[b][:, k, fc * 128:(fc + 1) * 128], hT[:, k, ts],
                            start=(k == 0), stop=(k == DC - 1))), r=[wgR[b], hR[tt]], pw=[pgR])
                    for k in range(DC):
                        P.op("pe", (lambda e, k=k, fc=fc, b=b, pu=pu, ts=ts: e.matmul(
                            pu[:], wub[b][:, k, fc * 128:(fc + 1) * 128], hT[:, k, ts],
                            start=(k == 0), stop=(k == DC - 1))), r=[wuR[b], hR[tt]], pw=[puR])
                    P.op("act", (lambda e, pb=pb, pg=pg: e.activation(tmpb[pb][:], pg[:], AF.Silu)),
                         r=[pgR], w=[tmpR[pb]])
                    P.op("dve", (lambda e, pb=pb, pu=pu, b=b, fc=fc, ts=ts: e.tensor_tensor(
                        act[b][:, fc, ts], tmpb[pb][:], pu[:], ALU.mult)), r=[tmpR[pb], puR], pw=[actR[b][tt]])
            for tt in range(NT):
                ts = slice(tt * 512, (tt + 1) * 512)
                for oc in range(DC):
                    pb = cnt % 2
                    cnt += 1
                    pd, pdR = E.ps[4 + pb], E.psR[4 + pb]
                    for fc in range(nch):
                        P.op("pe", (lambda e, fc=fc, oc=oc, b=b, pd=pd, ts=ts: e.matmul(
                            pd[:], wdb[b][:, fc, oc * 128:(oc + 1) * 128], act[b][:, fc, ts],
                            start=(fc == 0), stop=(fc == nch - 1))), r=[wdR[b], actR[b][tt]], pw=[pdR])
                    P.op("dve", (lambda e, oc=oc, pd=pd, ts=ts: e.scalar_tensor_tensor(
                        E.xT[:, oc, ts], pd[:], dv[:, 8 * Gcol + oc:8 * Gcol + oc + 1], E.xT[:, oc, ts],
                        ALU.mult, ALU.add)), r=[pdR, E.dvR], pw=[E.xR[tt]])
        fence(E, [hR, wgR, wuR, wdR, actR[0], actR[1], tmpR, sqR, [rsR]])


def fence(E, reglists):
    regs = [x for l in reglists for x in l]
    P = E.P
    if not hasattr(E, "fenceR"):
        E.fenceR = Reg("fence")
    o = P.op("dve", lambda e: e.tensor_copy(E.dv[:, 255:256], E.dv[:, 254:255]), r=[E.dvR], w=regs + [E.fenceR])
    for eng in ("pe", "act", "pool", "sp"):
        P.wait(eng, r=[E.fenceR])


def load_xT(E, xT_d):
    P = E.P
    for tt in range(NT):
        ts = slice(tt * 512, (tt + 1) * 512)
        P.dma("sp", E.xT[:, :, ts], xT_d.rearrange("(k p) t -> p k t", p=128)[:, :, ts], w=[E.xR[tt]])


def store_xT(E, out_d, outR):
    P = E.P
    for tt in range(NT):
        ts = slice(tt * 512, (tt + 1) * 512)
        P.dma("sp", out_d.rearrange("(k p) t -> p k t", p=128)[:, :, ts], E.xT[:, :, ts], r=[E.xR[tt]], pw=[outR])


def load_w(E, sb2, name, view, shape, q="pool"):
    t = sb2(name, shape, BF16)
    R = Reg(name)
    E.P.dma(q, t[:], view, w=[R])
    return t, R


def rms_chunks(E, zps_list, gcols, outT, outR, tt, nfeat, tmpb, tmpR, sqb, sqR, rsb, rsR, sspsi=6):
    P = E.P
    ts = slice(tt * 512, (tt + 1) * 512)
    pss, pssR = E.ps[sspsi], E.psR[sspsi]
    n = len(zps_list)
    for ci, (pz, pzR, nr) in enumerate(zps_list):
        b = ci % 2
        P.op("act", (lambda e, pz=pz, nr=nr, b=b: e.activation(sqb[b][0:nr, :], pz[0:nr, :], AF.Square)),
             r=[pzR], w=[sqR[b]])
        P.op("pe", (lambda e, nr=nr, b=b, ci=ci: e.matmul(pss[:], E.one1[0:nr, :], sqb[b][0:nr, :],
                                                          start=(ci == 0), stop=(ci == n - 1))),
             r=[sqR[b], E.one1R], pw=[pssR])
    P.op("act", lambda e: e.activation(rsb[:], pss[:], AF.Sqrt, bias=E.epsn[:, nfeat:nfeat + 1], scale=1.0),
         r=[pssR, E.epsR], w=[rsR])
    P.op("dve", lambda e: e.reciprocal(rsb[:], rsb[:]), r=[rsR], w=[rsR])
    for ci, (pz, pzR, nr) in enumerate(zps_list):
        P.op("dve", (lambda e, pz=pz, nr=nr, ci=ci: e.scalar_tensor_tensor(
            outT[0:nr, ci, ts], pz[0:nr, :], gcols[ci], rsb[0:nr, :], ALU.mult, ALU.mult)),
            r=[pzR, rsR, E.gvR], pw=[outR[tt]])


def head_norm_proj(E, Wsb, wR, kparts, zT, zR, dk, gcol, out_d, outR, stg, stgR, sqb, sqR, rsb, rsR, nfeat_idx, hook=None):
    P = E.P
    cnt = 0
    for h in range(16):
        for tt in range(NT):
            ts = slice(tt * 512, (tt + 1) * 512)
            pb = cnt % 2
            cnt += 1
            pq, pqR = E.ps[pb], E.psR[pb]
            pss, pssR = E.ps[2 + pb], E.psR[2 + pb]
            nk = len(kparts)
            for ci, ksz in enumerate(kparts):
                P.op("pe", (lambda e, ci=ci, ksz=ksz, h=h, pq=pq, ts=ts: e.matmul(
                    pq[0:dk, :], Wsb[0:ksz, ci, h * dk:(h + 1) * dk], zT[0:ksz, ci, ts],
                    start=(ci == 0), stop=(ci == nk - 1))), r=[wR, zR[tt]], pw=[pqR])
            P.op("act", (lambda e, pq=pq, pb=pb: e.activation(sqb[pb][0:dk, :], pq[0:dk, :], AF.Square)),
                 r=[pqR], w=[sqR[pb]])
            P.op("pe", (lambda e, pss=pss, pb=pb: e.matmul(pss[0:dk, :], E.one1[0:dk, 0:dk], sqb[pb][0:dk, :],
                                                          start=True, stop=True)),
                 r=[sqR[pb], E.one1R], pw=[pssR])
            P.op("act", (lambda e, pss=pss, pb=pb: e.activation(
                rsb[pb][0:dk, :], pss[0:dk, :], AF.Sqrt, bias=E.epsn[0:dk, nfeat_idx:nfeat_idx + 1], scale=1.0)),
                r=[pssR, E.epsR], w=[rsR[pb]])
            P.op("dve", (lambda e, pb=pb: e.reciprocal(rsb[pb][0:dk, :], rsb[pb][0:dk, :])), r=[rsR[pb]], w=[rsR[pb]])
            P.op("dve", (lambda e, pq=pq, pb=pb: e.scalar_tensor_tensor(
                stg[pb][0:dk, :], pq[0:dk, :], gcol[0:dk, :], rsb[pb][0:dk, :], ALU.mult, ALU.mult)),
                r=[pqR, rsR[pb], E.gvR], w=[stgR[pb]])
            P.dma("sp", out_d[h, :, ts], stg[pb][0:dk, :], r=[stgR[pb]], pw=[outR(h) if callable(outR) else outR])
        if hook:
            hook(h)


def v_proj(E, Wv, wvR, kparts, zT, zR, v_d, vR, stg, stgR, hook=None):
    P = E.P
    cnt = 0
    nk = len(kparts)
    for t16 in range(16):
        tt = t16 // 4
        tk = slice(t16 * 128, (t16 + 1) * 128)
        sb_ = t16 % 2
        for vc in range(2):
            pb = cnt % 2
            cnt += 1
            pv, pvR = E.ps[4 + pb], E.psR[4 + pb]
            for ci, ksz in enumerate(kparts):
                P.op("pe", (lambda e, ci=ci, ksz=ksz, vc=vc, pv=pv, tk=tk: e.matmul(
                    pv[:], zT[0:ksz, ci, tk], Wv[0:ksz, ci, vc * 512:(vc + 1) * 512],
                    start=(ci == 0), stop=(ci == nk - 1))), r=[wvR, zR[tt]], pw=[pvR])
            P.op("act", (lambda e, pv=pv, vc=vc, sb_=sb_: e.activation(stg[sb_][:, vc * 512:(vc + 1) * 512], pv[:], AF.Copy)),
                 r=[pvR], pw=[stgR[sb_]])
        P.dma("sp", v_d[tk, :], stg[sb_][:], r=[stgR[sb_]], pw=[vR(t16) if callable(vR) else vR])
        if hook:
            hook(t16)


def setup_eps(E):
    E.epsn = E.sb("epsn", [128, 4], F32)
    for i, n in enumerate((64, 96, 256, 384)):
        E.P.op("pool", (lambda e, i=i, n=n: e.memset(E.epsn[:, i:i + 1], n * EPS)), pw=[E.epsR])


def moba_prep(E, w_qkv, gq_d, gk_d, q_d, k_d, v_d, Acol, Bcol, hooks=None):
    nc, P = E.nc, E.P
    with ExitStack() as es2:
        sb2 = lambda n, s, d: es2.enter_context(nc.sbuf_tensor("s%d_" % next(_UID) + n, s, d))
        hT = sb2("m_hT", [128, DC, NTOK], BF16)
        hR = [Reg("m_hT%d" % t) for t in range(NT)]
        tmpb = [sb2("m_tmp%d" % i, [128, 512], F32) for i in range(2)]
        tmpR = [Reg("m_tmp%d" % i) for i in range(2)]
        sqb = [sb2("m_sq%d" % i, [128, 512], BF16) for i in range(2)]
        sqR = [Reg("m_sq%d" % i) for i in range(2)]
        rs1 = sb2("m_rs", [128, 512], F32)
        rs1R = Reg("m_rs")
        rsb = [sb2("m_rsb%d" % i, [128, 512], F32) for i in range(2)]
        rsR = [Reg("m_rsb%d" % i) for i in range(2)]
        stg = [sb2("m_stg%d" % i, [128, 512], BF16) for i in range(2)]
        stgR = [Reg("m_stg%d" % i) for i in range(2)]
        vst = [sb2("m_vst%d" % i, [128, 1024], BF16) for i in range(2)]
        vstR = [Reg("m_vst%d" % i) for i in range(2)]
        gv = sb2("m_gv", [128, 2], F32)
        E.gvR = Reg("m_gv")
        P.dma("sp", gv[:, 0:1], gq_d, pw=[E.gvR])
        P.dma("sp", gv[:, 1:2], gk_d, pw=[E.gvR])
        wv3 = w_qkv.rearrange("(k p) f -> p k f", p=128)
        Wq, wqR = load_w(E, sb2, "m_wq", wv3[:, :, 0:1024], [128, 8, 1024])
        Wk, wkR = load_w(E, sb2, "m_wk", wv3[:, :, 1024:2048], [128, 8, 1024])
        Wv, wvR = load_w(E, sb2, "m_wv", wv3[:, :, 2048:3072], [128, 8, 1024])
        norm_mod(E, Acol, Bcol, hT, hR, tmpb, tmpR, sqb, sqR, rs1, rs1R)
        qR, kR, vR = Reg("q_d"), Reg("k_d"), Reg("v_d")
        kp = [128] * 8
        H = hooks or {}
        head_norm_proj(E, Wk, wkR, kp, hT, hR, 64, gv[:, 1:2], k_d, H.get("kR", kR), stg, stgR, sqb, sqR, rsb, rsR, 0, hook=H.get("k"))
        v_proj(E, Wv, wvR, kp, hT, hR, v_d, H.get("vR", vR), vst, vstR, hook=H.get("v"))
        head_norm_proj(E, Wq, wqR, kp, hT, hR, 64, gv[:, 0:1], q_d, H.get("qR", qR), stg, stgR, sqb, sqR, rsb, rsR, 0, hook=H.get("q"))
        fence(E, [hR, tmpR, sqR, [rs1R], rsR, stgR, vstR, [E.gvR, wqR, wkR, wvR]])
        E.outRs = getattr(E, "outRs", []) + [qR, kR, vR]


def wo_phase(E, w_o, ld_otok, ident_d, Gcol):
    nc, P = E.nc, E.P
    dv = E.dv
    with ExitStack() as es2:
        sb2 = lambda n, s, d: es2.enter_context(nc.sbuf_tensor("s%d_" % next(_UID) + n, s, d))
        oT = sb2("w_oT", [128, DC, NTOK], BF16)
        oR = [Reg("w_oT%d" % t) for t in range(NT)]
        otok = [sb2("w_otok%d" % i, [128, 1024], BF16) for i in range(2)]
        otR = [Reg("w_otok%d" % i) for i in range(2)]
        ident = sb2("w_id", [128, 128], BF16)
        identR = Reg("w_id")
        P.dma("sp", ident[:], ident_d, w=[identR])
        Wo, woR = load_w(E, sb2, "w_wo", w_o.rearrange("(k p) f -> p k f", p=128), [128, 8, 1024])
        for t16 in range(16):
            tt = t16 // 4
            b = t16 % 2
            ld_otok(t16, otok[b], otR[b])
            for half in range(2):
                pz, pzR = E.ps[2 * b + half], E.psR[2 * b + half]
                for kq in range(4):
                    k = half * 4 + kq
                    P.op("pe", (lambda e, k=k, kq=kq, b=b, pz=pz: e.matmul(
                        pz[:, kq * 128:(kq + 1) * 128], otok[b][:, k * 128:(k + 1) * 128], ident[:],
                        start=True, stop=True)), r=[otR[b], identR], pw=[pzR])
                P.op("act", (lambda e, half=half, pz=pz, t16=t16: e.activation(
                    oT[:, half * 4:(half + 1) * 4, t16 * 128:(t16 + 1) * 128],
                    pz[:].rearrange("p (k t) -> p k t", t=128), AF.Copy)), r=[pzR], pw=[oR[tt]])
        P.wait("dve", r=E.xR)
        cnt = 0
        for tt in range(NT):
            ts = slice(tt * 512, (tt + 1) * 512)
            for oc in range(DC):
                pb = cnt % 2
                cnt += 1
                pd, pdR = E.ps[4 + pb], E.psR[4 + pb]
                for k in range(DC):
                    P.op("pe", (lambda e, k=k, oc=oc, pd=pd, ts=ts: e.matmul(
                        pd[:], Wo[:, k, oc * 128:(oc + 1) * 128], oT[:, k, ts],
                        start=(k == 0), stop=(k == DC - 1))), r=[woR, oR[tt]], pw=[pdR])
                P.op("dve", (lambda e, oc=oc, pd=pd, ts=ts: e.scalar_tensor_tensor(
                    E.xT[:, oc, ts], pd[:], dv[:, 8 * Gcol + oc:8 * Gcol + oc + 1], E.xT[:, oc, ts],
                    ALU.mult, ALU.add)), r=[pdR, E.dvR], pw=[E.xR[tt]])
        fence(E, [oR, otR, [woR, identR]])


def proj_chunks(E, Wsb, wR, zT, zR, mparts, tt, psbase=0):
    P = E.P
    ts = slice(tt * 512, (tt + 1) * 512)
    res = []
    m0 = 0
    for mi, msz in enumerate(mparts):
        pz, pzR = E.ps[psbase + mi], E.psR[psbase + mi]
        for k in range(DC):
            P.op("pe", (lambda e, k=k, m0=m0, msz=msz, pz=pz: e.matmul(
                pz[0:msz, :], Wsb[:, k, m0:m0 + msz], zT[:, k, ts], start=(k == 0), stop=(k == DC - 1))),
                r=[wR, zR[tt]], pw=[pzR])
        res.append((pz, pzR, msz))
        m0 += msz
    return res


def mla_prep(E, w_dq, gqa_d, w_uq, gq_d, w_dkv, gkva_d, wk_full, gk_d, w_uv, q_d, k_d, v_d, A2, B2, Akv, Bkv, hooks=None):
    nc, P = E.nc, E.P
    with ExitStack() as es2:
        sb2 = lambda n, s, d: es2.enter_context(nc.sbuf_tensor("s%d_" % next(_UID) + n, s, d))
        hT = sb2("l_hT", [128, DC, NTOK], BF16)
        hR = [Reg("l_hT%d" % t) for t in range(NT)]
        zT = sb2("l_zT", [128, 3, NTOK], BF16)
        zR = [Reg("l_zT%d" % t) for t in range(NT)]
        tmpb = [sb2("l_tmp%d" % i, [128, 512], F32) for i in range(2)]
        tmpR = [Reg("l_tmp%d" % i) for i in range(2)]
        sqb = [sb2("l_sq%d" % i, [128, 512], BF16) for i in range(2)]
        sqR = [Reg("l_sq%d" % i) for i in range(2)]
        rs1 = sb2("l_rs", [128, 512], F32)
        rs1R = Reg("l_rs")
        rsb = [sb2("l_rsb%d" % i, [128, 512], F32) for i in range(2)]
        rsR = [Reg("l_rsb%d" % i) for i in range(2)]
        stg = [sb2("l_stg%d" % i, [128, 512], BF16) for i in range(2)]
        stgR = [Reg("l_stg%d" % i) for i in range(2)]
        vst = [sb2("l_vst%d" % i, [128, 1024], BF16) for i in range(2)]
        vstR = [Reg("l_vst%d" % i) for i in range(2)]
        gv = sb2("l_gv", [128, 16], F32)
        E.gvR = Reg("l_gv")
        P.dma("sp", gv[:, 0:3], gqa_d, pw=[E.gvR])
        P.dma("sp", gv[:, 3:5], gkva_d, pw=[E.gvR])
        P.dma("sp", gv[0:96, 5:6], gq_d, pw=[E.gvR])
        P.dma("sp", gv[0:96, 6:7], gk_d, pw=[E.gvR])
        P.op("dve", lambda e: e.tensor_scalar(gv[:, 8:11], gv[:, 0:3], float(np.sqrt(384.0)), None, ALU.mult), r=[E.gvR], pw=[E.gvR])
        P.op("dve", lambda e: e.tensor_scalar(gv[:, 11:13], gv[:, 3:5], float(np.sqrt(256.0)), None, ALU.mult), r=[E.gvR], pw=[E.gvR])
        qR, kR, vR = Reg("q_d2"), Reg("k_d2"), Reg("v_d2")
        H = hooks or {}
        Wdq, wdqR = load_w(E, sb2, "l_wdq", w_dq.rearrange("(k p) f -> p k f", p=128), [128, 8, 384])
        Wuq, wuqR = load_w(E, sb2, "l_wuq", w_uq.rearrange("(k p) f -> p k f", p=128), [128, 3, 1536])
        Wdkv, wdkvR = load_w(E, sb2, "l_wdkv", w_dkv.rearrange("(k p) f -> p k f", p=128), [128, 8, 288])
        Wkf = sb2("l_wkf", [128, 3, 1536], BF16)
        wkfR = Reg("l_wkf")
        P.dma("pool", Wkf[:, 0:2, :], wk_full[0:256, :].rearrange("(k p) f -> p k f", p=128), pw=[wkfR])
        P.dma("pool", Wkf[0:32, 2, :], wk_full[256:288, :], pw=[wkfR])
        Wuv, wuvR = load_w(E, sb2, "l_wuv", w_uv.rearrange("(k p) f -> p k f", p=128), [128, 2, 1024])
        norm_mod(E, Akv, Bkv, hT, hR, tmpb, tmpR, sqb, sqR, rs1, rs1R)
        for tt in range(NT):
            ts = slice(tt * 512, (tt + 1) * 512)
            zps = proj_chunks(E, Wdkv, wdkvR, hT, hR, [128, 128, 32], tt, psbase=0)
            rms_chunks(E, zps[0:2], [gv[:, 11:12], gv[:, 12:13]], zT, zR, tt, 2, tmpb, tmpR, sqb, sqR, rs1, rs1R)
            pz, pzR, _ = zps[2]
            P.op("act", (lambda e, pz=pz, ts=ts: e.activation(zT[0:32, 2, ts], pz[0:32, :], AF.Copy)), r=[pzR], pw=[zR[tt]])
        head_norm_proj(E, Wkf, wkfR, [128, 128, 32], zT, zR, 96, gv[:, 6:7], k_d, H.get("kR", kR), stg, stgR, sqb, sqR, rsb, rsR, 1, hook=H.get("k"))
        v_proj(E, Wuv, wuvR, [128, 128], zT, zR, v_d, H.get("vR", vR), vst, vstR, hook=H.get("v"))
        norm_mod(E, A2, B2, hT, hR, tmpb, tmpR, sqb, sqR, rs1, rs1R)
        for tt in range(NT):
            zps = proj_chunks(E, Wdq, wdqR, hT, hR, [128, 128, 128], tt, psbase=0)
            rms_chunks(E, zps, [gv[:, 8:9], gv[:, 9:10], gv[:, 10:11]], zT, zR, tt, 3, tmpb, tmpR, sqb, sqR, rs1, rs1R)
        head_norm_proj(E, Wuq, wuqR, [128, 128, 128], zT, zR, 96, gv[:, 5:6], q_d, H.get("qR", qR), stg, stgR, sqb, sqR, rsb, rsR, 1, hook=H.get("q"))
        fence(E, [hR, zR, tmpR, sqR, [rs1R], rsR, stgR, vstR, [E.gvR, wdqR, wuqR, wdkvR, wkfR, wuvR]])
        E.outRs = getattr(E, "outRs", []) + [qR, kR, vR]


SEQ = 4096
NQT = 8
NKT = 32
BIGM = 30000.0


def attn_phase(E, ld, o_d, dk, scale, tri_d, moba=None, rope=None, side=None, pre=None):
    return _attn_phase(E, ld, o_d, dk, scale, tri_d, moba, rope, side, pre)


def _attn_phase(E, ld, o_d, dk, scale, tri_d, moba, rope, side_factory, pre):
    nc, P = E.nc, E.P
    with ExitStack() as es2:
        sb2 = lambda n, s, d: es2.enter_context(nc.sbuf_tensor("s%d_" % next(_UID) + n, s, d))
        Kb = [sb2("a_K%d" % i, [128, SEQ], BF16) for i in range(2)]
        Qb = [sb2("a_Q%d" % i, [128, SEQ], BF16) for i in range(2)]
        Vb = [sb2("a_V%d" % i, [128, NKT, 65], BF16) for i in range(2)]
        KR = [Reg("a_K%d" % i) for i in range(2)]
        QR = [Reg("a_Q%d" % i) for i in range(2)]
        VR = [Reg("a_V%d" % i) for i in range(2)]
        QmR = [[Reg("a_Qm%d_%d" % (i, j)) for j in range(NQT)] for i in range(2)]
        Pb = [sb2("a_P%d" % i, [128, 512], BF16) for i in range(4)]
        PR = [Reg("a_P%d" % i) for i in range(4)]
        osb = [sb2("a_o%d" % i, [128, 4, 64], BF16) for i in range(2)]
        osR = [Reg("a_o%d" % i) for i in range(2)]
        rec = [sb2("a_rec%d" % i, [128, 4], F32) for i in range(2)]
        recR = [Reg("a_rec%d" % i) for i in range(2)]
        tri = sb2("a_tri", [128, 128], BF16)
        triR = Reg("a_tri")
        P.dma("sp", tri[:], tri_d, w=[triR])
        oR = Reg("a_od")
        E.outRs = getattr(E, "outRs", []) + [oR]
        for i in range(2):
            P.op("pool", (lambda e, i=i: e.memset(Vb[i][:, :, 64:65], 1.0)), pw=[VR[i]])
        Dt = Et = None
        if moba is not None:
            for i in range(2):
                P.dma("sp", Kb[i][64:80, :], moba["onehot_d"], pw=[KR[i]])
            cst = sb2("a_cst", [128, 3, 512], F32)
            cstR = Reg("a_cst")
            P.dma("sp", cst[:, 0, :], moba["eligadd_d"], pw=[cstR])
            P.dma("sp", cst[:, 1, :], moba["elig01_d"], pw=[cstR])
            P.dma("sp", cst[:, 2, :], moba["own01_d"], pw=[cstR])
            ident = sb2("a_id", [128, 128], BF16)
            identR = Reg("a_id")
            P.dma("sp", ident[:], moba["ident_d"], w=[identR])
            nb31 = sb2("a_nb31", [128, 8], F32)
            nbR = Reg("a_nb31")
            P.dma("sp", nb31[:], moba["b31_d"], w=[nbR])
            P.op("dve", lambda e: e.tensor_scalar(nb31[:], nb31[:], -1.0, None, ALU.mult), r=[nbR], w=[nbR])
            rawt = [sb2("a_raw%d" % i, [128, 128], F32) for i in range(2)]
            rawR = [Reg("a_raw%d" % i) for i in range(2)]
            Dt = sb2("a_D", [128, 8, 128], BF16)
            Et = sb2("a_E", [128, 8, 128], BF16)
            DR, ER = Reg("a_D"), Reg("a_E")
            for h in range(8):
                P.dma("sp", rawt[0][:], moba["rawD_d"][h], w=[rawR[0]])
                P.op("act", (lambda e, h=h: e.activation(rawt[0][:], rawt[0][:], AF.Exp, bias=nb31[:, h:h + 1], scale=1.0)),
                     r=[nbR], w=[rawR[0]])
                P.op("dve", (lambda e, h=h: e.tensor_tensor(Dt[:, h, :], rawt[0][:], tri[:], ALU.mult)),
                     r=[rawR[0], triR], pw=[DR])
                P.dma("sp", rawt[1][:], moba["rawE_d"][h], w=[rawR[1]])
                P.op("act", (lambda e, h=h: e.activation(Et[:, h, :], rawt[1][:], AF.Exp, bias=nb31[:, h:h + 1], scale=1.0)),
                     r=[rawR[1], nbR], pw=[ER])
            ks32 = [sb2("a_ks32_%d" % i, [64, 16], F32) for i in range(2)]
            ksb = [sb2("a_ksb_%d" % i, [64, 16], BF16) for i in range(2)]
            ks32R = [Reg("a_ks32_%d" % i) for i in range(2)]
            ksbR = [Reg("a_ksb_%d" % i) for i in range(2)]
            gm = sb2("a_gm", [128, 64], F32)
            sel = sb2("a_sel", [128, 64], F32)
            mx8 = sb2("a_mx8", [128, 4, 8], F32)
            gmR, selR, mxR = Reg("a_gm"), Reg("a_sel"), Reg("a_mx8")
            Z = sb2("a_Z", [128, 4, 80], BF16)
            ZR = Reg("a_Z")
            P.op("pool", lambda e: e.memset(Z[:], 0.0), w=[ZR])
        if rope is not None:
            Ct = sb2("a_C", [128, SEQ], F32)
            St = sb2("a_S", [128, SEQ], F32)
            CR, SR = Reg("a_C"), Reg("a_S")
            pr = slice(64, 96)
            P.dma("sp", Ct[pr, :], rope["C_d"], r=[Reg("ropetab_d")], w=[CR])
            P.dma("sp", St[pr, :], rope["S_d"], r=[Reg("ropetab_d")], w=[SR])
            Qs = [sb2("a_Qs%d" % i, [128, SEQ], BF16) for i in range(2)]
            Ks = [sb2("a_Ks%d" % i, [128, SEQ], BF16) for i in range(2)]
            QsR = [Reg("a_Qs%d" % i) for i in range(2)]
            KsR = [Reg("a_Ks%d" % i) for i in range(2)]
            rt1 = sb2("a_rt1", [128, 2048], F32)
            rt2 = sb2("a_rt2", [128, 2048], F32)
            rt1R, rt2R = Reg("a_rt1"), Reg("a_rt2")

        LOOK = 3
        side = side_factory(sb2) if side_factory else None

        def head_prologue(h):
            hb = h % 2
            K, Q, V = Kb[hb], Qb[hb], Vb[hb]
            ld["K"](h, K, KR[hb])
            ld["Q"](h, Q, QR[hb])
            ld["V"](h, V, VR[hb])
            if rope is not None:
                pr = slice(64, 96)
                ld["Qs"](h, Qs[hb], QsR[hb])
                ld["Ks"](h, Ks[hb], KsR[hb])
                for (X, XR, Xs, XsR) in ((Q, QR[hb], Qs[hb], QsR[hb]), (K, KR[hb], Ks[hb], KsR[hb])):
                    for hf in range(2):
                        cs_ = slice(hf * 2048, (hf + 1) * 2048)
                        P.op("dve", (lambda e, X=X, cs_=cs_: e.tensor_tensor(rt1[pr, :], X[pr, cs_], Ct[pr, cs_], ALU.mult)),
                             r=[XR, CR], w=[rt1R])
                        P.op("pool", (lambda e, Xs=Xs, cs_=cs_: e.tensor_tensor(rt2[pr, :], Xs[pr, cs_], St[pr, cs_], ALU.mult)),
                             r=[XsR, SR], w=[rt2R])
                        P.op("dve", (lambda e, X=X, cs_=cs_: e.tensor_tensor(X[pr, cs_], rt1[pr, :], rt2[pr, :], ALU.add)),
                             r=[rt1R, rt2R, XR], pw=[XR])
            if moba is not None:
                P.op("dve", (lambda e, K=K, hb=hb: e.tensor_reduce(
                    ks32[hb][:], K[0:64, :].rearrange("p (n t) -> p n t", t=256), AX.X, ALU.add)), r=[KR[hb]], w=[ks32R[hb]])
                P.op("act", (lambda e, hb=hb: e.activation(ksb[hb][:], ks32[hb][:], AF.Copy)), r=[ks32R[hb]], w=[ksbR[hb]])

        def gate_prologue(h, j):
            hb = h % 2
            Q = Qb[hb]
            qs0 = j * 512
            pg, pgR = E.ps[6], E.psR[6]
            for ip in range(4):
                P.op("pe", (lambda e, ip=ip, Q=Q, qs0=qs0, hb=hb: e.matmul(
                    pg[:, ip * 16:(ip + 1) * 16], Q[0:64, qs0 + ip * 128:qs0 + (ip + 1) * 128], ksb[hb][:],
                    start=True, stop=True)), r=[QR[hb], ksbR[hb]], pw=[pgR])
            cs = slice(j * 64, (j + 1) * 64)
            P.op("dve", (lambda e, cs=cs: e.tensor_tensor(gm[:], pg[:, 0:64], cst[:, 0, cs], ALU.add)),
                 r=[pgR, cstR], w=[gmR])
            for ip in range(4):
                P.op("dve", (lambda e, ip=ip: e.max(mx8[:, ip, :], gm[:, ip * 16:(ip + 1) * 16])), r=[gmR], pw=[mxR])
            for ip in range(4):
                P.op("dve", (lambda e, ip=ip: e.tensor_scalar(
                    sel[:, ip * 16:(ip + 1) * 16], gm[:, ip * 16:(ip + 1) * 16], mx8[:, ip, 2:3], None, ALU.is_ge)),
                    r=[gmR, mxR], pw=[selR])
            P.op("dve", (lambda e, cs=cs: e.tensor_tensor(sel[:], sel[:], cst[:, 1, cs], ALU.mult)), r=[selR, cstR], w=[selR])
            P.op("dve", (lambda e, cs=cs: e.tensor_tensor(sel[:], sel[:], cst[:, 2, cs], ALU.add)), r=[selR, cstR], w=[selR])
            P.op("dve", lambda e: e.tensor_scalar(
                Z[:, :, 64:80], sel[:].rearrange("p (i n) -> p i n", n=16), -1.0, BIGM, ALU.add, ALU.mult),
                r=[selR], w=[ZR])
            pz, pzR = E.ps[7], E.psR[7]
            for ip in range(4):
                P.op("pe", (lambda e, ip=ip: e.matmul(pz[0:80, ip * 128:(ip + 1) * 128], Z[:, ip, :], ident[:],
                                                     start=True, stop=True)), r=[ZR, identR], pw=[pzR])
            P.op("act", (lambda e, Q=Q, qs0=qs0: e.activation(Q[64:80, qs0:qs0 + 512], pz[64:80, :], AF.Copy)),
                 r=[pzR], w=[QmR[hb][j]])

        units = [(h, j, g) for h in range(8) for j in range(NQT) for g in range(4 * j + 4)]
        first_of_head = {}
        for idx, (h, j, g) in enumerate(units):
            if (j, g) == (0, 0):
                first_of_head[h] = idx
        NU = len(units)

        def front(idx):
            h, j, g = units[idx]
            hb = h % 2
            K, Q = Kb[hb], Qb[hb]
            qs0 = j * 512
            imin = max(0, g - 4 * j)
            c0 = imin * 128
            pb = idx % 4
            pS, pSR = E.ps[pb], E.psR[pb]
            rds = [KR[hb], QR[hb]] + ([QmR[hb][j]] if moba is not None else [])
            kk = 80 if moba is not None else dk
            P.op("pe", (lambda e: e.matmul(
                pS[:, c0:512], K[0:kk, g * 128:(g + 1) * 128], Q[0:kk, qs0 + c0:qs0 + 512],
                start=True, stop=True)), r=rds, pw=[pSR])
            P.op("act", (lambda e: e.activation(Pb[pb][:, c0:512], pS[:, c0:512], AF.Exp, scale=scale)),
                 r=[pSR], w=[PR[pb]])
            for ip in range(imin, 4):
                G = 4 * j + ip
                tab = None
                if g == G:
                    tab = (Dt[:, h, :], DR) if moba is not None else (tri[:], triR)
                elif g == G - 1 and moba is not None:
                    tab = (Et[:, h, :], ER)
                if tab is not None:
                    P.op("dve", (lambda e, ip=ip, tab=tab: e.tensor_tensor(
                        Pb[pb][:, ip * 128:(ip + 1) * 128], Pb[pb][:, ip * 128:(ip + 1) * 128], tab[0], ALU.mult)),
                        r=[tab[1], PR[pb]], pw=[PR[pb]])

        def back(idx):
            h, j, g = units[idx]
            hb = h % 2
            V = Vb[hb]
            qs0 = j * 512
            imin = max(0, g - 4 * j)
            pb = idx % 4
            ob = (h * NQT + j) % 2
            po, poR = E.ps[4 + ob], E.psR[4 + ob]
            for ip in range(imin, 4):
                G = 4 * j + ip
                P.op("pe", (lambda e, ip=ip, G=G: e.matmul(
                    po[:, ip * 65:(ip + 1) * 65], Pb[pb][:, ip * 128:(ip + 1) * 128], V[:, g, :],
                    start=(g == 0 and ip == 0), stop=(g == G), skip_group_check=True)), r=[PR[pb], VR[hb]], pw=[poR])
            if g == 4 * j + 3:
                P.op("dve", (lambda e: e.reciprocal(
                    rec[ob][:], po[:, 0:260].rearrange("p (i c) -> p i c", c=65)[:, :, 64])), r=[poR], w=[recR[ob]])
                for ip in range(4):
                    P.op("dve", (lambda e, ip=ip: e.tensor_scalar(
                        osb[ob][:, ip, :], po[:, ip * 65:ip * 65 + 64], rec[ob][:, ip:ip + 1], None, ALU.mult)),
                        r=[poR, recR[ob]], pw=[osR[ob]])
                P.dma("sp", o_d[qs0:qs0 + 512, h * 64:(h + 1) * 64].rearrange("(i p) d -> p i d", p=128), osb[ob][:],
                      r=[osR[ob]], pw=[oR])

        if pre:
            pre()
        head_prologue(0)
        if moba is not None:
            gate_prologue(0, 0)
        side_jobs = list(side) if side else []
        for idx in range(NU + LOOK):
            if idx < NU:
                h, j, g = units[idx]
                if g == 0 and moba is not None and j + 1 < NQT:
                    gate_prologue(h, j + 1)
                if side_jobs and g == 0 and j >= 2:
                    side_jobs.pop(0)()
                if h + 1 < 8 and idx == first_of_head[h] + LOOK + 1:
                    head_prologue(h + 1)
                    if moba is not None:
                        gate_prologue(h + 1, 0)
                front(idx)
            if idx - LOOK >= 0:
                back(idx - LOOK)
        while side_jobs:
            side_jobs.pop(0)()


def rope_table_jobs(E, pos_d, inv_d, C_d, S_d):
    P = E.P
    TWO_PI = float(2 * np.pi)

    def factory(sb2):
        CW = 1024
        posi = sb2("r_posi", [128, CW], I32)
        rr = sb2("r_rr", [128, CW], F32)
        ni = sb2("r_ni", [128, CW], I32)
        nf = sb2("r_nf", [128, CW], F32)
        dst = sb2("r_dst", [128, CW], F32)
        inv = sb2("r_inv", [128, 2], F32)
        posR, rrR, niR, nfR, dstR, invR = Reg("r_posi"), Reg("r_rr"), Reg("r_ni"), Reg("r_nf"), Reg("r_dst"), Reg("r_inv")
        tabR = Reg("ropetab_d")
        pr = slice(64, 96)
        P.dma("sp", inv[:], inv_d, w=[invR])
        jobs = []

        def job(ck, which):
            cs = slice(ck * CW, (ck + 1) * CW)
            shift = 0.0 if which == "S" else 0.25
            if which == "S":
                P.dma("sp", posi[pr, :], pos_d[:, cs], w=[posR])
                P.op("dve", lambda e: e.tensor_copy(rr[pr, :], posi[pr, :]), r=[posR], w=[rrR])
                P.op("dve", lambda e: e.tensor_scalar(rr[pr, :], rr[pr, :], inv[pr, 0:1], 1.0 / TWO_PI, ALU.mult, ALU.mult),
                     r=[rrR, invR], w=[rrR])
            P.op("dve", lambda e: e.tensor_scalar(nf[pr, :], rr[pr, :], shift, None, ALU.add), r=[rrR], w=[nfR])
            P.op("dve", lambda e: e.tensor_copy(ni[pr, :], nf[pr, :]), r=[nfR], w=[niR])
            P.op("dve", lambda e: e.tensor_copy(dst[pr, :], ni[pr, :]), r=[niR], w=[dstR])
            P.op("dve", lambda e: e.tensor_tensor(nf[pr, :], nf[pr, :], dst[pr, :], ALU.subtract), r=[nfR, dstR], w=[nfR])
            P.op("dve", lambda e: e.tensor_scalar(dst[pr, :], nf[pr, :], 0.5, None, ALU.is_gt), r=[nfR], w=[dstR])
            P.op("dve", lambda e: e.tensor_tensor(nf[pr, :], nf[pr, :], dst[pr, :], ALU.subtract), r=[nfR, dstR], w=[nfR])
            P.op("dve", lambda e: e.tensor_scalar(dst[pr, :], nf[pr, :], -0.5, None, ALU.is_lt), r=[nfR], w=[dstR])
            P.op("dve", lambda e: e.tensor_tensor(nf[pr, :], nf[pr, :], dst[pr, :], ALU.add), r=[nfR, dstR], w=[nfR])
            P.op("act", lambda e: e.activation(dst[pr, :], nf[pr, :], AF.Sin, scale=TWO_PI), r=[nfR], w=[dstR])
            if which == "S":
                P.op("dve", lambda e: e.tensor_scalar(dst[pr, :], dst[pr, :], inv[pr, 1:2], None, ALU.mult), r=[dstR, invR], w=[dstR])
            P.dma("sp", (S_d if which == "S" else C_d)[:, cs], dst[pr, :], r=[dstR], pw=[tabR])

        for ck in range(SEQ // CW):
            for which in ("S", "C"):
                jobs.append(lambda ck=ck, which=which: job(ck, which))
        return jobs
    return factory


import math
import ml_dtypes
from concourse.bass_utils import run_bass_kernel_spmd

BF = ml_dtypes.bfloat16


def _pl(v):
    return np.ascontiguousarray(np.asarray(v).reshape(-1, 128).T)


def _dt(nc, n, s, d=F32, k="ExternalInput"):
    return nc.dram_tensor(n, list(s), d, kind=k).ap()


def _finish(P, E):
    P.emit()


def _common_inputs(nc):
    a = {}
    a["wg"] = _dt(nc, "wg", [2, 2, 1024, 2816])
    a["wu"] = _dt(nc, "wu", [2, 2, 1024, 2816])
    a["wd"] = _dt(nc, "wd", [2, 2, 2816, 1024])
    return a


def _t5_bucket_np(n):
    n = np.maximum(np.asarray(n, np.int32), 0)
    nf = np.maximum(n, 1).astype(np.float32)
    large = 16 + (np.log(nf / np.float32(16)) / np.float32(math.log(128 / 16)) * np.float32(16)).astype(np.int32)
    large = np.minimum(large, 31)
    return np.where(n < 16, n, large)


PAIRS = [[0, 1], [2, 3], [4, 5], [6, 7]]


def build_fused():
    nc = bass.Bass("TRN2", target_bir_lowering=False)
    a = _common_inputs(nc)
    xT_d = _dt(nc, "xT", [1024, 2048]); cT = _dt(nc, "cT", [128, 8])
    ada_w = _dt(nc, "ada_w", [2, 1024, 9216]); ada_bT = _dt(nc, "ada_bT", [2, 128, 72]); norm_gT = _dt(nc, "norm_gT", [2, 128, 24])
    kv_ada_w = _dt(nc, "kv_ada_w", [1024, 2048]); kv_ada_bT = _dt(nc, "kv_ada_bT", [128, 16]); kv_norm_gT = _dt(nc, "kv_norm_gT", [128, 8])
    w_qkv = _dt(nc, "w_qkv", [1024, 3072]); mgq = _dt(nc, "mgq", [128, 1]); mgk = _dt(nc, "mgk", [128, 1])
    w_o1 = _dt(nc, "w_o1", [1024, 1024]); w_o2 = _dt(nc, "w_o2", [1024, 1024])
    w_dq = _dt(nc, "w_dq", [1024, 384]); gqa = _dt(nc, "gqa", [128, 3]); w_uq = _dt(nc, "w_uq", [384, 1536]); gq = _dt(nc, "gq", [96, 1])
    w_dkv = _dt(nc, "w_dkv", [1024, 288]); gkva = _dt(nc, "gkva", [128, 2]); wkf = _dt(nc, "wkf", [288, 1536]); gk = _dt(nc, "gk", [96, 1])
    w_uv = _dt(nc, "w_uv", [256, 1024])
    tri = _dt(nc, "tri", [128, 128], BF16); ident = _dt(nc, "ident", [128, 128], BF16)
    mo = dict(onehot_d=_dt(nc, "onehot", [16, 4096], BF16), eligadd_d=_dt(nc, "eligadd", [128, 512]),
              elig01_d=_dt(nc, "elig01", [128, 512]), own01_d=_dt(nc, "own01", [128, 512]),
              rawD_d=_dt(nc, "rawD", [8, 128, 128]), rawE_d=_dt(nc, "rawE", [8, 128, 128]),
              b31_d=_dt(nc, "b31", [128, 8]), ident_d=ident)
    pos_d = _dt(nc, "pos", [32, 4096], I32)
    inv_d = _dt(nc, "inv", [128, 2])
    C_d = nc.dram_tensor("i_ropeC", [32, 4096], F32).ap()
    S_d = nc.dram_tensor("i_ropeS", [32, 4096], F32).ap()
    ro = dict(C_d=C_d, S_d=S_d)
    out = _dt(nc, "outT", [1024, 2048], F32, "ExternalOutput")
    it = lambda n, s: nc.dram_tensor(n, list(s), BF16)
    q1, k1, v1 = it("i_q1", [1024, 2048]), it("i_k1", [1024, 2048]), it("i_v1", [2048, 1024])
    q1g, k1g, v1g = it("i_q1g", [2048, 2048]), it("i_k1g", [2048, 2048]), it("i_v1g", [4096, 1024])
    o1, o1g = it("i_o1", [4096, 512]), it("i_o1g", [8192, 512])
    q2, k2, v2 = it("i_q2", [1536, 2048]), it("i_k2", [1536, 2048]), it("i_v2", [2048, 1024])
    q2g, k2g, v2g = it("i_q2g", [3072, 2048]), it("i_k2g", [3072, 2048]), it("i_v2g", [4096, 1024])
    o2, o2g = it("i_o2", [4096, 512]), it("i_o2g", [8192, 512])

    with ExitStack() as es:
        P = Prog(nc, es)
        E = alloc_common(nc, es, P)
        setup_eps(E)
        parc = {}

        def par(e):
            k = id(e)
            if k not in parc:
                parc[k] = e.snap(e.partition_id() % 2)
            return parc[k]

        def gather(src, dst, names, nch):
            srcR, dstR = Reg(names[0]), Reg(names[1])
            rows = src.shape[0] // nch
            for j in range(nch):
                P.cc("AllGather", PAIRS, src.ap()[j * rows:(j + 1) * rows, :], dst.ap()[j * 2 * rows:(j + 1) * 2 * rows, :],
                     r=[srcR], pw=[dstR])
            return dstR

        def make_exchange(q, k, v, qg, kg, vg, dk, tag, nq, names):
            jj = 1 if dk == 64 else 2
            qs_ = nc.dram_tensor("i_qs" + tag, [2, 8 * dk, 2048], BF16)
            ks_ = nc.dram_tensor("i_ks" + tag, [2, 8 * dk, 2048], BF16)
            vs_ = nc.dram_tensor("i_vs" + tag, [4096, 512], BF16)
            qsR, ksR, vsR = Reg("qsel"), Reg("ksel"), Reg("vsel")

            gRs = {}
            hpc = 16 // nq

            def chunk_gather(src, g_, nm, nch, j):
                rows = src.shape[0] // nch
                P.cc("AllGather", PAIRS, src.ap()[j * rows:(j + 1) * rows, :], g_.ap()[j * 2 * rows:(j + 1) * 2 * rows, :],
                     r=[Reg(nm[0])], pw=[Reg(nm[1])])

            def hqk(src, g_, nm, key):
                def f(h):
                    if (h + 1) % hpc == 0:
                        chunk_gather(src, g_, nm, nq, h // hpc)
                        gRs[key] = Reg(nm[1])
                return f

            def hv(t16):
                if (t16 + 1) % 8 == 0:
                    chunk_gather(v, vg, names[2], 2, t16 // 8)
                    gRs["v"] = Reg(names[2][1])

            def finish():
                for (g_, s_, sR, key) in ((qg, qs_, qsR, "q"), (kg, ks_, ksR, "k")):
                    view = g_.ap().rearrange("(g j t r) n -> g j t r n", g=2, j=jj, t=2)
                    for j in range(jj):
                        P.dma("sp", s_.ap().rearrange("t (j r) n -> j t r n", j=jj)[j],
                              (lambda e, view=view, j=j: view[bass.ds(par(e), 1), j, :, :, :].rearrange("1 t r n -> t r n")),
                              r=[gRs[key]], pw=[sR])
                vview = vg.ap().rearrange("(j t i) (a c) -> j t i a c", j=2, t=2, a=2)
                for j in range(2):
                    P.dma("sp", vs_.ap().rearrange("(t j i) c -> j t i c", t=2, j=2)[j],
                          (lambda e, j=j: vview[j, :, :, bass.ds(par(e), 1), :].rearrange("t i 1 c -> t i c")), r=[gRs["v"]], pw=[vsR])
            hooks = dict(q=hqk(q, qg, names[0], "q"), k=hqk(k, kg, names[1], "k"), v=hv)
            return hooks, (qs_, ks_, vs_, qsR, ksR, vsR), finish

        def attn_loaders(qs_, ks_, vs_, dk, qsR, ksR, vsR, rope):
            qv = qs_.ap().rearrange("t (h d) n -> t h d n", h=8)
            kv = ks_.ap().rearrange("t (h d) n -> t h d n", h=8)
            vv = vs_.ap()

            def mk(view, gR, rows=None, prow=None):
                def f(h, tile, reg):
                    r0, r1 = rows if rows is not None else (0, dk)
                    p0 = prow if prow is not None else r0
                    P.dma("sp", tile[p0:p0 + (r1 - r0), :].rearrange("d (t n) -> d t n", t=2),
                          view[:, h, r0:r1, :].rearrange("t d n -> d t n"), r=[gR(h) if callable(gR) else gR], pw=[reg])
                return f

            def fV(h, tile, reg):
                P.dma("sp", tile[:, :, 0:64], vv[:, h * 64:(h + 1) * 64].rearrange("(g p) d -> p g d", p=128),
                      r=[vsR], pw=[reg])
            ld = dict(K=mk(kv, ksR), Q=mk(qv, qsR), V=fV)
            if rope:
                def sw(view, gR):
                    f1 = mk(view, gR, rows=(80, 96), prow=64)
                    f2 = mk(view, gR, rows=(64, 80), prow=80)
                    return lambda h, tile, reg: (f1(h, tile, reg), f2(h, tile, reg))
                ld["Qs"] = sw(qv, qsR)
                ld["Ks"] = sw(kv, ksR)
            return ld

        def otok_loader(og, ogR, tag):
            os_ = nc.dram_tensor("i_os" + tag, [2, 2048, 512], BF16)
            osR_ = Reg("osel")
            ov = og.ap().rearrange("(j g n) c -> j g n c", j=2, g=2)
            P.dma("sp", os_.ap(), (lambda e: ov[bass.ds(par(e), 1), :, :, :].rearrange("1 g n c -> g n c")), r=[ogR], w=[osR_])

            def f(t16, tile, reg):
                P.dma("sp", tile[:, :].rearrange("n (g c) -> n g c", g=2),
                      os_.ap()[:, t16 * 128:(t16 + 1) * 128, :].rearrange("g n c -> n g c"), r=[osR_], w=[reg])
            return f

        load_xT(E, xT_d)
        phase0_mods(E, cT, ada_w, ada_bT, norm_gT, kv_ada_w, kv_ada_bT, kv_norm_gT)
        ffn_phase(E, a["wg"][0, 0], a["wu"][0, 0], a["wd"][0, 0], 0, 1, 2)
        hooks, sel, fin = make_exchange(q1, k1, v1, q1g, k1g, v1g, 64, "1", 2, (("q_d", "q1g"), ("k_d", "k1g"), ("v_d", "v1g")))
        moba_prep(E, w_qkv, mgq, mgk, q1.ap().rearrange("(h d) n -> h d n", h=16), k1.ap().rearrange("(h d) n -> h d n", h=16),
                  v1.ap(), 3, 4, hooks=hooks)
        attn_phase(E, attn_loaders(sel[0], sel[1], sel[2], 64, sel[3], sel[4], sel[5], False), o1.ap(), 64, 8.0, tri, moba=mo,
                   side=rope_table_jobs(E, pos_d, inv_d, C_d, S_d), pre=fin)
        ogR = gather(o1, o1g, ("a_od", "o1g"), 2)
        wo_phase(E, w_o1, otok_loader(o1g, ogR, "1"), ident, 5)
        ffn_phase(E, a["wg"][0, 1], a["wu"][0, 1], a["wd"][0, 1], 6, 7, 8)
        ffn_phase(E, a["wg"][1, 0], a["wu"][1, 0], a["wd"][1, 0], 9, 10, 11)
        hooks, sel, fin = make_exchange(q2, k2, v2, q2g, k2g, v2g, 96, "2", 4, (("q_d2", "q2g"), ("k_d2", "k2g"), ("v_d2", "v2g")))
        mla_prep(E, w_dq, gqa, w_uq, gq, w_dkv, gkva, wkf, gk, w_uv, q2.ap().rearrange("(h d) n -> h d n", h=16),
                 k2.ap().rearrange("(h d) n -> h d n", h=16), v2.ap(), 12, 13, 18, 19, hooks=hooks)
        attn_phase(E, attn_loaders(sel[0], sel[1], sel[2], 96, sel[3], sel[4], sel[5], True), o2.ap(), 96, float(math.sqrt(96)), tri, rope=ro, pre=fin)
        ogR = gather(o2, o2g, ("a_od", "o2g"), 2)
        wo_phase(E, w_o2, otok_loader(o2g, ogR, "2"), ident, 14)
        ffn_phase(E, a["wg"][1, 1], a["wu"][1, 1], a["wd"][1, 1], 15, 16, 17)
        store_xT(E, out, Reg("outo"))
        P.emit()
    return nc


def kernel(**inp):
    inp = {k: np.asarray(v) for k, v in inp.items()}
    x = inp["x"]
    ada_bT = np.stack([_pl(inp["ada_b"][L]) for L in range(2)])
    norm_gT = np.stack([_pl(inp["norm_g"][L].reshape(-1)) for L in range(2)])
    kk = np.arange(128)[:, None]
    qq = np.arange(128)[None, :]
    tri = (kk <= qq).astype(np.float32).astype(BF)
    bD = _t5_bucket_np(qq - kk)
    bE = _t5_bucket_np(qq - kk + 128)
    rb = inp["rel_bias"]
    onehot = (np.arange(4096)[None, :] // 256 == np.arange(16)[:, None]).astype(np.float32).astype(BF)
    eligadd = np.zeros((8, 4, 16), np.float32); elig01 = np.zeros((8, 4, 16), np.float32); own01 = np.zeros((8, 4, 16), np.float32)
    for j in range(8):
        for ip in range(4):
            qb = (4 * j + ip) // 2
            n = np.arange(16)
            eligadd[j, ip] = np.where(n < qb, 0.0, -1e30)
            elig01[j, ip] = (n < qb)
            own01[j, ip] = (n == qb)
    rep = lambda t: np.ascontiguousarray(np.broadcast_to(t.reshape(1, -1), (128, 512))).astype(np.float32)
    ident = np.eye(128, dtype=np.float32).astype(BF)
    wkf = np.zeros((288, 16, 96), np.float32)
    wkf[0:256, :, 0:64] = inp["w_uk"].reshape(256, 16, 64)
    wkf[256:288, :, 64:96] = np.eye(32, dtype=np.float32)[:, None, :]
    wkf = wkf.reshape(288, 1536)
    invf = (np.float32(10000.0) ** (-np.arange(16, dtype=np.float32) / np.float32(16))).astype(np.float32)
    inv = np.zeros((128, 2), np.float32)
    inv[64:96, 0] = np.tile(invf, 2)
    inv[64:80, 1] = -1.0
    inv[80:96, 1] = 1.0
    shared = dict(wg=inp["ffn_w_gate"], wu=inp["ffn_w_up"], wd=inp["ffn_w_down"], ada_w=inp["ada_w"], ada_bT=ada_bT,
                  norm_gT=norm_gT, kv_ada_w=inp["kv_ada_w"], kv_ada_bT=_pl(inp["kv_ada_b"]), kv_norm_gT=_pl(inp["kv_norm_g"]),
                  w_qkv=inp["moba_w_qkv"][0],
                  mgq=np.tile(inp["moba_q_g"][0], 2).reshape(128, 1).astype(np.float32),
                  mgk=np.tile(inp["moba_k_g"][0], 2).reshape(128, 1).astype(np.float32),
                  w_o1=inp["moba_w_o"][0], w_o2=inp["mla_w_o"][0],
                  w_dq=inp["mla_w_dq"][0], gqa=_pl(inp["mla_q_a_norm_g"][0]), w_uq=inp["mla_w_uq"][0],
                  gq=inp["mla_q_g"][0].reshape(96, 1).astype(np.float32), w_dkv=inp["w_dkv"], gkva=_pl(inp["kv_a_norm_g"]),
                  wkf=wkf, gk=inp["mla_k_g"].reshape(96, 1).astype(np.float32), w_uv=inp["w_uv"],
                  tri=tri, ident=ident, onehot=onehot, eligadd=rep(eligadd), elig01=rep(elig01), own01=rep(own01), inv=inv)
    maps = []
    for b in range(4):
        for c in range(2):
            hs = slice(c * 8, (c + 1) * 8)
            m = dict(shared)
            m.update(xT=np.ascontiguousarray(x[b, c * 2048:(c + 1) * 2048].T), cT=_pl(inp["c"][b]),
                     rawD=np.ascontiguousarray(np.transpose(rb[bD][:, :, hs], (2, 0, 1))).astype(np.float32),
                     rawE=np.ascontiguousarray(np.transpose(rb[bE][:, :, hs], (2, 0, 1))).astype(np.float32),
                     b31=np.ascontiguousarray(np.broadcast_to(rb[31, hs].reshape(1, 8), (128, 8))).astype(np.float32),
                     pos=np.ascontiguousarray(np.broadcast_to(inp["positions"][b].reshape(1, 4096), (32, 4096))).astype(np.int32))
            maps.append(m)
    res = run_bass_kernel_spmd(build_fused(), maps, core_ids=list(range(8))).results
    out = np.zeros((4, 4096, 1024), np.float32)
    for i in range(8):
        b, c = divmod(i, 2)
        out[b, c * 2048:(c + 1) * 2048] = np.asarray(res[i]["outT"]).T
    return out
```

```python
from contextlib import ExitStack
import numpy as np
import concourse.bass as bass
import concourse.mybir as mybir

F32 = mybir.dt.float32
BF16 = mybir.dt.bfloat16
I32 = mybir.dt.int32
ALU = mybir.AluOpType
AF = mybir.ActivationFunctionType
AX = mybir.AxisListType

ENGS = ("pe", "act", "dve", "pool", "sp")


class Reg:
    registry = {}

    def __new__(cls, name=""):
        r = cls.registry.get(name)
        if r is None:
            r = object.__new__(cls)
            r.name = name
            r.writers = []
            r.readers = []
            r.sem = None
            r.dcount = 0
            cls.registry[name] = r
        return r


class Op:
    __slots__ = ("eng", "fn", "deps", "needs_inc", "val", "sem", "is_dma", "pos", "default_inc", "dbg")

    def __init__(self, eng, fn):
        self.eng = eng
        self.fn = fn
        self.deps = {}
        self.needs_inc = False
        self.val = None
        self.sem = None
        self.is_dma = False
        self.pos = 0
        self.default_inc = False


class Prog:
    def __init__(self, nc, es):
        Reg.registry.clear()
        self.nc = nc
        self.es = es
        self.ops = {e: [] for e in ENGS}
        self.n = 0
        self.esem = {}
        self.nsem = 0
        for e in ENGS:
            self.esem[e] = self.newsem("e_" + e)

    def newsem(self, name):
        self.nsem += 1
        return self.es.enter_context(self.nc.semaphore(name + "_%d" % self.nsem))

    def _key(self, op):
        return op.sem if op.is_dma else op.eng

    def _same(self, a, b):
        if a.is_dma != b.is_dma:
            return False
        if a.is_dma:
            return a.sem is b.sem
        return a.eng == b.eng

    def _adddep(self, op, d):
        if d is op:
            return
        if (not d.is_dma) and d.eng == op.eng and not op.is_dma:
            if op.eng == "pe":
                return
        k = id(d.sem) if d.is_dma else d.eng
        cur = op.deps.get(k)
        if cur is None or cur.pos < d.pos:
            op.deps[k] = d

    def _track(self, op, r, w, pw, same_eng_war=False):
        for x in r:
            for d in x.writers:
                self._adddep(op, d)
        for x in w:
            for d in x.writers:
                self._adddep(op, d)
            for d in x.readers:
                if d.eng == op.eng and not d.is_dma and not op.is_dma:
                    continue
                self._adddep(op, d)
        for x in pw:
            if x.readers:
                for d in x.readers:
                    if d.eng == op.eng and not d.is_dma and not op.is_dma:
                        continue
                    self._adddep(op, d)
        for d in op.deps.values():
            d.needs_inc = True
        for x in r:
            x.readers = [q for q in x.readers if not self._same(q, op)] + [op]
        for x in w:
            x.writers = [op]
            x.readers = []
        for x in pw:
            if x.readers:
                x.writers = []
                x.readers = []
            x.writers = [q for q in x.writers if not self._same(q, op)] + [op]

    def op(self, eng, fn, r=(), w=(), pw=()):
        o = Op(eng, fn)
        self.n += 1
        o.pos = self.n
        o.sem = self.esem[eng]
        self._track(o, r, w, pw)
        self.ops[eng].append(o)
        return o

    def wait(self, eng, r=()):
        o = Op(eng, lambda e: None)
        self.n += 1
        o.pos = self.n
        o.sem = self.esem[eng]
        for x in r:
            for d in x.writers:
                self._adddep(o, d)
        for d in o.deps.values():
            d.needs_inc = True
        self.ops[eng].append(o)
        return o

    def dma(self, q, out, in_, r=(), w=(), pw=()):
        dst = (list(w) + list(pw))[0]
        if dst.sem is None:
            dst.sem = self.newsem("d_" + dst.name)
        def _ap(a, e):
            return a(e) if callable(a) else a
        o = Op(q, lambda e: e.dma_start(out=_ap(out, e), in_=_ap(in_, e)))
        o.is_dma = True
        o.dbg = dst.name
        self.n += 1
        o.pos = self.n
        o.sem = dst.sem
        self._track(o, r, w, pw)
        dst.dcount += 16
        o.val = dst.dcount
        o.needs_inc = True
        self.ops[q].append(o)
        return o

    def cc(self, kind, groups, in_ap, out_ap, r=(), w=(), pw=()):
        dst = (list(w) + list(pw))[0]
        sem = self.newsem("cc_" + dst.name)
        o = Op("pool", lambda e: e.collective_compute(kind, ALU.bypass, replica_groups=groups, ins=[in_ap], outs=[out_ap]))
        o.is_dma = True
        o.default_inc = True
        self.n += 1
        o.pos = self.n
        o.sem = sem
        for x in r:
            for d in x.writers:
                self._adddep(o, d)
        self._track(o, (), w, pw)
        o.val = 1
        o.needs_inc = True
        self.ops["pool"].append(o)
        return o

    def final_waits(self):
        out = []
        seen = set()
        for e in ENGS:
            for o in self.ops[e]:
                if o.is_dma:
                    seen.add(id(o.sem))
                    out = [x for x in out if x[0] is not o.sem] + [(o.sem, o.val)]
        return out

    def emit(self):
        nc = self.nc
        for e in ENGS:
            c = 0
            for o in self.ops[e]:
                if o.is_dma:
                    continue
                if o.needs_inc:
                    c += 1
                    o.val = c
        with nc.Block() as block:
            def run(eng_name, eh):
                waited = {}
                for o in self.ops[eng_name]:
                    for d in o.deps.values():
                        k = id(d.sem)
                        if waited.get(k, 0) < d.val:
                            eh.wait_ge(d.sem, d.val)
                            waited[k] = d.val
                    try:
                        ins = o.fn(eh)
                    except Exception:
                        print("EMIT FAIL", eng_name, getattr(o, "dbg", None), o.pos)
                        raise
                    if o.needs_inc:
                        if ins is None:
                            raise RuntimeError("op without instruction needs inc")
                        if o.default_inc:
                            ins.then_inc(o.sem)
                        else:
                            ins.then_inc(o.sem, 16 if o.is_dma else 1)

            @block.tensor
            def _(eh):
                run("pe", eh)

            @block.scalar
            def _(eh):
                run("act", eh)

            @block.vector
            def _(eh):
                run("dve", eh)

            @block.gpsimd
            def _(eh):
                run("pool", eh)

            @block.sync
            def _(eh):
                run("sp", eh)
                for sem, cnt in self.final_waits():
                    eh.wait_ge(sem, cnt)

NTOK = 2048
NT = 4
D = 1024
DC = 8
FF = 2816
EPS = 1e-6
FGROUPS = [(0, 512), (512, 512), (1024, 512), (1536, 512), (2048, 512), (2560, 256)]
NV = 20


import itertools
_UID = itertools.count()


class Env:
    pass


def alloc_common(nc, es, P):
    E = Env()
    E.nc, E.es, E.P = nc, es, P
    sb = lambda n, s, d: es.enter_context(nc.sbuf_tensor("s%d_" % next(_UID) + n, s, d))
    E.sb = sb
    E.ps = [es.enter_context(nc.psum_tensor("ps%d" % i, [128, 512], F32)) for i in range(8)]
    E.psR = [Reg("ps%d" % i) for i in range(8)]
    E.xT = sb("xT", [128, DC, NTOK], F32)
    E.xR = [Reg("xT%d" % t) for t in range(NT)]
    E.dv = sb("dv", [128, 256], F32)
    E.dvR = Reg("dv")
    E.ones = sb("ones", [128, 128], BF16)
    E.onesR = Reg("ones")
    E.one1 = sb("one1", [128, 128], BF16)
    E.one1R = Reg("one1")
    E.epsc = sb("epsc", [128, 1], F32)
    E.epsR = Reg("epsc")
    P.op("pool", lambda e: e.memset(E.ones[:], 1.0 / 1024), w=[E.onesR])
    P.op("pool", lambda e: e.memset(E.one1[:], 1.0), w=[E.one1R])
    P.op("pool", lambda e: e.memset(E.epsc[:], EPS), w=[E.epsR])
    P.op("pool", lambda e: e.memset(E.dv[:, 248:256], 0.0), pw=[E.dvR])
    return E


def dvcol(i):
    return slice(8 * i, 8 * i + 8)


def phase0_mods(E, cT, ada_w, ada_bT, norm_gT, kv_ada_w, kv_ada_bT, kv_norm_gT):
    nc, P, sb = E.nc, E.P, E.sb
    with ExitStack() as es2:
        sb2 = lambda n, s, d: es2.enter_context(nc.sbuf_tensor("s%d_" % next(_UID) + n, s, d))
        cs = sb2("p0_c", [128, 8], F32)
        cact = sb2("p0_cact", [128, 8], BF16)
        bT = sb2("p0_b", [128, 160], F32)
        gT = sb2("p0_g", [128, 56], F32)
        raw = sb2("p0_raw", [128, 160], F32)
        wp = [sb2("p0_w%d" % i, [128, 8, 1024], BF16) for i in range(2)]
        wR = [Reg("p0w%d" % i) for i in range(2)]
        cR, caR, bR, gR, rawR = Reg("p0c"), Reg("p0ca"), Reg("p0b"), Reg("p0g"), Reg("p0raw")
        P.dma("sp", cs[:], cT, w=[cR])
        P.dma("sp", bT[:, 0:72], ada_bT[0], pw=[bR])
        P.dma("sp", bT[:, 72:144], ada_bT[1], pw=[bR])
        P.dma("sp", bT[:, 144:160], kv_ada_bT, pw=[bR])
        P.dma("sp", gT[:, 0:24], norm_gT[0], pw=[gR])
        P.dma("sp", gT[:, 24:48], norm_gT[1], pw=[gR])
        P.dma("sp", gT[:, 48:56], kv_norm_gT, pw=[gR])
        P.op("act", lambda e: e.activation(cact[:], cs[:], AF.Silu), r=[cR], w=[caR])
        pieces = []
        for L in range(2):
            for v in range(9):
                pieces.append((ada_w[L].rearrange("(k p) f -> p k f", p=128)[:, :, v * 1024:(v + 1) * 1024], 9 * L + v))
        for v in range(2):
            pieces.append((kv_ada_w.rearrange("(k p) f -> p k f", p=128)[:, :, v * 1024:(v + 1) * 1024], 18 + v))
        psb = E.ps[7]
        for n, (src, vi) in enumerate(pieces):
            b = n % 2
            P.dma("pool", wp[b][:], src, w=[wR[b]])
            pst = E.ps[6 + (n % 2)]
            psr = E.psR[6 + (n % 2)]
            for kc in range(8):
                for ic in range(8):
                    P.op("pe", (lambda e, b=b, kc=kc, ic=ic, pst=pst: e.matmul(
                        pst[:, kc:kc + 1], wp[b][:, ic, kc * 128:(kc + 1) * 128], cact[:, ic:ic + 1],
                        start=(ic == 0), stop=(ic == 7))), r=[wR[b], caR], pw=[psr])
            P.op("dve", (lambda e, vi=vi, pst=pst: e.tensor_tensor(
                raw[:, dvcol(vi)], pst[:, 0:8], bT[:, dvcol(vi)], ALU.add)), r=[psr, bR], pw=[rawR])
        dv = E.dv
        for L in range(2):
            for j in range(3):
                base = 9 * L + 3 * j
                sh, sc, gg = base, base + 1, base + 2
                gcol = slice(24 * L + 8 * j, 24 * L + 8 * j + 8)
                P.op("dve", (lambda e, sc=sc, gcol=gcol, base=base: e.scalar_tensor_tensor(
                    dv[:, dvcol(base)], raw[:, dvcol(sc)], 1.0, gT[:, gcol], ALU.add, ALU.mult)),
                    r=[rawR, gR], pw=[E.dvR])
                P.op("dve", (lambda e, sh=sh, base=base: e.tensor_copy(dv[:, dvcol(base + 1)], raw[:, dvcol(sh)])),
                     r=[rawR], pw=[E.dvR])
                mul = 1.0 if j == 1 else 0.5
                P.op("dve", (lambda e, gg=gg, base=base, mul=mul: e.tensor_scalar(
                    dv[:, dvcol(base + 2)], raw[:, dvcol(gg)], mul, None, ALU.mult)),
                    r=[rawR], pw=[E.dvR])
        P.op("dve", lambda e: e.scalar_tensor_tensor(dv[:, dvcol(18)], raw[:, dvcol(19)], 1.0, gT[:, 48:56], ALU.add, ALU.mult),
             r=[rawR, gR], pw=[E.dvR])
        P.op("dve", lambda e: e.tensor_copy(dv[:, dvcol(19)], raw[:, dvcol(18)]), r=[rawR], pw=[E.dvR])
        fence(E, [[rawR, gR, bR, caR, cR, wR[0], wR[1]]])


def norm_mod(E, Acol, Bcol, hT, hR, tmpb, tmpR, sqb, sqR, rsb, rsR, psi=6):
    P = E.P
    dv = E.dv
    pst, psr = E.ps[psi], E.psR[psi]
    for tt in range(NT):
        ts = slice(tt * 512, (tt + 1) * 512)
        for k in range(DC):
            b = k % 2
            P.op("act", (lambda e, k=k, b=b, ts=ts: e.activation(sqb[b][:], E.xT[:, k, ts], AF.Square)),
                 r=[E.xR[tt]], w=[sqR[b]])
            P.op("pe", (lambda e, k=k, b=b: e.matmul(pst[:], E.ones[:], sqb[b][:], start=(k == 0), stop=(k == DC - 1))),
                 r=[sqR[b], E.onesR], pw=[psr])
        P.op("act", lambda e: e.activation(rsb[:], pst[:], AF.Sqrt, bias=E.epsc[:, 0:1], scale=1.0),
             r=[psr, E.epsR], w=[rsR])
        P.op("dve", lambda e: e.reciprocal(rsb[:], rsb[:]), r=[rsR], w=[rsR])
        for k in range(DC):
            b = k % 2
            P.op("dve", (lambda e, k=k, b=b, ts=ts: e.tensor_tensor(tmpb[b][:], E.xT[:, k, ts], rsb[:], ALU.mult)),
                 r=[E.xR[tt], rsR], w=[tmpR[b]])
            P.op("act", (lambda e, k=k, b=b, ts=ts: e.activation(
                hT[:, k, ts], tmpb[b][:], AF.Identity, bias=dv[:, 8 * Bcol + k:8 * Bcol + k + 1],
                scale=dv[:, 8 * Acol + k:8 * Acol + k + 1])), r=[tmpR[b], E.dvR], pw=[hR[tt]])


def ffn_phase(E, wg, wu, wd, Acol, Bcol, Gcol):
    nc, P = E.nc, E.P
    dv = E.dv
    with ExitStack() as es2:
        sb2 = lambda n, s, d: es2.enter_context(nc.sbuf_tensor("s%d_" % next(_UID) + n, s, d))
        hT = sb2("f_hT", [128, DC, NTOK], BF16)
        hR = [Reg("f_hT%d" % t) for t in range(NT)]
        wgb = [sb2("f_wg%d" % i, [128, DC, 512], BF16) for i in range(2)]
        wub = [sb2("f_wu%d" % i, [128, DC, 512], BF16) for i in range(2)]
        wdb = [sb2("f_wd%d" % i, [128, 4, D], BF16) for i in range(2)]
        wgR = [Reg("f_wg%d" % i) for i in range(2)]
        wuR = [Reg("f_wu%d" % i) for i in range(2)]
        wdR = [Reg("f_wd%d" % i) for i in range(2)]
        act = [sb2("f_act%d" % i, [128, 4, NTOK], BF16) for i in range(2)]
        actR = [[Reg("f_act%d_%d" % (i, t)) for t in range(NT)] for i in range(2)]
        tmpb = [sb2("f_tmp%d" % i, [128, 512], F32) for i in range(2)]
        tmpR = [Reg("f_tmp%d" % i) for i in range(2)]
        sqb = [sb2("f_sq%d" % i, [128, 512], BF16) for i in range(2)]
        sqR = [Reg("f_sq%d" % i) for i in range(2)]
        rsb = sb2("f_rs", [128, 512], F32)
        rsR = Reg("f_rs")

        def load_group(gi):
            f0, fw = FGROUPS[gi]
            b = gi % 2
            nch = fw // 128
            P.dma("pool", wgb[b][:, :, 0:fw], wg.rearrange("(k p) f -> p k f", p=128)[:, :, f0:f0 + fw], w=[wgR[b]])
            P.dma("pool", wub[b][:, :, 0:fw], wu.rearrange("(k p) f -> p k f", p=128)[:, :, f0:f0 + fw], w=[wuR[b]])
            P.dma("pool", wdb[b][:, 0:nch, :], wd[f0:f0 + fw, :].rearrange("(k p) d -> p k d", p=128), w=[wdR[b]])

        P.wait("dve", r=E.xR)
        load_group(0)
        norm_mod(E, Acol, Bcol, hT, hR, tmpb, tmpR, sqb, sqR, rsb, rsR)
        cnt = 0
        for gi, (f0, fw) in enumerate(FGROUPS):
            b = gi % 2
            nch = fw // 128
            if gi + 1 < len(FGROUPS):
                load_group(gi + 1)
            for tt in range(NT):
                ts = slice(tt * 512, (tt + 1) * 512)
                for fc in range(nch):
                    pb = cnt % 2
                    cnt += 1
                    pg, pgR = E.ps[pb], E.psR[pb]
                    pu, puR = E.ps[2 + pb], E.psR[2 + pb]
                    for k in range(DC):
                        P.op("pe", (lambda e, k=k, fc=fc, b=b, pg=pg, ts=ts: e.matmul(
                            pg[:], wgb[b][:, k, fc * 128:(fc + 1) * 128], hT[:, k, ts],
                            start=(k == 0), stop=(k == DC - 1))), r=[wgR[b], hR[tt]], pw=[pgR])
                    for k in range(DC):
                        P.op("pe", (lambda e, k=k, fc=fc, b=b, pu=pu, ts=ts: e.matmul(
                            pu[:], wub[b][:, k, fc * 128:(fc + 1) * 128], hT[:, k, ts],
                            start=(k == 0), stop=(k == DC - 1))), r=[wuR[b], hR[tt]], pw=[puR])
                    P.op("act", (lambda e, pb=pb, pg=pg: e.activation(tmpb[pb][:], pg[:], AF.Silu)),
                         r=[pgR], w=[tmpR[pb]])
                    P.op("dve", (lambda e, pb=pb, pu=pu, b=b, fc=fc, ts=ts: e.tensor_tensor(
                        act[b][:, fc, ts], tmpb[pb][:], pu[:], ALU.mult)), r=[tmpR[pb], puR], pw=[actR[b][tt]])
            for tt in range(NT):
                ts = slice(tt * 512, (tt + 1) * 512)
                for oc in range(DC):
                    pb = cnt % 2
                    cnt += 1
                    pd, pdR = E.ps[4 + pb], E.psR[4 + pb]
                    for fc in range(nch):
                        P.op("pe", (lambda e, fc=fc, oc=oc, b=b, pd=pd, ts=ts: e.matmul(
                            pd[:], wdb[b][:, fc, oc * 128:(oc + 1) * 128], act[b][:, fc, ts],
                            start=(fc == 0), stop=(fc == nch - 1))), r=[wdR[b], actR[b][tt]], pw=[pdR])
                    P.op("dve", (lambda e, oc=oc, pd=pd, ts=ts: e.scalar_tensor_tensor(
                        E.xT[:, oc, ts], pd[:], dv[:, 8 * Gcol + oc:8 * Gcol + oc + 1], E.xT[:, oc, ts],
                        ALU.mult, ALU.add)), r=[pdR, E.dvR], pw=[E.xR[tt]])
        fence(E, [hR, wgR, wuR, wdR, actR[0], actR[1], tmpR, sqR, [rsR]])


def fence(E, reglists):
    regs = [x for l in reglists for x in l]
    P = E.P
    if not hasattr(E, "fenceR"):
        E.fenceR = Reg("fence")
    o = P.op("dve", lambda e: e.tensor_copy(E.dv[:, 255:256], E.dv[:, 254:255]), r=[E.dvR], w=regs + [E.fenceR])
    for eng in ("pe", "act", "pool", "sp"):
        P.wait(eng, r=[E.fenceR])


def load_xT(E, xT_d):
    P = E.P
    for tt in range(NT):
        ts = slice(tt * 512, (tt + 1) * 512)
        P.dma("sp", E.xT[:, :, ts], xT_d.rearrange("(k p) t -> p k t", p=128)[:, :, ts], w=[E.xR[tt]])


def store_xT(E, out_d, outR):
    P = E.P
    for tt in range(NT):
        ts = slice(tt * 512, (tt + 1) * 512)
        P.dma("sp", out_d.rearrange("(k p) t -> p k t", p=128)[:, :, ts], E.xT[:, :, ts], r=[E.xR[tt]], pw=[outR])


def load_w(E, sb2, name, view, shape, q="pool"):
    t = sb2(name, shape, BF16)
    R = Reg(name)
    E.P.dma(q, t[:], view, w=[R])
    return t, R


def rms_chunks(E, zps_list, gcols, outT, outR, tt, nfeat, tmpb, tmpR, sqb, sqR, rsb, rsR, sspsi=6):
    P = E.P
    ts = slice(tt * 512, (tt + 1) * 512)
    pss, pssR = E.ps[sspsi], E.psR[sspsi]
    n = len(zps_list)
    for ci, (pz, pzR, nr) in enumerate(zps_list):
        b = ci % 2
        P.op("act", (lambda e, pz=pz, nr=nr, b=b: e.activation(sqb[b][0:nr, :], pz[0:nr, :], AF.Square)),
             r=[pzR], w=[sqR[b]])
        P.op("pe", (lambda e, nr=nr, b=b, ci=ci: e.matmul(pss[:], E.one1[0:nr, :], sqb[b][0:nr, :],
                                                          start=(ci == 0), stop=(ci == n - 1))),
             r=[sqR[b], E.one1R], pw=[pssR])
    P.op("act", lambda e: e.activation(rsb[:], pss[:], AF.Sqrt, bias=E.epsn[:, nfeat:nfeat + 1], scale=1.0),
         r=[pssR, E.epsR], w=[rsR])
    P.op("dve", lambda e: e.reciprocal(rsb[:], rsb[:]), r=[rsR], w=[rsR])
    for ci, (pz, pzR, nr) in enumerate(zps_list):
        P.op("dve", (lambda e, pz=pz, nr=nr, ci=ci: e.scalar_tensor_tensor(
            outT[0:nr, ci, ts], pz[0:nr, :], gcols[ci], rsb[0:nr, :], ALU.mult, ALU.mult)),
            r=[pzR, rsR, E.gvR], pw=[outR[tt]])


def head_norm_proj(E, Wsb, wR, kparts, zT, zR, dk, gcol, out_d, outR, stg, stgR, sqb, sqR, rsb, rsR, nfeat_idx, hook=None):
    P = E.P
    cnt = 0
    for h in range(16):
        for tt in range(NT):
            ts = slice(tt * 512, (tt + 1) * 512)
            pb = cnt % 2
            cnt += 1
            pq, pqR = E.ps[pb], E.psR[pb]
            pss, pssR = E.ps[2 + pb], E.psR[2 + pb]
            nk = len(kparts)
            for ci, ksz in enumerate(kparts):
                P.op("pe", (lambda e, ci=ci, ksz=ksz, h=h, pq=pq, ts=ts: e.matmul(
                    pq[0:dk, :], Wsb[0:ksz, ci, h * dk:(h + 1) * dk], zT[0:ksz, ci, ts],
                    start=(ci == 0), stop=(ci == nk - 1))), r=[wR, zR[tt]], pw=[pqR])
            P.op("act", (lambda e, pq=pq, pb=pb: e.activation(sqb[pb][0:dk, :], pq[0:dk, :], AF.Square)),
                 r=[pqR], w=[sqR[pb]])
            P.op("pe", (lambda e, pss=pss, pb=pb: e.matmul(pss[0:dk, :], E.one1[0:dk, 0:dk], sqb[pb][0:dk, :],
                                                          start=True, stop=True)),
                 r=[sqR[pb], E.one1R], pw=[pssR])
            P.op("act", (lambda e, pss=pss, pb=pb: e.activation(
                rsb[pb][0:dk, :], pss[0:dk, :], AF.Sqrt, bias=E.epsn[0:dk, nfeat_idx:nfeat_idx + 1], scale=1.0)),
                r=[pssR, E.epsR], w=[rsR[pb]])
            P.op("dve", (lambda e, pb=pb: e.reciprocal(rsb[pb][0:dk, :], rsb[pb][0:dk, :])), r=[rsR[pb]], w=[rsR[pb]])
            P.op("dve", (lambda e, pq=pq, pb=pb: e.scalar_tensor_tensor(
                stg[pb][0:dk, :], pq[0:dk, :], gcol[0:dk, :], rsb[pb][0:dk, :], ALU.mult, ALU.mult)),
                r=[pqR, rsR[pb], E.gvR], w=[stgR[pb]])
            P.dma("sp", out_d[h, :, ts], stg[pb][0:dk, :], r=[stgR[pb]], pw=[outR(h) if callable(outR) else outR])
        if hook:
            hook(h)


def v_proj(E, Wv, wvR, kparts, zT, zR, v_d, vR, stg, stgR, hook=None):
    P = E.P
    cnt = 0
    nk = len(kparts)
    for t16 in range(16):
        tt = t16 // 4
        tk = slice(t16 * 128, (t16 + 1) * 128)
        sb_ = t16 % 2
        for vc in range(2):
            pb = cnt % 2
            cnt += 1
            pv, pvR = E.ps[4 + pb], E.psR[4 + pb]
            for ci, ksz in enumerate(kparts):
                P.op("pe", (lambda e, ci=ci, ksz=ksz, vc=vc, pv=pv, tk=tk: e.matmul(
                    pv[:], zT[0:ksz, ci, tk], Wv[0:ksz, ci, vc * 512:(vc + 1) * 512],
                    start=(ci == 0), stop=(ci == nk - 1))), r=[wvR, zR[tt]], pw=[pvR])
            P.op("act", (lambda e, pv=pv, vc=vc, sb_=sb_: e.activation(stg[sb_][:, vc * 512:(vc + 1) * 512], pv[:], AF.Copy)),
                 r=[pvR], pw=[stgR[sb_]])
        P.dma("sp", v_d[tk, :], stg[sb_][:], r=[stgR[sb_]], pw=[vR(t16) if callable(vR) else vR])
        if hook:
            hook(t16)


def setup_eps(E):
    E.epsn = E.sb("epsn", [128, 4], F32)
    for i, n in enumerate((64, 96, 256, 384)):
        E.P.op("pool", (lambda e, i=i, n=n: e.memset(E.epsn[:, i:i + 1], n * EPS)), pw=[E.epsR])


def moba_prep(E, w_qkv, gq_d, gk_d, q_d, k_d, v_d, Acol, Bcol, hooks=None):
    nc, P = E.nc, E.P
    with ExitStack() as es2:
        sb2 = lambda n, s, d: es2.enter_context(nc.sbuf_tensor("s%d_" % next(_UID) + n, s, d))
        hT = sb2("m_hT", [128, DC, NTOK], BF16)
        hR = [Reg("m_hT%d" % t) for t in range(NT)]
        tmpb = [sb2("m_tmp%d" % i, [128, 512], F32) for i in range(2)]
        tmpR = [Reg("m_tmp%d" % i) for i in range(2)]
        sqb = [sb2("m_sq%d" % i, [128, 512], BF16) for i in range(2)]
        sqR = [Reg("m_sq%d" % i) for i in range(2)]
        rs1 = sb2("m_rs", [128, 512], F32)
        rs1R = Reg("m_rs")
        rsb = [sb2("m_rsb%d" % i, [128, 512], F32) for i in range(2)]
        rsR = [Reg("m_rsb%d" % i) for i in range(2)]
        stg = [sb2("m_stg%d" % i, [128, 512], BF16) for i in range(2)]
        stgR = [Reg("m_stg%d" % i) for i in range(2)]
        vst = [sb2("m_vst%d" % i, [128, 1024], BF16) for i in range(2)]
        vstR = [Reg("m_vst%d" % i) for i in range(2)]
        gv = sb2("m_gv", [128, 2], F32)
        E.gvR = Reg("m_gv")
        P.dma("sp", gv[:, 0:1], gq_d, pw=[E.gvR])
        P.dma("sp", gv[:, 1:2], gk_d, pw=[E.gvR])
        wv3 = w_qkv.rearrange("(k p) f -> p k f", p=128)
        Wq, wqR = load_w(E, sb2, "m_wq", wv3[:, :, 0:1024], [128, 8, 1024])
        Wk, wkR = load_w(E, sb2, "m_wk", wv3[:, :, 1024:2048], [128, 8, 1024])
        Wv, wvR = load_w(E, sb2, "m_wv", wv3[:, :, 2048:3072], [128, 8, 1024])
        norm_mod(E, Acol, Bcol, hT, hR, tmpb, tmpR, sqb, sqR, rs1, rs1R)
        qR, kR, vR = Reg("q_d"), Reg("k_d"), Reg("v_d")
        kp = [128] * 8
        H = hooks or {}
        head_norm_proj(E, Wk, wkR, kp, hT, hR, 64, gv[:, 1:2], k_d, H.get("kR", kR), stg, stgR, sqb, sqR, rsb, rsR, 0, hook=H.get("k"))
        v_proj(E, Wv, wvR, kp, hT, hR, v_d, H.get("vR", vR), vst, vstR, hook=H.get("v"))
        head_norm_proj(E, Wq, wqR, kp, hT, hR, 64, gv[:, 0:1], q_d, H.get("qR", qR), stg, stgR, sqb, sqR, rsb, rsR, 0, hook=H.get("q"))
        fence(E, [hR, tmpR, sqR, [rs1R], rsR, stgR, vstR, [E.gvR, wqR, wkR, wvR]])
        E.outRs = getattr(E, "outRs", []) + [qR, kR, vR]


def wo_phase(E, w_o, ld_otok, ident_d, Gcol):
    nc, P = E.nc, E.P
    dv = E.dv
    with ExitStack() as es2:
        sb2 = lambda n, s, d: es2.enter_context(nc.sbuf_tensor("s%d_" % next(_UID) + n, s, d))
        oT = sb2("w_oT", [128, DC, NTOK], BF16)
        oR = [Reg("w_oT%d" % t) for t in range(NT)]
        otok = [sb2("w_otok%d" % i, [128, 1024], BF16) for i in range(2)]
        otR = [Reg("w_otok%d" % i) for i in range(2)]
        ident = sb2("w_id", [128, 128], BF16)
        identR = Reg("w_id")
        P.dma("sp", ident[:], ident_d, w=[identR])
        Wo, woR = load_w(E, sb2, "w_wo", w_o.rearrange("(k p) f -> p k f", p=128), [128, 8, 1024])
        for t16 in range(16):
            tt = t16 // 4
            b = t16 % 2
            ld_otok(t16, otok[b], otR[b])
            for half in range(2):
                pz, pzR = E.ps[2 * b + half], E.psR[2 * b + half]
                for kq in range(4):
                    k = half * 4 + kq
                    P.op("pe", (lambda e, k=k, kq=kq, b=b, pz=pz: e.matmul(
                        pz[:, kq * 128:(kq + 1) * 128], otok[b][:, k * 128:(k + 1) * 128], ident[:],
                        start=True, stop=True)), r=[otR[b], identR], pw=[pzR])
                P.op("act", (lambda e, half=half, pz=pz, t16=t16: e.activation(
                    oT[:, half * 4:(half + 1) * 4, t16 * 128:(t16 + 1) * 128],
                    pz[:].rearrange("p (k t) -> p k t", t=128), AF.Copy)), r=[pzR], pw=[oR[tt]])
        P.wait("dve", r=E.xR)
        cnt = 0
        for tt in range(NT):
            ts = slice(tt * 512, (tt + 1) * 512)
            for oc in range(DC):
                pb = cnt % 2
                cnt += 1
                pd, pdR = E.ps[4 + pb], E.psR[4 + pb]
                for k in range(DC):
                    P.op("pe", (lambda e, k=k, oc=oc, pd=pd, ts=ts: e.matmul(
                        pd[:], Wo[:, k, oc * 128:(oc + 1) * 128], oT[:, k, ts],
                        start=(k == 0), stop=(k == DC - 1))), r=[woR, oR[tt]], pw=[pdR])
                P.op("dve", (lambda e, oc=oc, pd=pd, ts=ts: e.scalar_tensor_tensor(
                    E.xT[:, oc, ts], pd[:], dv[:, 8 * Gcol + oc:8 * Gcol + oc + 1], E.xT[:, oc, ts],
                    ALU.mult, ALU.add)), r=[pdR, E.dvR], pw=[E.xR[tt]])
        fence(E, [oR, otR, [woR, identR]])


def proj_chunks(E, Wsb, wR, zT, zR, mparts, tt, psbase=0):
    P = E.P
    ts = slice(tt * 512, (tt + 1) * 512)
    res = []
    m0 = 0
    for mi, msz in enumerate(mparts):
        pz, pzR = E.ps[psbase + mi], E.psR[psbase + mi]
        for k in range(DC):
            P.op("pe", (lambda e, k=k, m0=m0, msz=msz, pz=pz: e.matmul(
                pz[0:msz, :], Wsb[:, k, m0:m0 + msz], zT[:, k, ts], start=(k == 0), stop=(k == DC - 1))),
                r=[wR, zR[tt]], pw=[pzR])
        res.append((pz, pzR, msz))
        m0 += msz
    return res


def mla_prep(E, w_dq, gqa_d, w_uq, gq_d, w_dkv, gkva_d, wk_full, gk_d, w_uv, q_d, k_d, v_d, A2, B2, Akv, Bkv, hooks=None):
    nc, P = E.nc, E.P
    with ExitStack() as es2:
        sb2 = lambda n, s, d: es2.enter_context(nc.sbuf_tensor("s%d_" % next(_UID) + n, s, d))
        hT = sb2("l_hT", [128, DC, NTOK], BF16)
        hR = [Reg("l_hT%d" % t) for t in range(NT)]
        zT = sb2("l_zT", [128, 3, NTOK], BF16)
        zR = [Reg("l_zT%d" % t) for t in range(NT)]
        tmpb = [sb2("l_tmp%d" % i, [128, 512], F32) for i in range(2)]
        tmpR = [Reg("l_tmp%d" % i) for i in range(2)]
        sqb = [sb2("l_sq%d" % i, [128, 512], BF16) for i in range(2)]
        sqR = [Reg("l_sq%d" % i) for i in range(2)]
        rs1 = sb2("l_rs", [128, 512], F32)
        rs1R = Reg("l_rs")
        rsb = [sb2("l_rsb%d" % i, [128, 512], F32) for i in range(2)]
        rsR = [Reg("l_rsb%d" % i) for i in range(2)]
        stg = [sb2("l_stg%d" % i, [128, 512], BF16) for i in range(2)]
        stgR = [Reg("l_stg%d" % i) for i in range(2)]
        vst = [sb2("l_vst%d" % i, [128, 1024], BF16) for i in range(2)]
        vstR = [Reg("l_vst%d" % i) for i in range(2)]
        gv = sb2("l_gv", [128, 16], F32)
        E.gvR = Reg("l_gv")
        P.dma("sp", gv[:, 0:3], gqa_d, pw=[E.gvR])
        P.dma("sp", gv[:, 3:5], gkva_d, pw=[E.gvR])
        P.dma("sp", gv[0:96, 5:6], gq_d, pw=[E.gvR])
        P.dma("sp", gv[0:96, 6:7], gk_d, pw=[E.gvR])
        P.op("dve", lambda e: e.tensor_scalar(gv[:, 8:11], gv[:, 0:3], float(np.sqrt(384.0)), None, ALU.mult), r=[E.gvR], pw=[E.gvR])
        P.op("dve", lambda e: e.tensor_scalar(gv[:, 11:13], gv[:, 3:5], float(np.sqrt(256.0)), None, ALU.mult), r=[E.gvR], pw=[E.gvR])
        qR, kR, vR = Reg("q_d2"), Reg("k_d2"), Reg("v_d2")
        H = hooks or {}
        Wdq, wdqR = load_w(E, sb2, "l_wdq", w_dq.rearrange("(k p) f -> p k f", p=128), [128, 8, 384])
        Wuq, wuqR = load_w(E, sb2, "l_wuq", w_uq.rearrange("(k p) f -> p k f", p=128), [128, 3, 1536])
        Wdkv, wdkvR = load_w(E, sb2, "l_wdkv", w_dkv.rearrange("(k p) f -> p k f", p=128), [128, 8, 288])
        Wkf = sb2("l_wkf", [128, 3, 1536], BF16)
        wkfR = Reg("l_wkf")
        P.dma("pool", Wkf[:, 0:2, :], wk_full[0:256, :].rearrange("(k p) f -> p k f", p=128), pw=[wkfR])
        P.dma("pool", Wkf[0:32, 2, :], wk_full[256:288, :], pw=[wkfR])
        Wuv, wuvR = load_w(E, sb2, "l_wuv", w_uv.rearrange("(k p) f -> p k f", p=128), [128, 2, 1024])
        norm_mod(E, Akv, Bkv, hT, hR, tmpb, tmpR, sqb, sqR, rs1, rs1R)
        for tt in range(NT):
            ts = slice(tt * 512, (tt + 1) * 512)
            zps = proj_chunks(E, Wdkv, wdkvR, hT, hR, [128, 128, 32], tt, psbase=0)
            rms_chunks(E, zps[0:2], [gv[:, 11:12], gv[:, 12:13]], zT, zR, tt, 2, tmpb, tmpR, sqb, sqR, rs1, rs1R)
            pz, pzR, _ = zps[2]
            P.op("act", (lambda e, pz=pz, ts=ts: e.activation(zT[0:32, 2, ts], pz[0:32, :], AF.Copy)), r=[pzR], pw=[zR[tt]])
        head_norm_proj(E, Wkf, wkfR, [128, 128, 32], zT, zR, 96, gv[:, 6:7], k_d, H.get("kR", kR), stg, stgR, sqb, sqR, rsb, rsR, 1, hook=H.get("k"))
        v_proj(E, Wuv, wuvR, [128, 128], zT, zR, v_d, H.get("vR", vR), vst, vstR, hook=H.get("v"))
        norm_mod(E, A2, B2, hT, hR, tmpb, tmpR, sqb, sqR, rs1, rs1R)
        for tt in range(NT):
            zps = proj_chunks(E, Wdq, wdqR, hT, hR, [128, 128, 128], tt, psbase=0)
            rms_chunks(E, zps, [gv[:, 8:9], gv[:, 9:10], gv[:, 10:11]], zT, zR, tt, 3, tmpb, tmpR, sqb, sqR, rs1, rs1R)
        head_norm_proj(E, Wuq, wuqR, [128, 128, 128], zT, zR, 96, gv[:, 5:6], q_d, H.get("qR", qR), stg, stgR, sqb, sqR, rsb, rsR, 1, hook=H.get("q"))
        fence(E, [hR, zR, tmpR, sqR, [rs1R], rsR, stgR, vstR, [E.gvR, wdqR, wuqR, wdkvR, wkfR, wuvR]])
        E.outRs = getattr(E, "outRs", []) + [qR, kR, vR]


SEQ = 4096
NQT = 8
NKT = 32
BIGM = 30000.0


def attn_phase(E, ld, o_d, dk, scale, tri_d, moba=None, rope=None, side=None, pre=None):
    return _attn_phase(E, ld, o_d, dk, scale, tri_d, moba, rope, side, pre)


def _attn_phase(E, ld, o_d, dk, scale, tri_d, moba, rope, side_factory, pre):
    nc, P = E.nc, E.P
    with ExitStack() as es2:
        sb2 = lambda n, s, d: es2.enter_context(nc.sbuf_tensor("s%d_" % next(_UID) + n, s, d))
        Kb = [sb2("a_K%d" % i, [128, SEQ], BF16) for i in range(2)]
        Qb = [sb2("a_Q%d" % i, [128, SEQ], BF16) for i in range(2)]
        Vb = [sb2("a_V%d" % i, [128, NKT, 65], BF16) for i in range(2)]
        KR = [Reg("a_K%d" % i) for i in range(2)]
        QR = [Reg("a_Q%d" % i) for i in range(2)]
        VR = [Reg("a_V%d" % i) for i in range(2)]
        QmR = [[Reg("a_Qm%d_%d" % (i, j)) for j in range(NQT)] for i in range(2)]
        sbanks = [0, 1, 2, 3] if moba is not None else [0, 1, 2, 3, 6, 7]
        NB = len(sbanks)
        Pb = [sb2("a_P%d" % i, [128, 512], BF16) for i in range(NB)]
        PR = [Reg("a_P%d" % i) for i in range(NB)]
        osb = [sb2("a_o%d" % i, [128, 4, 64], BF16) for i in range(2)]
        osR = [Reg("a_o%d" % i) for i in range(2)]
        rec = [sb2("a_rec%d" % i, [128, 4], F32) for i in range(2)]
        recR = [Reg("a_rec%d" % i) for i in range(2)]
        tri = sb2("a_tri", [128, 128], BF16)
        triR = Reg("a_tri")
        P.dma("sp", tri[:], tri_d, w=[triR])
        oR = Reg("a_od")
        E.outRs = getattr(E, "outRs", []) + [oR]
        for i in range(2):
            P.op("pool", (lambda e, i=i: e.memset(Vb[i][:, :, 64:65], 1.0)), pw=[VR[i]])
        Dt = Et = None
        if moba is not None:
            for i in range(2):
                P.dma("sp", Kb[i][64:80, :], moba["onehot_d"], pw=[KR[i]])
            cst = sb2("a_cst", [128, 3, 512], F32)
            cstR = Reg("a_cst")
            P.dma("sp", cst[:, 0, :], moba["eligadd_d"], pw=[cstR])
            P.dma("sp", cst[:, 1, :], moba["elig01_d"], pw=[cstR])
            P.dma("sp", cst[:, 2, :], moba["own01_d"], pw=[cstR])
            ident = sb2("a_id", [128, 128], BF16)
            identR = Reg("a_id")
            P.dma("sp", ident[:], moba["ident_d"], w=[identR])
            nb31 = sb2("a_nb31", [128, 8], F32)
            nbR = Reg("a_nb31")
            P.dma("sp", nb31[:], moba["b31_d"], w=[nbR])
            P.op("dve", lambda e: e.tensor_scalar(nb31[:], nb31[:], -1.0, None, ALU.mult), r=[nbR], w=[nbR])
            rawt = [sb2("a_raw%d" % i, [128, 128], F32) for i in range(2)]
            rawR = [Reg("a_raw%d" % i) for i in range(2)]
            Dt = sb2("a_D", [128, 8, 128], BF16)
            Et = sb2("a_E", [128, 8, 128], BF16)
            DR, ER = Reg("a_D"), Reg("a_E")
            for h in range(8):
                P.dma("sp", rawt[0][:], moba["rawD_d"][h], w=[rawR[0]])
                P.op("act", (lambda e, h=h: e.activation(rawt[0][:], rawt[0][:], AF.Exp, bias=nb31[:, h:h + 1], scale=1.0)),
                     r=[nbR], w=[rawR[0]])
                P.op("dve", (lambda e, h=h: e.tensor_tensor(Dt[:, h, :], rawt[0][:], tri[:], ALU.mult)),
                     r=[rawR[0], triR], pw=[DR])
                P.dma("sp", rawt[1][:], moba["rawE_d"][h], w=[rawR[1]])
                P.op("act", (lambda e, h=h: e.activation(Et[:, h, :], rawt[1][:], AF.Exp, bias=nb31[:, h:h + 1], scale=1.0)),
                     r=[rawR[1], nbR], pw=[ER])
            ks32 = [sb2("a_ks32_%d" % i, [64, 16], F32) for i in range(2)]
            ksb = [sb2("a_ksb_%d" % i, [64, 16], BF16) for i in range(2)]
            ks32R = [Reg("a_ks32_%d" % i) for i in range(2)]
            ksbR = [Reg("a_ksb_%d" % i) for i in range(2)]
            gm = sb2("a_gm", [128, 64], F32)
            sel = sb2("a_sel", [128, 64], F32)
            mx8 = sb2("a_mx8", [128, 4, 8], F32)
            gmR, selR, mxR = Reg("a_gm"), Reg("a_sel"), Reg("a_mx8")
            Z = sb2("a_Z", [128, 4, 80], BF16)
            ZR = Reg("a_Z")
            P.op("pool", lambda e: e.memset(Z[:], 0.0), w=[ZR])
        if rope is not None:
            Ct = sb2("a_C", [128, SEQ], F32)
            St = sb2("a_S", [128, SEQ], F32)
            CR, SR = Reg("a_C"), Reg("a_S")
            pr = slice(64, 96)
            P.dma("sp", Ct[pr, :], rope["C_d"], r=[Reg("ropetab_d")], w=[CR])
            P.dma("sp", St[pr, :], rope["S_d"], r=[Reg("ropetab_d")], w=[SR])
            Qs = [sb2("a_Qs%d" % i, [128, SEQ], BF16) for i in range(2)]
            Ks = [sb2("a_Ks%d" % i, [128, SEQ], BF16) for i in range(2)]
            QsR = [Reg("a_Qs%d" % i) for i in range(2)]
            KsR = [Reg("a_Ks%d" % i) for i in range(2)]
            rt1 = sb2("a_rt1", [128, 2048], F32)
            rt2 = sb2("a_rt2", [128, 2048], F32)
            rt1R, rt2R = Reg("a_rt1"), Reg("a_rt2")

        LOOK = NB - 1
        side = side_factory(sb2) if side_factory else None

        def head_prologue(h):
            hb = h % 2
            K, Q, V = Kb[hb], Qb[hb], Vb[hb]
            ld["K"](h, K, KR[hb])
            ld["Q"](h, Q, QR[hb])
            ld["V"](h, V, VR[hb])
            if rope is not None:
                pr = slice(64, 96)
                ld["Qs"](h, Qs[hb], QsR[hb])
                ld["Ks"](h, Ks[hb], KsR[hb])
                for (X, XR, Xs, XsR) in ((Q, QR[hb], Qs[hb], QsR[hb]), (K, KR[hb], Ks[hb], KsR[hb])):
                    for hf in range(2):
                        cs_ = slice(hf * 2048, (hf + 1) * 2048)
                        P.op("dve", (lambda e, X=X, cs_=cs_: e.tensor_tensor(rt1[pr, :], X[pr, cs_], Ct[pr, cs_], ALU.mult)),
                             r=[XR, CR], w=[rt1R])
                        P.op("pool", (lambda e, Xs=Xs, cs_=cs_: e.tensor_tensor(rt2[pr, :], Xs[pr, cs_], St[pr, cs_], ALU.mult)),
                             r=[XsR, SR], w=[rt2R])
                        P.op("dve", (lambda e, X=X, cs_=cs_: e.tensor_tensor(X[pr, cs_], rt1[pr, :], rt2[pr, :], ALU.add)),
                             r=[rt1R, rt2R, XR], pw=[XR])
            if moba is not None:
                P.op("dve", (lambda e, K=K, hb=hb: e.tensor_reduce(
                    ks32[hb][:], K[0:64, :].rearrange("p (n t) -> p n t", t=256), AX.X, ALU.add)), r=[KR[hb]], w=[ks32R[hb]])
                P.op("act", (lambda e, hb=hb: e.activation(ksb[hb][:], ks32[hb][:], AF.Copy)), r=[ks32R[hb]], w=[ksbR[hb]])

        def gate_prologue(h, j):
            hb = h % 2
            Q = Qb[hb]
            qs0 = j * 512
            pg, pgR = E.ps[6], E.psR[6]
            for ip in range(4):
                P.op("pe", (lambda e, ip=ip, Q=Q, qs0=qs0, hb=hb: e.matmul(
                    pg[:, ip * 16:(ip + 1) * 16], Q[0:64, qs0 + ip * 128:qs0 + (ip + 1) * 128], ksb[hb][:],
                    start=True, stop=True)), r=[QR[hb], ksbR[hb]], pw=[pgR])
            cs = slice(j * 64, (j + 1) * 64)
            P.op("dve", (lambda e, cs=cs: e.tensor_tensor(gm[:], pg[:, 0:64], cst[:, 0, cs], ALU.add)),
                 r=[pgR, cstR], w=[gmR])
            for ip in range(4):
                P.op("dve", (lambda e, ip=ip: e.max(mx8[:, ip, :], gm[:, ip * 16:(ip + 1) * 16])), r=[gmR], pw=[mxR])
            for ip in range(4):
                P.op("dve", (lambda e, ip=ip: e.tensor_scalar(
                    sel[:, ip * 16:(ip + 1) * 16], gm[:, ip * 16:(ip + 1) * 16], mx8[:, ip, 2:3], None, ALU.is_ge)),
                    r=[gmR, mxR], pw=[selR])
            P.op("dve", (lambda e, cs=cs: e.tensor_tensor(sel[:], sel[:], cst[:, 1, cs], ALU.mult)), r=[selR, cstR], w=[selR])
            P.op("dve", (lambda e, cs=cs: e.tensor_tensor(sel[:], sel[:], cst[:, 2, cs], ALU.add)), r=[selR, cstR], w=[selR])
            P.op("dve", lambda e: e.tensor_scalar(
                Z[:, :, 64:80], sel[:].rearrange("p (i n) -> p i n", n=16), -1.0, BIGM, ALU.add, ALU.mult),
                r=[selR], w=[ZR])
            pz, pzR = E.ps[7], E.psR[7]
            for ip in range(4):
                P.op("pe", (lambda e, ip=ip: e.matmul(pz[0:80, ip * 128:(ip + 1) * 128], Z[:, ip, :], ident[:],
                                                     start=True, stop=True)), r=[ZR, identR], pw=[pzR])
            P.op("act", (lambda e, Q=Q, qs0=qs0: e.activation(Q[64:80, qs0:qs0 + 512], pz[64:80, :], AF.Copy)),
                 r=[pzR], w=[QmR[hb][j]])

        units = [(h, j, g) for h in range(8) for j in range(NQT) for g in range(4 * j + 4)]
        first_of_head = {}
        for idx, (h, j, g) in enumerate(units):
            if (j, g) == (0, 0):
                first_of_head[h] = idx
        NU = len(units)

        def front(idx):
            h, j, g = units[idx]
            hb = h % 2
            K, Q = Kb[hb], Qb[hb]
            qs0 = j * 512
            imin = max(0, g - 4 * j)
            c0 = imin * 128
            pb = idx % NB
            pS, pSR = E.ps[sbanks[pb]], E.psR[sbanks[pb]]
            rds = [KR[hb], QR[hb]] + ([QmR[hb][j]] if moba is not None else [])
            kk = 80 if moba is not None else dk
            P.op("pe", (lambda e: e.matmul(
                pS[:, c0:512], K[0:kk, g * 128:(g + 1) * 128], Q[0:kk, qs0 + c0:qs0 + 512],
                start=True, stop=True)), r=rds, pw=[pSR])
            P.op("act", (lambda e: e.activation(Pb[pb][:, c0:512], pS[:, c0:512], AF.Exp, scale=scale)),
                 r=[pSR], w=[PR[pb]])
            for ip in range(imin, 4):
                G = 4 * j + ip
                tab = None
                if g == G:
                    tab = (Dt[:, h, :], DR) if moba is not None else (tri[:], triR)
                elif g == G - 1 and moba is not None:
                    tab = (Et[:, h, :], ER)
                if tab is not None:
                    P.op("dve", (lambda e, ip=ip, tab=tab: e.tensor_tensor(
                        Pb[pb][:, ip * 128:(ip + 1) * 128], Pb[pb][:, ip * 128:(ip + 1) * 128], tab[0], ALU.mult)),
                        r=[tab[1], PR[pb]], pw=[PR[pb]])

        def back(idx):
            h, j, g = units[idx]
            hb = h % 2
            V = Vb[hb]
            qs0 = j * 512
            imin = max(0, g - 4 * j)
            pb = idx % NB
            ob = (h * NQT + j) % 2
            po, poR = E.ps[4 + ob], E.psR[4 + ob]
            for ip in range(imin, 4):
                G = 4 * j + ip
                P.op("pe", (lambda e, ip=ip, G=G: e.matmul(
                    po[:, ip * 65:(ip + 1) * 65], Pb[pb][:, ip * 128:(ip + 1) * 128], V[:, g, :],
                    start=(g == 0 and ip == 0), stop=(g == G), skip_group_check=True)), r=[PR[pb], VR[hb]], pw=[poR])
            if g == 4 * j + 3:
                P.op("dve", (lambda e: e.reciprocal(
                    rec[ob][:], po[:, 0:260].rearrange("p (i c) -> p i c", c=65)[:, :, 64])), r=[poR], w=[recR[ob]])
                for ip in range(4):
                    P.op("dve", (lambda e, ip=ip: e.tensor_scalar(
                        osb[ob][:, ip, :], po[:, ip * 65:ip * 65 + 64], rec[ob][:, ip:ip + 1], None, ALU.mult)),
                        r=[poR, recR[ob]], pw=[osR[ob]])
                P.dma("sp", o_d[qs0:qs0 + 512, h * 64:(h + 1) * 64].rearrange("(i p) d -> p i d", p=128), osb[ob][:],
                      r=[osR[ob]], pw=[oR])

        if pre:
            pre()
        head_prologue(0)
        if moba is not None:
            gate_prologue(0, 0)
        side_jobs = list(side) if side else []
        for idx in range(NU + LOOK):
            if idx < NU:
                h, j, g = units[idx]
                if g == 0 and moba is not None and j + 1 < NQT:
                    gate_prologue(h, j + 1)
                if side_jobs and g == 0 and j >= 2:
                    side_jobs.pop(0)()
                if h + 1 < 8 and idx == first_of_head[h] + LOOK + 1:
                    head_prologue(h + 1)
                    if moba is not None:
                        gate_prologue(h + 1, 0)
                front(idx)
            if idx - LOOK >= 0:
                back(idx - LOOK)
        while side_jobs:
            side_jobs.pop(0)()


def rope_table_jobs(E, pos_d, inv_d, C_d, S_d):
    P = E.P
    TWO_PI = float(2 * np.pi)

    def factory(sb2):
        CW = 1024
        posi = sb2("r_posi", [128, CW], I32)
        rr = sb2("r_rr", [128, CW], F32)
        ni = sb2("r_ni", [128, CW], I32)
        nf = sb2("r_nf", [128, CW], F32)
        dst = sb2("r_dst", [128, CW], F32)
        inv = sb2("r_inv", [128, 2], F32)
        posR, rrR, niR, nfR, dstR, invR = Reg("r_posi"), Reg("r_rr"), Reg("r_ni"), Reg("r_nf"), Reg("r_dst"), Reg("r_inv")
        tabR = Reg("ropetab_d")
        pr = slice(64, 96)
        P.dma("sp", inv[:], inv_d, w=[invR])
        jobs = []

        def job(ck, which):
            cs = slice(ck * CW, (ck + 1) * CW)
            shift = 0.0 if which == "S" else 0.25
            if which == "S":
                P.dma("sp", posi[pr, :], pos_d[:, cs], w=[posR])
                P.op("dve", lambda e: e.tensor_copy(rr[pr, :], posi[pr, :]), r=[posR], w=[rrR])
                P.op("dve", lambda e: e.tensor_scalar(rr[pr, :], rr[pr, :], inv[pr, 0:1], 1.0 / TWO_PI, ALU.mult, ALU.mult),
                     r=[rrR, invR], w=[rrR])
            P.op("dve", lambda e: e.tensor_scalar(nf[pr, :], rr[pr, :], shift, None, ALU.add), r=[rrR], w=[nfR])
            P.op("dve", lambda e: e.tensor_copy(ni[pr, :], nf[pr, :]), r=[nfR], w=[niR])
            P.op("dve", lambda e: e.tensor_copy(dst[pr, :], ni[pr, :]), r=[niR], w=[dstR])
            P.op("dve", lambda e: e.tensor_tensor(nf[pr, :], nf[pr, :], dst[pr, :], ALU.subtract), r=[nfR, dstR], w=[nfR])
            P.op("dve", lambda e: e.tensor_scalar(dst[pr, :], nf[pr, :], 0.5, None, ALU.is_gt), r=[nfR], w=[dstR])
            P.op("dve", lambda e: e.tensor_tensor(nf[pr, :], nf[pr, :], dst[pr, :], ALU.subtract), r=[nfR, dstR], w=[nfR])
            P.op("dve", lambda e: e.tensor_scalar(dst[pr, :], nf[pr, :], -0.5, None, ALU.is_lt), r=[nfR], w=[dstR])
            P.op("dve", lambda e: e.tensor_tensor(nf[pr, :], nf[pr, :], dst[pr, :], ALU.add), r=[nfR, dstR], w=[nfR])
            P.op("act", lambda e: e.activation(dst[pr, :], nf[pr, :], AF.Sin, scale=TWO_PI), r=[nfR], w=[dstR])
            if which == "S":
                P.op("dve", lambda e: e.tensor_scalar(dst[pr, :], dst[pr, :], inv[pr, 1:2], None, ALU.mult), r=[dstR, invR], w=[dstR])
            P.dma("sp", (S_d if which == "S" else C_d)[:, cs], dst[pr, :], r=[dstR], pw=[tabR])

        for ck in range(SEQ // CW):
            for which in ("S", "C"):
                jobs.append(lambda ck=ck, which=which: job(ck, which))
        return jobs
    return factory


import math
import ml_dtypes
from concourse.bass_utils import run_bass_kernel_spmd

BF = ml_dtypes.bfloat16


def _pl(v):
    return np.ascontiguousarray(np.asarray(v).reshape(-1, 128).T)


def _dt(nc, n, s, d=F32, k="ExternalInput"):
    return nc.dram_tensor(n, list(s), d, kind=k).ap()


def _finish(P, E):
    P.emit()


def _common_inputs(nc):
    a = {}
    a["wg"] = _dt(nc, "wg", [2, 2, 1024, 2816])
    a["wu"] = _dt(nc, "wu", [2, 2, 1024, 2816])
    a["wd"] = _dt(nc, "wd", [2, 2, 2816, 1024])
    return a


def _t5_bucket_np(n):
    n = np.maximum(np.asarray(n, np.int32), 0)
    nf = np.maximum(n, 1).astype(np.float32)
    large = 16 + (np.log(nf / np.float32(16)) / np.float32(math.log(128 / 16)) * np.float32(16)).astype(np.int32)
    large = np.minimum(large, 31)
    return np.where(n < 16, n, large)


PAIRS = [[0, 1], [2, 3], [4, 5], [6, 7]]


def build_fused():
    nc = bass.Bass("TRN2", target_bir_lowering=False)
    a = _common_inputs(nc)
    xT_d = _dt(nc, "xT", [1024, 2048]); cT = _dt(nc, "cT", [128, 8])
    ada_w = _dt(nc, "ada_w", [2, 1024, 9216]); ada_bT = _dt(nc, "ada_bT", [2, 128, 72]); norm_gT = _dt(nc, "norm_gT", [2, 128, 24])
    kv_ada_w = _dt(nc, "kv_ada_w", [1024, 2048]); kv_ada_bT = _dt(nc, "kv_ada_bT", [128, 16]); kv_norm_gT = _dt(nc, "kv_norm_gT", [128, 8])
    w_qkv = _dt(nc, "w_qkv", [1024, 3072]); mgq = _dt(nc, "mgq", [128, 1]); mgk = _dt(nc, "mgk", [128, 1])
    w_o1 = _dt(nc, "w_o1", [1024, 1024]); w_o2 = _dt(nc, "w_o2", [1024, 1024])
    w_dq = _dt(nc, "w_dq", [1024, 384]); gqa = _dt(nc, "gqa", [128, 3]); w_uq = _dt(nc, "w_uq", [384, 1536]); gq = _dt(nc, "gq", [96, 1])
    w_dkv = _dt(nc, "w_dkv", [1024, 288]); gkva = _dt(nc, "gkva", [128, 2]); wkf = _dt(nc, "wkf", [288, 1536]); gk = _dt(nc, "gk", [96, 1])
    w_uv = _dt(nc, "w_uv", [256, 1024])
    tri = _dt(nc, "tri", [128, 128], BF16); ident = _dt(nc, "ident", [128, 128], BF16)
    mo = dict(onehot_d=_dt(nc, "onehot", [16, 4096], BF16), eligadd_d=_dt(nc, "eligadd", [128, 512]),
              elig01_d=_dt(nc, "elig01", [128, 512]), own01_d=_dt(nc, "own01", [128, 512]),
              rawD_d=_dt(nc, "rawD", [8, 128, 128]), rawE_d=_dt(nc, "rawE", [8, 128, 128]),
              b31_d=_dt(nc, "b31", [128, 8]), ident_d=ident)
    pos_d = _dt(nc, "pos", [32, 4096], I32)
    inv_d = _dt(nc, "inv", [128, 2])
    C_d = nc.dram_tensor("i_ropeC", [32, 4096], F32).ap()
    S_d = nc.dram_tensor("i_ropeS", [32, 4096], F32).ap()
    ro = dict(C_d=C_d, S_d=S_d)
    out = _dt(nc, "outT", [1024, 2048], F32, "ExternalOutput")
    it = lambda n, s: nc.dram_tensor(n, list(s), BF16)
    q1, k1, v1 = it("i_q1", [1024, 2048]), it("i_k1", [1024, 2048]), it("i_v1", [2048, 1024])
    q1g, k1g, v1g = it("i_q1g", [2048, 2048]), it("i_k1g", [2048, 2048]), it("i_v1g", [4096, 1024])
    o1, o1g = it("i_o1", [4096, 512]), it("i_o1g", [8192, 512])
    q2, k2, v2 = it("i_q2", [1536, 2048]), it("i_k2", [1536, 2048]), it("i_v2", [2048, 1024])
    q2g, k2g, v2g = it("i_q2g", [3072, 2048]), it("i_k2g", [3072, 2048]), it("i_v2g", [4096, 1024])
    o2, o2g = it("i_o2", [4096, 512]), it("i_o2g", [8192, 512])

    with ExitStack() as es:
        P = Prog(nc, es)
        E = alloc_common(nc, es, P)
        setup_eps(E)
        parc = {}

        def par(e):
            k = id(e)
            if k not in parc:
                parc[k] = e.snap(e.partition_id() % 2)
            return parc[k]

        def gather(src, dst, names, nch):
            srcR, dstR = Reg(names[0]), Reg(names[1])
            rows = src.shape[0] // nch
            for j in range(nch):
                P.cc("AllGather", PAIRS, src.ap()[j * rows:(j + 1) * rows, :], dst.ap()[j * 2 * rows:(j + 1) * 2 * rows, :],
                     r=[srcR], pw=[dstR])
            return dstR

        def make_exchange(q, k, v, qg, kg, vg, dk, tag, nq, names):
            jj = 1 if dk == 64 else 2
            qs_ = nc.dram_tensor("i_qs" + tag, [2, 8 * dk, 2048], BF16)
            ks_ = nc.dram_tensor("i_ks" + tag, [2, 8 * dk, 2048], BF16)
            vs_ = nc.dram_tensor("i_vs" + tag, [4096, 512], BF16)
            qsR, ksR, vsR = Reg("qsel"), Reg("ksel"), Reg("vsel")

            gRs = {}
            hpc = 16 // nq

            def chunk_gather(src, g_, nm, nch, j):
                rows = src.shape[0] // nch
                P.cc("AllGather", PAIRS, src.ap()[j * rows:(j + 1) * rows, :], g_.ap()[j * 2 * rows:(j + 1) * 2 * rows, :],
                     r=[Reg(nm[0])], pw=[Reg(nm[1])])

            def hqk(src, g_, nm, key):
                def f(h):
                    if (h + 1) % hpc == 0:
                        chunk_gather(src, g_, nm, nq, h // hpc)
                        gRs[key] = Reg(nm[1])
                return f

            def hv(t16):
                if (t16 + 1) % 8 == 0:
                    chunk_gather(v, vg, names[2], 2, t16 // 8)
                    gRs["v"] = Reg(names[2][1])

            def finish():
                for (g_, s_, sR, key) in ((qg, qs_, qsR, "q"), (kg, ks_, ksR, "k")):
                    view = g_.ap().rearrange("(g j t r) n -> g j t r n", g=2, j=jj, t=2)
                    for j in range(jj):
                        P.dma("sp", s_.ap().rearrange("t (j r) n -> j t r n", j=jj)[j],
                              (lambda e, view=view, j=j: view[bass.ds(par(e), 1), j, :, :, :].rearrange("1 t r n -> t r n")),
                              r=[gRs[key]], pw=[sR])
                vview = vg.ap().rearrange("(j t i) (a c) -> j t i a c", j=2, t=2, a=2)
                for j in range(2):
                    P.dma("sp", vs_.ap().rearrange("(t j i) c -> j t i c", t=2, j=2)[j],
                          (lambda e, j=j: vview[j, :, :, bass.ds(par(e), 1), :].rearrange("t i 1 c -> t i c")), r=[gRs["v"]], pw=[vsR])
            hooks = dict(q=hqk(q, qg, names[0], "q"), k=hqk(k, kg, names[1], "k"), v=hv)
            return hooks, (qs_, ks_, vs_, qsR, ksR, vsR), finish

        def attn_loaders(qs_, ks_, vs_, dk, qsR, ksR, vsR, rope):
            qv = qs_.ap().rearrange("t (h d) n -> t h d n", h=8)
            kv = ks_.ap().rearrange("t (h d) n -> t h d n", h=8)
            vv = vs_.ap()

            def mk(view, gR, rows=None, prow=None):
                def f(h, tile, reg):
                    r0, r1 = rows if rows is not None else (0, dk)
                    p0 = prow if prow is not None else r0
                    P.dma("sp", tile[p0:p0 + (r1 - r0), :].rearrange("d (t n) -> d t n", t=2),
                          view[:, h, r0:r1, :].rearrange("t d n -> d t n"), r=[gR(h) if callable(gR) else gR], pw=[reg])
                return f

            def fV(h, tile, reg):
                P.dma("sp", tile[:, :, 0:64], vv[:, h * 64:(h + 1) * 64].rearrange("(g p) d -> p g d", p=128),
                      r=[vsR], pw=[reg])
            ld = dict(K=mk(kv, ksR), Q=mk(qv, qsR), V=fV)
            if rope:
                def sw(view, gR):
                    f1 = mk(view, gR, rows=(80, 96), prow=64)
                    f2 = mk(view, gR, rows=(64, 80), prow=80)
                    return lambda h, tile, reg: (f1(h, tile, reg), f2(h, tile, reg))
                ld["Qs"] = sw(qv, qsR)
                ld["Ks"] = sw(kv, ksR)
            return ld

        def otok_loader(og, ogR, tag):
            os_ = nc.dram_tensor("i_os" + tag, [2, 2048, 512], BF16)
            osR_ = Reg("osel")
            ov = og.ap().rearrange("(j g n) c -> j g n c", j=2, g=2)
            P.dma("sp", os_.ap(), (lambda e: ov[bass.ds(par(e), 1), :, :, :].rearrange("1 g n c -> g n c")), r=[ogR], w=[osR_])

            def f(t16, tile, reg):
                P.dma("sp", tile[:, :].rearrange("n (g c) -> n g c", g=2),
                      os_.ap()[:, t16 * 128:(t16 + 1) * 128, :].rearrange("g n c -> n g c"), r=[osR_], w=[reg])
            return f

        load_xT(E, xT_d)
        phase0_mods(E, cT, ada_w, ada_bT, norm_gT, kv_ada_w, kv_ada_bT, kv_norm_gT)
        ffn_phase(E, a["wg"][0, 0], a["wu"][0, 0], a["wd"][0, 0], 0, 1, 2)
        hooks, sel, fin = make_exchange(q1, k1, v1, q1g, k1g, v1g, 64, "1", 2, (("q_d", "q1g"), ("k_d", "k1g"), ("v_d", "v1g")))
        moba_prep(E, w_qkv, mgq, mgk, q1.ap().rearrange("(h d) n -> h d n", h=16), k1.ap().rearrange("(h d) n -> h d n", h=16),
                  v1.ap(), 3, 4, hooks=hooks)
        attn_phase(E, attn_loaders(sel[0], sel[1], sel[2], 64, sel[3], sel[4], sel[5], False), o1.ap(), 64, 8.0, tri, moba=mo,
                   side=rope_table_jobs(E, pos_d, inv_d, C_d, S_d), pre=fin)
        ogR = gather(o1, o1g, ("a_od", "o1g"), 2)
        wo_phase(E, w_o1, otok_loader(o1g, ogR, "1"), ident, 5)
        ffn_phase(E, a["wg"][0, 1], a["wu"][0, 1], a["wd"][0, 1], 6, 7, 8)
        ffn_phase(E, a["wg"][1, 0], a["wu"][1, 0], a["wd"][1, 0], 9, 10, 11)
        hooks, sel, fin = make_exchange(q2, k2, v2, q2g, k2g, v2g, 96, "2", 4, (("q_d2", "q2g"), ("k_d2", "k2g"), ("v_d2", "v2g")))
        mla_prep(E, w_dq, gqa, w_uq, gq, w_dkv, gkva, wkf, gk, w_uv, q2.ap().rearrange("(h d) n -> h d n", h=16),
                 k2.ap().rearrange("(h d) n -> h d n", h=16), v2.ap(), 12, 13, 18, 19, hooks=hooks)
        attn_phase(E, attn_loaders(sel[0], sel[1], sel[2], 96, sel[3], sel[4], sel[5], True), o2.ap(), 96, float(math.sqrt(96)), tri, rope=ro, pre=fin)
        ogR = gather(o2, o2g, ("a_od", "o2g"), 2)
        wo_phase(E, w_o2, otok_loader(o2g, ogR, "2"), ident, 14)
        ffn_phase(E, a["wg"][1, 1], a["wu"][1, 1], a["wd"][1, 1], 15, 16, 17)
        store_xT(E, out, Reg("outo"))
        P.emit()
    return nc


def kernel(**inp):
    inp = {k: np.asarray(v) for k, v in inp.items()}
    x = inp["x"]
    ada_bT = np.stack([_pl(inp["ada_b"][L]) for L in range(2)])
    norm_gT = np.stack([_pl(inp["norm_g"][L].reshape(-1)) for L in range(2)])
    kk = np.arange(128)[:, None]
    qq = np.arange(128)[None, :]
    tri = (kk <= qq).astype(np.float32).astype(BF)
    bD = _t5_bucket_np(qq - kk)
    bE = _t5_bucket_np(qq - kk + 128)
    rb = inp["rel_bias"]
    onehot = (np.arange(4096)[None, :] // 256 == np.arange(16)[:, None]).astype(np.float32).astype(BF)
    eligadd = np.zeros((8, 4, 16), np.float32); elig01 = np.zeros((8, 4, 16), np.float32); own01 = np.zeros((8, 4, 16), np.float32)
    for j in range(8):
        for ip in range(4):
            qb = (4 * j + ip) // 2
            n = np.arange(16)
            eligadd[j, ip] = np.where(n < qb, 0.0, -1e30)
            elig01[j, ip] = (n < qb)
            own01[j, ip] = (n == qb)
    rep = lambda t: np.ascontiguousarray(np.broadcast_to(t.reshape(1, -1), (128, 512))).astype(np.float32)
    ident = np.eye(128, dtype=np.float32).astype(BF)
    wkf = np.zeros((288, 16, 96), np.float32)
    wkf[0:256, :, 0:64] = inp["w_uk"].reshape(256, 16, 64)
    wkf[256:288, :, 64:96] = np.eye(32, dtype=np.float32)[:, None, :]
    wkf = wkf.reshape(288, 1536)
    invf = (np.float32(10000.0) ** (-np.arange(16, dtype=np.float32) / np.float32(16))).astype(np.float32)
    inv = np.zeros((128, 2), np.float32)
    inv[64:96, 0] = np.tile(invf, 2)
    inv[64:80, 1] = -1.0
    inv[80:96, 1] = 1.0
    shared = dict(wg=inp["ffn_w_gate"], wu=inp["ffn_w_up"], wd=inp["ffn_w_down"], ada_w=inp["ada_w"], ada_bT=ada_bT,
                  norm_gT=norm_gT, kv_ada_w=inp["kv_ada_w"], kv_ada_bT=_pl(inp["kv_ada_b"]), kv_norm_gT=_pl(inp["kv_norm_g"]),
                  w_qkv=inp["moba_w_qkv"][0],
                  mgq=np.tile(inp["moba_q_g"][0], 2).reshape(128, 1).astype(np.float32),
                  mgk=np.tile(inp["moba_k_g"][0], 2).reshape(128, 1).astype(np.float32),
                  w_o1=inp["moba_w_o"][0], w_o2=inp["mla_w_o"][0],
                  w_dq=inp["mla_w_dq"][0], gqa=_pl(inp["mla_q_a_norm_g"][0]), w_uq=inp["mla_w_uq"][0],
                  gq=inp["mla_q_g"][0].reshape(96, 1).astype(np.float32), w_dkv=inp["w_dkv"], gkva=_pl(inp["kv_a_norm_g"]),
                  wkf=wkf, gk=inp["mla_k_g"].reshape(96, 1).astype(np.float32), w_uv=inp["w_uv"],
                  tri=tri, ident=ident, onehot=onehot, eligadd=rep(eligadd), elig01=rep(elig01), own01=rep(own01), inv=inv)
    maps = []
    for b in range(4):
        for c in range(2):
            hs = slice(c * 8, (c + 1) * 8)
            m = dict(shared)
            m.update(xT=np.ascontiguousarray(x[b, c * 2048:(c + 1) * 2048].T), cT=_pl(inp["c"][b]),
                     rawD=np.ascontiguousarray(np.transpose(rb[bD][:, :, hs], (2, 0, 1))).astype(np.float32),
                     rawE=np.ascontiguousarray(np.transpose(rb[bE][:, :, hs], (2, 0, 1))).astype(np.float32),
                     b31=np.ascontiguousarray(np.broadcast_to(rb[31, hs].reshape(1, 8), (128, 8))).astype(np.float32),
                     pos=np.ascontiguousarray(np.broadcast_to(inp["positions"][b].reshape(1, 4096), (32, 4096))).astype(np.int32))
            maps.append(m)
    res = run_bass_kernel_spmd(build_fused(), maps, core_ids=list(range(8))).results
    out = np.zeros((4, 4096, 1024), np.float32)
    for i in range(8):
        b, c = divmod(i, 2)
        out[b, c * 2048:(c + 1) * 2048] = np.asarray(res[i]["outT"]).T
    return out
```

```python
from contextlib import ExitStack
import numpy as np
import concourse.bass as bass
import concourse.mybir as mybir

F32 = mybir.dt.float32
BF16 = mybir.dt.bfloat16
I32 = mybir.dt.int32
ALU = mybir.AluOpType
AF = mybir.ActivationFunctionType
AX = mybir.AxisListType

ENGS = ("pe", "act", "dve", "pool", "sp")


class Reg:
    registry = {}

    def __new__(cls, name=""):
        r = cls.registry.get(name)
        if r is None:
            r = object.__new__(cls)
            r.name = name
            r.writers = []
            r.readers = []
            r.sem = None
            r.dcount = 0
            cls.registry[name] = r
        return r


class Op:
    __slots__ = ("eng", "fn", "deps", "needs_inc", "val", "sem", "is_dma", "pos", "default_inc", "dbg")

    def __init__(self, eng, fn):
        self.eng = eng
        self.fn = fn
        self.deps = {}
        self.needs_inc = False
        self.val = None
        self.sem = None
        self.is_dma = False
        self.pos = 0
        self.default_inc = False


class Prog:
    def __init__(self, nc, es):
        Reg.registry.clear()
        self.nc = nc
        self.es = es
        self.ops = {e: [] for e in ENGS}
        self.n = 0
        self.esem = {}
        self.nsem = 0
        for e in ENGS:
            self.esem[e] = self.newsem("e_" + e)

    def newsem(self, name):
        self.nsem += 1
        return self.es.enter_context(self.nc.semaphore(name + "_%d" % self.nsem))

    def _key(self, op):
        return op.sem if op.is_dma else op.eng

    def _same(self, a, b):
        if a.is_dma != b.is_dma:
            return False
        if a.is_dma:
            return a.sem is b.sem
        return a.eng == b.eng

    def _adddep(self, op, d):
        if d is op:
            return
        if (not d.is_dma) and d.eng == op.eng and not op.is_dma:
            if op.eng == "pe":
                return
        k = id(d.sem) if d.is_dma else d.eng
        cur = op.deps.get(k)
        if cur is None or cur.pos < d.pos:
            op.deps[k] = d

    def _track(self, op, r, w, pw, same_eng_war=False):
        for x in r:
            for d in x.writers:
                self._adddep(op, d)
        for x in w:
            for d in x.writers:
                self._adddep(op, d)
            for d in x.readers:
                self._adddep(op, d)
        for x in pw:
            if x.readers:
                for d in x.readers:
                    self._adddep(op, d)
        for d in op.deps.values():
            d.needs_inc = True
        for x in r:
            x.readers = [q for q in x.readers if not self._same(q, op)] + [op]
        for x in w:
            x.writers = [op]
            x.readers = []
        for x in pw:
            if x.readers:
                x.writers = []
                x.readers = []
            x.writers = [q for q in x.writers if not self._same(q, op)] + [op]

    def op(self, eng, fn, r=(), w=(), pw=()):
        o = Op(eng, fn)
        self.n += 1
        o.pos = self.n
        o.sem = self.esem[eng]
        self._track(o, r, w, pw)
        self.ops[eng].append(o)
        return o

    def wait(self, eng, r=()):
        o = Op(eng, lambda e: None)
        self.n += 1
        o.pos = self.n
        o.sem = self.esem[eng]
        for x in r:
            for d in x.writers:
                self._adddep(o, d)
        for d in o.deps.values():
            d.needs_inc = True
        self.ops[eng].append(o)
        return o

    def dma(self, q, out, in_, r=(), w=(), pw=()):
        dst = (list(w) + list(pw))[0]
        if dst.sem is None:
            dst.sem = self.newsem("d_" + dst.name)
        def _ap(a, e):
            return a(e) if callable(a) else a
        o = Op(q, lambda e: e.dma_start(out=_ap(out, e), in_=_ap(in_, e)))
        o.is_dma = True
        o.dbg = dst.name
        self.n += 1
        o.pos = self.n
        o.sem = dst.sem
        self._track(o, r, w, pw)
        dst.dcount += 16
        o.val = dst.dcount
        o.needs_inc = True
        self.ops[q].append(o)
        return o

    def cc(self, kind, groups, in_ap, out_ap, r=(), w=(), pw=()):
        dst = (list(w) + list(pw))[0]
        sem = self.newsem("cc_" + dst.name)
        o = Op("pool", lambda e: e.collective_compute(kind, ALU.bypass, replica_groups=groups, ins=[in_ap], outs=[out_ap]))
        o.is_dma = True
        o.default_inc = True
        self.n += 1
        o.pos = self.n
        o.sem = sem
        for x in r:
            for d in x.writers:
                self._adddep(o, d)
        self._track(o, (), w, pw)
        o.val = 1
        o.needs_inc = True
        self.ops["pool"].append(o)
        return o

    def final_waits(self):
        out = []
        seen = set()
        for e in ENGS:
            for o in self.ops[e]:
                if o.is_dma:
                    seen.add(id(o.sem))
                    out = [x for x in out if x[0] is not o.sem] + [(o.sem, o.val)]
        return out

    def emit(self):
        nc = self.nc
        for e in ENGS:
            c = 0
            for o in self.ops[e]:
                if o.is_dma:
                    continue
                if o.needs_inc:
                    c += 1
                    o.val = c
        with nc.Block() as block:
            def run(eng_name, eh):
                waited = {}
                for o in self.ops[eng_name]:
                    for d in o.deps.values():
                        k = id(d.sem)
                        if waited.get(k, 0) < d.val:
                            eh.wait_ge(d.sem, d.val)
                            waited[k] = d.val
                    try:
                        ins = o.fn(eh)
                    except Exception:
                        print("EMIT FAIL", eng_name, getattr(o, "dbg", None), o.pos)
                        raise
                    if o.needs_inc:
                        if ins is None:
                            raise RuntimeError("op without instruction needs inc")
                        if o.default_inc:
                            ins.then_inc(o.sem)
                        else:
                            ins.then_inc(o.sem, 16 if o.is_dma else 1)

            @block.tensor
            def _(eh):
                run("pe", eh)

            @block.scalar
            def _(eh):
                run("act", eh)

            @block.vector
            def _(eh):
                run("dve", eh)

            @block.gpsimd
            def _(eh):
                run("pool", eh)

            @block.sync
            def _(eh):
                run("sp", eh)
                for sem, cnt in self.final_waits():
                    eh.wait_ge(sem, cnt)

NTOK = 2048
NT = 4
D = 1024
DC = 8
FF = 2816
EPS = 1e-6
FGROUPS = [(0, 512), (512, 512), (1024, 512), (1536, 512), (2048, 512), (2560, 256)]
NV = 20


import itertools
_UID = itertools.count()


class Env:
    pass


def alloc_common(nc, es, P):
    E = Env()
    E.nc, E.es, E.P = nc, es, P
    sb = lambda n, s, d: es.enter_context(nc.sbuf_tensor("s%d_" % next(_UID) + n, s, d))
    E.sb = sb
    E.ps = [es.enter_context(nc.psum_tensor("ps%d" % i, [128, 512], F32)) for i in range(8)]
    E.psR = [Reg("ps%d" % i) for i in range(8)]
    E.xT = sb("xT", [128, DC, NTOK], F32)
    E.xR = [Reg("xT%d" % t) for t in range(NT)]
    E.dv = sb("dv", [128, 256], F32)
    E.dvR = Reg("dv")
    E.ones = sb("ones", [128, 128], BF16)
    E.onesR = Reg("ones")
    E.one1 = sb("one1", [128, 128], BF16)
    E.one1R = Reg("one1")
    E.epsc = sb("epsc", [128, 1], F32)
    E.epsR = Reg("epsc")
    P.op("pool", lambda e: e.memset(E.ones[:], 1.0 / 1024), w=[E.onesR])
    P.op("pool", lambda e: e.memset(E.one1[:], 1.0), w=[E.one1R])
    P.op("pool", lambda e: e.memset(E.epsc[:], EPS), w=[E.epsR])
    P.op("pool", lambda e: e.memset(E.dv[:, 248:256], 0.0), pw=[E.dvR])
    return E


def dvcol(i):
    return slice(8 * i, 8 * i + 8)


def phase0_mods(E, cT, ada_w, ada_bT, norm_gT, kv_ada_w, kv_ada_bT, kv_norm_gT):
    nc, P, sb = E.nc, E.P, E.sb
    with ExitStack() as es2:
        sb2 = lambda n, s, d: es2.enter_context(nc.sbuf_tensor("s%d_" % next(_UID) + n, s, d))
        cs = sb2("p0_c", [128, 8], F32)
        cact = sb2("p0_cact", [128, 8], BF16)
        bT = sb2("p0_b", [128, 160], F32)
        gT = sb2("p0_g", [128, 56], F32)
        raw = sb2("p0_raw", [128, 160], F32)
        wp = [sb2("p0_w%d" % i, [128, 8, 1024], BF16) for i in range(2)]
        wR = [Reg("p0w%d" % i) for i in range(2)]
        cR, caR, bR, gR, rawR = Reg("p0c"), Reg("p0ca"), Reg("p0b"), Reg("p0g"), Reg("p0raw")
        P.dma("sp", cs[:], cT, w=[cR])
        P.dma("sp", bT[:, 0:72], ada_bT[0], pw=[bR])
        P.dma("sp", bT[:, 72:144], ada_bT[1], pw=[bR])
        P.dma("sp", bT[:, 144:160], kv_ada_bT, pw=[bR])
        P.dma("sp", gT[:, 0:24], norm_gT[0], pw=[gR])
        P.dma("sp", gT[:, 24:48], norm_gT[1], pw=[gR])
        P.dma("sp", gT[:, 48:56], kv_norm_gT, pw=[gR])
        P.op("act", lambda e: e.activation(cact[:], cs[:], AF.Silu), r=[cR], w=[caR])
        pieces = []
        for L in range(2):
            for v in range(9):
                pieces.append((ada_w[L].rearrange("(k p) f -> p k f", p=128)[:, :, v * 1024:(v + 1) * 1024], 9 * L + v))
        for v in range(2):
            pieces.append((kv_ada_w.rearrange("(k p) f -> p k f", p=128)[:, :, v * 1024:(v + 1) * 1024], 18 + v))
        psb = E.ps[7]
        for n, (src, vi) in enumerate(pieces):
            b = n % 2
            P.dma("pool", wp[b][:], src, w=[wR[b]])
            pst = E.ps[6 + (n % 2)]
            psr = E.psR[6 + (n % 2)]
            for kc in range(8):
                for ic in range(8):
                    P.op("pe", (lambda e, b=b, kc=kc, ic=ic, pst=pst: e.matmul(
                        pst[:, kc:kc + 1], wp[b][:, ic, kc * 128:(kc + 1) * 128], cact[:, ic:ic + 1],
                        start=(ic == 0), stop=(ic == 7))), r=[wR[b], caR], pw=[psr])
            P.op("dve", (lambda e, vi=vi, pst=pst: e.tensor_tensor(
                raw[:, dvcol(vi)], pst[:, 0:8], bT[:, dvcol(vi)], ALU.add)), r=[psr, bR], pw=[rawR])
        dv = E.dv
        for L in range(2):
            for j in range(3):
                base = 9 * L + 3 * j
                sh, sc, gg = base, base + 1, base + 2
                gcol = slice(24 * L + 8 * j, 24 * L + 8 * j + 8)
                P.op("dve", (lambda e, sc=sc, gcol=gcol, base=base: e.scalar_tensor_tensor(
                    dv[:, dvcol(base)], raw[:, dvcol(sc)], 1.0, gT[:, gcol], ALU.add, ALU.mult)),
                    r=[rawR, gR], pw=[E.dvR])
                P.op("dve", (lambda e, sh=sh, base=base: e.tensor_copy(dv[:, dvcol(base + 1)], raw[:, dvcol(sh)])),
                     r=[rawR], pw=[E.dvR])
                mul = 1.0 if j == 1 else 0.5
                P.op("dve", (lambda e, gg=gg, base=base, mul=mul: e.tensor_scalar(
                    dv[:, dvcol(base + 2)], raw[:, dvcol(gg)], mul, None, ALU.mult)),
                    r=[rawR], pw=[E.dvR])
        P.op("dve", lambda e: e.scalar_tensor_tensor(dv[:, dvcol(18)], raw[:, dvcol(19)], 1.0, gT[:, 48:56], ALU.add, ALU.mult),
             r=[rawR, gR], pw=[E.dvR])
        P.op("dve", lambda e: e.tensor_copy(dv[:, dvcol(19)], raw[:, dvcol(18)]), r=[rawR], pw=[E.dvR])
        fence(E, [[rawR, gR, bR, caR, cR, wR[0], wR[1]]])


def norm_mod(E, Acol, Bcol, hT, hR, tmpb, tmpR, sqb, sqR, rsb, rsR, psi=6):
    P = E.P
    dv = E.dv
    pst, psr = E.ps[psi], E.psR[psi]
    for tt in range(NT):
        ts = slice(tt * 512, (tt + 1) * 512)
        for k in range(DC):
            b = k % 2
            P.op("act", (lambda e, k=k, b=b, ts=ts: e.activation(sqb[b][:], E.xT[:, k, ts], AF.Square)),
                 r=[E.xR[tt]], w=[sqR[b]])
            P.op("pe", (lambda e, k=k, b=b: e.matmul(pst[:], E.ones[:], sqb[b][:], start=(k == 0), stop=(k == DC - 1))),
                 r=[sqR[b], E.onesR], pw=[psr])
        P.op("act", lambda e: e.activation(rsb[:], pst[:], AF.Sqrt, bias=E.epsc[:, 0:1], scale=1.0),
             r=[psr, E.epsR], w=[rsR])
        P.op("dve", lambda e: e.reciprocal(rsb[:], rsb[:]), r=[rsR], w=[rsR])
        for k in range(DC):
            b = k % 2
            P.op("dve", (lambda e, k=k, b=b, ts=ts: e.tensor_tensor(tmpb[b][:], E.xT[:, k, ts], rsb[:], ALU.mult)),
                 r=[E.xR[tt], rsR], w=[tmpR[b]])
            P.op("act", (lambda e, k=k, b=b, ts=ts: e.activation(
                hT[:, k, ts], tmpb[b][:], AF.Identity, bias=dv[:, 8 * Bcol + k:8 * Bcol + k + 1],
                scale=dv[:, 8 * Acol + k:8 * Acol + k + 1])), r=[tmpR[b], E.dvR], pw=[hR[tt]])


def ffn_phase(E, wg, wu, wd, Acol, Bcol, Gcol):
    nc, P = E.nc, E.P
    dv = E.dv
    with ExitStack() as es2:
        sb2 = lambda n, s, d: es2.enter_context(nc.sbuf_tensor("s%d_" % next(_UID) + n, s, d))
        hT = sb2("f_hT", [128, DC, NTOK], BF16)
        hR = [Reg("f_hT%d" % t) for t in range(NT)]
        wgb = [sb2("f_wg%d" % i, [128, DC, 512], BF16) for i in range(2)]
        wub = [sb2("f_wu%d" % i, [128, DC, 512], BF16) for i in range(2)]
        wdb = [sb2("f_wd%d" % i, [128, 4, D], BF16) for i in range(2)]
        wgR = [Reg("f_wg%d" % i) for i in range(2)]
        wuR = [Reg("f_wu%d" % i) for i in range(2)]
        wdR = [Reg("f_wd%d" % i) for i in range(2)]
        act = [sb2("f_act%d" % i, [128, 4, NTOK], BF16) for i in range(2)]
        actR = [[Reg("f_act%d_%d" % (i, t)) for t in range(NT)] for i in range(2)]
        tmpb = [sb2("f_tmp%d" % i, [128, 512], F32) for i in range(2)]
        tmpR = [Reg("f_tmp%d" % i) for i in range(2)]
        sqb = [sb2("f_sq%d" % i, [128, 512], BF16) for i in range(2)]
        sqR = [Reg("f_sq%d" % i) for i in range(2)]
        rsb = sb2("f_rs", [128, 512], F32)
        rsR = Reg("f_rs")

        def load_group(gi):
            f0, fw = FGROUPS[gi]
            b = gi % 2
            nch = fw // 128
            P.dma("pool", wgb[b][:, :, 0:fw], wg.rearrange("(k p) f -> p k f", p=128)[:, :, f0:f0 + fw], w=[wgR[b]])
            P.dma("pool", wub[b][:, :, 0:fw], wu.rearrange("(k p) f -> p k f", p=128)[:, :, f0:f0 + fw], w=[wuR[b]])
            P.dma("pool", wdb[b][:, 0:nch, :], wd[f0:f0 + fw, :].rearrange("(k p) d -> p k d", p=128), w=[wdR[b]])

        P.wait("dve", r=E.xR)
        load_group(0)
        norm_mod(E, Acol, Bcol, hT, hR, tmpb, tmpR, sqb, sqR, rsb, rsR)
        cnt = 0
        for gi, (f0, fw) in enumerate(FGROUPS):
            b = gi % 2
            nch = fw // 128
            if gi + 1 < len(FGROUPS):
                load_group(gi + 1)
            for tt in range(NT):
                ts = slice(tt * 512, (tt + 1) * 512)
                for fc in range(nch):
                    pb = cnt % 2
                    cnt += 1
                    pg, pgR = E.ps[pb], E.psR[pb]
                    pu, puR = E.ps[2 + pb], E.psR[2 + pb]
                    for k in range(DC):
                        P.op("pe", (lambda e, k=k, fc=fc, b=b, pg=pg, ts=ts: e.matmul(
                            pg[:], wgb[b][:, k, fc * 128:(fc + 1) * 128], hT[:, k, ts],
                            start=(k == 0), stop=(k == DC - 1))), r=[wgR[b], hR[tt]], pw=[pgR])
                    for k in range(DC):
                        P.op("pe", (lambda e, k=k, fc=fc, b=b, pu=pu, ts=ts: e.matmul(
                            pu[:], wub[b][:, k, fc * 128:(fc + 1) * 128], hT[:, k, ts],
                            start=(k == 0), stop=(k == DC - 1))), r=[wuR[b], hR[tt]], pw=[puR])
                    P.op("act", (lambda e, pb=pb, pg=pg: e.activation(tmpb[pb][:], pg[:], AF.Silu)),
                         r=[pgR], w=[tmpR[pb]])
                    P.op("dve", (lambda e, pb=pb, pu=pu, b=b, fc=fc, ts=ts: e.tensor_tensor(
                        act[b][:, fc, ts], tmpb[pb][:], pu[:], ALU.mult)), r=[tmpR[pb], puR], pw=[actR[b][tt]])
            for tt in range(NT):
                ts = slice(tt * 512, (tt + 1) * 512)
                for oc in range(DC):
                    pb = cnt % 2
                    cnt += 1
                    pd, pdR = E.ps[4 + pb], E.psR[4 + pb]
                    for fc in range(nch):
                        P.op("pe", (lambda e, fc=fc, oc=oc, b=b, pd=pd, ts=ts: e.matmul(
                            pd[:], wdb[b][:, fc, oc * 128:(oc + 1) * 128], act[b][:, fc, ts],
                            start=(fc == 0), stop=(fc == nch - 1))), r=[wdR[b], actR[b][tt]], pw=[pdR])
                    P.op("dve", (lambda e, oc=oc, pd=pd, ts=ts: e.scalar_tensor_tensor(
                        E.xT[:, oc, ts], pd[:], dv[:, 8 * Gcol + oc:8 * Gcol + oc + 1], E.xT[:, oc, ts],
                        ALU.mult, ALU.add)), r=[pdR, E.dvR], pw=[E.xR[tt]])
        fence(E, [hR, wgR, wuR, wdR, actR[0], actR[1], tmpR, sqR, [rsR]])


def fence(E, reglists):
    regs = [x for l in reglists for x in l]
    P = E.P
    if not hasattr(E, "fenceR"):
        E.fenceR = Reg("fence")
    o = P.op("dve", lambda e: e.tensor_copy(E.dv[:, 255:256], E.dv[:, 254:255]), r=[E.dvR], w=regs + [E.fenceR])
    for eng in ("pe", "act", "pool", "sp"):
        P.wait(eng, r=[E.fenceR])


def load_xT(E, xT_d):
    P = E.P
    for tt in range(NT):
        ts = slice(tt * 512, (tt + 1) * 512)
        P.dma("sp", E.xT[:, :, ts], xT_d.rearrange("(k p) t -> p k t", p=128)[:, :, ts], w=[E.xR[tt]])


def store_xT(E, out_d, outR):
    P = E.P
    for tt in range(NT):
        ts = slice(tt * 512, (tt + 1) * 512)
        P.dma("sp", out_d.rearrange("(k p) t -> p k t", p=128)[:, :, ts], E.xT[:, :, ts], r=[E.xR[tt]], pw=[outR])


def load_w(E, sb2, name, view, shape, q="pool"):
    t = sb2(name, shape, BF16)
    R = Reg(name)
    E.P.dma(q, t[:], view, w=[R])
    return t, R


def rms_chunks(E, zps_list, gcols, outT, outR, tt, nfeat, tmpb, tmpR, sqb, sqR, rsb, rsR, sspsi=6):
    P = E.P
    ts = slice(tt * 512, (tt + 1) * 512)
    pss, pssR = E.ps[sspsi], E.psR[sspsi]
    n = len(zps_list)
    for ci, (pz, pzR, nr) in enumerate(zps_list):
        b = ci % 2
        P.op("act", (lambda e, pz=pz, nr=nr, b=b: e.activation(sqb[b][0:nr, :], pz[0:nr, :], AF.Square)),
             r=[pzR], w=[sqR[b]])
        P.op("pe", (lambda e, nr=nr, b=b, ci=ci: e.matmul(pss[:], E.one1[0:nr, :], sqb[b][0:nr, :],
                                                          start=(ci == 0), stop=(ci == n - 1))),
             r=[sqR[b], E.one1R], pw=[pssR])
    P.op("act", lambda e: e.activation(rsb[:], pss[:], AF.Sqrt, bias=E.epsn[:, nfeat:nfeat + 1], scale=1.0),
         r=[pssR, E.epsR], w=[rsR])
    P.op("dve", lambda e: e.reciprocal(rsb[:], rsb[:]), r=[rsR], w=[rsR])
    for ci, (pz, pzR, nr) in enumerate(zps_list):
        P.op("dve", (lambda e, pz=pz, nr=nr, ci=ci: e.scalar_tensor_tensor(
            outT[0:nr, ci, ts], pz[0:nr, :], gcols[ci], rsb[0:nr, :], ALU.mult, ALU.mult)),
            r=[pzR, rsR, E.gvR], pw=[outR[tt]])


def head_norm_proj(E, Wsb, wR, kparts, zT, zR, dk, gcol, out_d, outR, stg, stgR, sqb, sqR, rsb, rsR, nfeat_idx, hook=None):
    P = E.P
    cnt = 0
    for h in range(16):
        for tt in range(NT):
            ts = slice(tt * 512, (tt + 1) * 512)
            pb = cnt % 2
            cnt += 1
            pq, pqR = E.ps[pb], E.psR[pb]
            pss, pssR = E.ps[2 + pb], E.psR[2 + pb]
            nk = len(kparts)
            for ci, ksz in enumerate(kparts):
                P.op("pe", (lambda e, ci=ci, ksz=ksz, h=h, pq=pq, ts=ts: e.matmul(
                    pq[0:dk, :], Wsb[0:ksz, ci, h * dk:(h + 1) * dk], zT[0:ksz, ci, ts],
                    start=(ci == 0), stop=(ci == nk - 1))), r=[wR, zR[tt]], pw=[pqR])
            P.op("act", (lambda e, pq=pq, pb=pb: e.activation(sqb[pb][0:dk, :], pq[0:dk, :], AF.Square)),
                 r=[pqR], w=[sqR[pb]])
            P.op("pe", (lambda e, pss=pss, pb=pb: e.matmul(pss[0:dk, :], E.one1[0:dk, 0:dk], sqb[pb][0:dk, :],
                                                          start=True, stop=True)),
                 r=[sqR[pb], E.one1R], pw=[pssR])
            P.op("act", (lambda e, pss=pss, pb=pb: e.activation(
                rsb[pb][0:dk, :], pss[0:dk, :], AF.Sqrt, bias=E.epsn[0:dk, nfeat_idx:nfeat_idx + 1], scale=1.0)),
                r=[pssR, E.epsR], w=[rsR[pb]])
            P.op("dve", (lambda e, pb=pb: e.reciprocal(rsb[pb][0:dk, :], rsb[pb][0:dk, :])), r=[rsR[pb]], w=[rsR[pb]])
            P.op("dve", (lambda e, pq=pq, pb=pb: e.scalar_tensor_tensor(
                stg[pb][0:dk, :], pq[0:dk, :], gcol[0:dk, :], rsb[pb][0:dk, :], ALU.mult, ALU.mult)),
                r=[pqR, rsR[pb], E.gvR], w=[stgR[pb]])
            P.dma("sp", out_d[h, :, ts], stg[pb][0:dk, :], r=[stgR[pb]], pw=[outR(h) if callable(outR) else outR])
        if hook:
            hook(h)


def v_proj(E, Wv, wvR, kparts, zT, zR, v_d, vR, stg, stgR, hook=None):
    P = E.P
    cnt = 0
    nk = len(kparts)
    for t16 in range(16):
        tt = t16 // 4
        tk = slice(t16 * 128, (t16 + 1) * 128)
        sb_ = t16 % 2
        for vc in range(2):
            pb = cnt % 2
            cnt += 1
            pv, pvR = E.ps[4 + pb], E.psR[4 + pb]
            for ci, ksz in enumerate(kparts):
                P.op("pe", (lambda e, ci=ci, ksz=ksz, vc=vc, pv=pv, tk=tk: e.matmul(
                    pv[:], zT[0:ksz, ci, tk], Wv[0:ksz, ci, vc * 512:(vc + 1) * 512],
                    start=(ci == 0), stop=(ci == nk - 1))), r=[wvR, zR[tt]], pw=[pvR])
            P.op("act", (lambda e, pv=pv, vc=vc, sb_=sb_: e.activation(stg[sb_][:, vc * 512:(vc + 1) * 512], pv[:], AF.Copy)),
                 r=[pvR], pw=[stgR[sb_]])
        P.dma("sp", v_d[tk, :], stg[sb_][:], r=[stgR[sb_]], pw=[vR(t16) if callable(vR) else vR])
        if hook:
            hook(t16)


def setup_eps(E):
    E.epsn = E.sb("epsn", [128, 4], F32)
    for i, n in enumerate((64, 96, 256, 384)):
        E.P.op("pool", (lambda e, i=i, n=n: e.memset(E.epsn[:, i:i + 1], n * EPS)), pw=[E.epsR])


def moba_prep(E, w_qkv, gq_d, gk_d, q_d, k_d, v_d, Acol, Bcol, hooks=None):
    nc, P = E.nc, E.P
    with ExitStack() as es2:
        sb2 = lambda n, s, d: es2.enter_context(nc.sbuf_tensor("s%d_" % next(_UID) + n, s, d))
        hT = sb2("m_hT", [128, DC, NTOK], BF16)
        hR = [Reg("m_hT%d" % t) for t in range(NT)]
        tmpb = [sb2("m_tmp%d" % i, [128, 512], F32) for i in range(2)]
        tmpR = [Reg("m_tmp%d" % i) for i in range(2)]
        sqb = [sb2("m_sq%d" % i, [128, 512], BF16) for i in range(2)]
        sqR = [Reg("m_sq%d" % i) for i in range(2)]
        rs1 = sb2("m_rs", [128, 512], F32)
        rs1R = Reg("m_rs")
        rsb = [sb2("m_rsb%d" % i, [128, 512], F32) for i in range(2)]
        rsR = [Reg("m_rsb%d" % i) for i in range(2)]
        stg = [sb2("m_stg%d" % i, [128, 512], BF16) for i in range(2)]
        stgR = [Reg("m_stg%d" % i) for i in range(2)]
        vst = [sb2("m_vst%d" % i, [128, 1024], BF16) for i in range(2)]
        vstR = [Reg("m_vst%d" % i) for i in range(2)]
        gv = sb2("m_gv", [128, 2], F32)
        E.gvR = Reg("m_gv")
        P.dma("sp", gv[:, 0:1], gq_d, pw=[E.gvR])
        P.dma("sp", gv[:, 1:2], gk_d, pw=[E.gvR])
        wv3 = w_qkv.rearrange("(k p) f -> p k f", p=128)
        Wq, wqR = load_w(E, sb2, "m_wq", wv3[:, :, 0:1024], [128, 8, 1024])
        Wk, wkR = load_w(E, sb2, "m_wk", wv3[:, :, 1024:2048], [128, 8, 1024])
        Wv, wvR = load_w(E, sb2, "m_wv", wv3[:, :, 2048:3072], [128, 8, 1024])
        norm_mod(E, Acol, Bcol, hT, hR, tmpb, tmpR, sqb, sqR, rs1, rs1R)
        qR, kR, vR = Reg("q_d"), Reg("k_d"), Reg("v_d")
        kp = [128] * 8
        H = hooks or {}
        head_norm_proj(E, Wk, wkR, kp, hT, hR, 64, gv[:, 1:2], k_d, H.get("kR", kR), stg, stgR, sqb, sqR, rsb, rsR, 0, hook=H.get("k"))
        v_proj(E, Wv, wvR, kp, hT, hR, v_d, H.get("vR", vR), vst, vstR, hook=H.get("v"))
        head_norm_proj(E, Wq, wqR, kp, hT, hR, 64, gv[:, 0:1], q_d, H.get("qR", qR), stg, stgR, sqb, sqR, rsb, rsR, 0, hook=H.get("q"))
        fence(E, [hR, tmpR, sqR, [rs1R], rsR, stgR, vstR, [E.gvR, wqR, wkR, wvR]])
        E.outRs = getattr(E, "outRs", []) + [qR, kR, vR]


def wo_phase(E, w_o, ld_otok, ident_d, Gcol):
    nc, P = E.nc, E.P
    dv = E.dv
    with ExitStack() as es2:
        sb2 = lambda n, s, d: es2.enter_context(nc.sbuf_tensor("s%d_" % next(_UID) + n, s, d))
        oT = sb2("w_oT", [128, DC, NTOK], BF16)
        oR = [Reg("w_oT%d" % t) for t in range(NT)]
        otok = [sb2("w_otok%d" % i, [128, 1024], BF16) for i in range(2)]
        otR = [Reg("w_otok%d" % i) for i in range(2)]
        ident = sb2("w_id", [128, 128], BF16)
        identR = Reg("w_id")
        P.dma("sp", ident[:], ident_d, w=[identR])
        Wo, woR = load_w(E, sb2, "w_wo", w_o.rearrange("(k p) f -> p k f", p=128), [128, 8, 1024])
        for t16 in range(16):
            tt = t16 // 4
            b = t16 % 2
            ld_otok(t16, otok[b], otR[b])
            for half in range(2):
                pz, pzR = E.ps[2 * b + half], E.psR[2 * b + half]
                for kq in range(4):
                    k = half * 4 + kq
                    P.op("pe", (lambda e, k=k, kq=kq, b=b, pz=pz: e.matmul(
                        pz[:, kq * 128:(kq + 1) * 128], otok[b][:, k * 128:(k + 1) * 128], ident[:],
                        start=True, stop=True)), r=[otR[b], identR], pw=[pzR])
                P.op("act", (lambda e, half=half, pz=pz, t16=t16: e.activation(
                    oT[:, half * 4:(half + 1) * 4, t16 * 128:(t16 + 1) * 128],
                    pz[:].rearrange("p (k t) -> p k t", t=128), AF.Copy)), r=[pzR], pw=[oR[tt]])
        P.wait("dve", r=E.xR)
        cnt = 0
        for tt in range(NT):
            ts = slice(tt * 512, (tt + 1) * 512)
            for oc in range(DC):
                pb = cnt % 2
                cnt += 1
                pd, pdR = E.ps[4 + pb], E.psR[4 + pb]
                for k in range(DC):
                    P.op("pe", (lambda e, k=k, oc=oc, pd=pd, ts=ts: e.matmul(
                        pd[:], Wo[:, k, oc * 128:(oc + 1) * 128], oT[:, k, ts],
                        start=(k == 0), stop=(k == DC - 1))), r=[woR, oR[tt]], pw=[pdR])
                P.op("dve", (lambda e, oc=oc, pd=pd, ts=ts: e.scalar_tensor_tensor(
                    E.xT[:, oc, ts], pd[:], dv[:, 8 * Gcol + oc:8 * Gcol + oc + 1], E.xT[:, oc, ts],
                    ALU.mult, ALU.add)), r=[pdR, E.dvR], pw=[E.xR[tt]])
        fence(E, [oR, otR, [woR, identR]])


def proj_chunks(E, Wsb, wR, zT, zR, mparts, tt, psbase=0):
    P = E.P
    ts = slice(tt * 512, (tt + 1) * 512)
    res = []
    m0 = 0
    for mi, msz in enumerate(mparts):
        pz, pzR = E.ps[psbase + mi], E.psR[psbase + mi]
        for k in range(DC):
            P.op("pe", (lambda e, k=k, m0=m0, msz=msz, pz=pz: e.matmul(
                pz[0:msz, :], Wsb[:, k, m0:m0 + msz], zT[:, k, ts], start=(k == 0), stop=(k == DC - 1))),
                r=[wR, zR[tt]], pw=[pzR])
        res.append((pz, pzR, msz))
        m0 += msz
    return res


def mla_prep(E, w_dq, gqa_d, w_uq, gq_d, w_dkv, gkva_d, wk_full, gk_d, w_uv, q_d, k_d, v_d, A2, B2, Akv, Bkv, hooks=None):
    nc, P = E.nc, E.P
    with ExitStack() as es2:
        sb2 = lambda n, s, d: es2.enter_context(nc.sbuf_tensor("s%d_" % next(_UID) + n, s, d))
        hT = sb2("l_hT", [128, DC, NTOK], BF16)
        hR = [Reg("l_hT%d" % t) for t in range(NT)]
        zT = sb2("l_zT", [128, 3, NTOK], BF16)
        zR = [Reg("l_zT%d" % t) for t in range(NT)]
        tmpb = [sb2("l_tmp%d" % i, [128, 512], F32) for i in range(2)]
        tmpR = [Reg("l_tmp%d" % i) for i in range(2)]
        sqb = [sb2("l_sq%d" % i, [128, 512], BF16) for i in range(2)]
        sqR = [Reg("l_sq%d" % i) for i in range(2)]
        rs1 = sb2("l_rs", [128, 512], F32)
        rs1R = Reg("l_rs")
        rsb = [sb2("l_rsb%d" % i, [128, 512], F32) for i in range(2)]
        rsR = [Reg("l_rsb%d" % i) for i in range(2)]
        stg = [sb2("l_stg%d" % i, [128, 512], BF16) for i in range(2)]
        stgR = [Reg("l_stg%d" % i) for i in range(2)]
        vst = [sb2("l_vst%d" % i, [128, 1024], BF16) for i in range(2)]
        vstR = [Reg("l_vst%d" % i) for i in range(2)]
        gv = sb2("l_gv", [128, 16], F32)
        E.gvR = Reg("l_gv")
        P.dma("sp", gv[:, 0:3], gqa_d, pw=[E.gvR])
        P.dma("sp", gv[:, 3:5], gkva_d, pw=[E.gvR])
        P.dma("sp", gv[0:96, 5:6], gq_d, pw=[E.gvR])
        P.dma("sp", gv[0:96, 6:7], gk_d, pw=[E.gvR])
        P.op("dve", lambda e: e.tensor_scalar(gv[:, 8:11], gv[:, 0:3], float(np.sqrt(384.0)), None, ALU.mult), r=[E.gvR], pw=[E.gvR])
        P.op("dve", lambda e: e.tensor_scalar(gv[:, 11:13], gv[:, 3:5], float(np.sqrt(256.0)), None, ALU.mult), r=[E.gvR], pw=[E.gvR])
        qR, kR, vR = Reg("q_d2"), Reg("k_d2"), Reg("v_d2")
        H = hooks or {}
        Wdq, wdqR = load_w(E, sb2, "l_wdq", w_dq.rearrange("(k p) f -> p k f", p=128), [128, 8, 384])
        Wuq, wuqR = load_w(E, sb2, "l_wuq", w_uq.rearrange("(k p) f -> p k f", p=128), [128, 3, 1536])
        Wdkv, wdkvR = load_w(E, sb2, "l_wdkv", w_dkv.rearrange("(k p) f -> p k f", p=128), [128, 8, 288])
        Wkf = sb2("l_wkf", [128, 3, 1536], BF16)
        wkfR = Reg("l_wkf")
        P.dma("pool", Wkf[:, 0:2, :], wk_full[0:256, :].rearrange("(k p) f -> p k f", p=128), pw=[wkfR])
        P.dma("pool", Wkf[0:32, 2, :], wk_full[256:288, :], pw=[wkfR])
        Wuv, wuvR = load_w(E, sb2, "l_wuv", w_uv.rearrange("(k p) f -> p k f", p=128), [128, 2, 1024])
        norm_mod(E, Akv, Bkv, hT, hR, tmpb, tmpR, sqb, sqR, rs1, rs1R)
        for tt in range(NT):
            ts = slice(tt * 512, (tt + 1) * 512)
            zps = proj_chunks(E, Wdkv, wdkvR, hT, hR, [128, 128, 32], tt, psbase=0)
            rms_chunks(E, zps[0:2], [gv[:, 11:12], gv[:, 12:13]], zT, zR, tt, 2, tmpb, tmpR, sqb, sqR, rs1, rs1R)
            pz, pzR, _ = zps[2]
            P.op("act", (lambda e, pz=pz, ts=ts: e.activation(zT[0:32, 2, ts], pz[0:32, :], AF.Copy)), r=[pzR], pw=[zR[tt]])
        head_norm_proj(E, Wkf, wkfR, [128, 128, 32], zT, zR, 96, gv[:, 6:7], k_d, H.get("kR", kR), stg, stgR, sqb, sqR, rsb, rsR, 1, hook=H.get("k"))
        v_proj(E, Wuv, wuvR, [128, 128], zT, zR, v_d, H.get("vR", vR), vst, vstR, hook=H.get("v"))
        norm_mod(E, A2, B2, hT, hR, tmpb, tmpR, sqb, sqR, rs1, rs1R)
        for tt in range(NT):
            zps = proj_chunks(E, Wdq, wdqR, hT, hR, [128, 128, 128], tt, psbase=0)
            rms_chunks(E, zps, [gv[:, 8:9], gv[:, 9:10], gv[:, 10:11]], zT, zR, tt, 3, tmpb, tmpR, sqb, sqR, rs1, rs1R)
        head_norm_proj(E, Wuq, wuqR, [128, 128, 128], zT, zR, 96, gv[:, 5:6], q_d, H.get("qR", qR), stg, stgR, sqb, sqR, rsb, rsR, 1, hook=H.get("q"))
        fence(E, [hR, zR, tmpR, sqR, [rs1R], rsR, stgR, vstR, [E.gvR, wdqR, wuqR, wdkvR, wkfR, wuvR]])
        E.outRs = getattr(E, "outRs", []) + [qR, kR, vR]


SEQ = 4096
NQT = 8
NKT = 32
BIGM = 30000.0


def attn_phase(E, ld, o_d, dk, scale, tri_d, moba=None, rope=None, side=None, pre=None):
    return _attn_phase(E, ld, o_d, dk, scale, tri_d, moba, rope, side, pre)


def _attn_phase(E, ld, o_d, dk, scale, tri_d, moba, rope, side_factory, pre):
    nc, P = E.nc, E.P
    with ExitStack() as es2:
        sb2 = lambda n, s, d: es2.enter_context(nc.sbuf_tensor("s%d_" % next(_UID) + n, s, d))
        Kb = [sb2("a_K%d" % i, [128, SEQ], BF16) for i in range(2)]
        Qb = [sb2("a_Q%d" % i, [128, SEQ], BF16) for i in range(2)]
        Vb = [sb2("a_V%d" % i, [128, NKT, 65], BF16) for i in range(2)]
        KR = [Reg("a_K%d" % i) for i in range(2)]
        QR = [Reg("a_Q%d" % i) for i in range(2)]
        VR = [Reg("a_V%d" % i) for i in range(2)]
        QmR = [[Reg("a_Qm%d_%d" % (i, j)) for j in range(NQT)] for i in range(2)]
        sbanks = [0, 1, 2, 3]
        NB = len(sbanks)
        Pb = [sb2("a_P%d" % i, [128, 512], BF16) for i in range(NB)]
        PR = [Reg("a_P%d" % i) for i in range(NB)]
        osb = [sb2("a_o%d" % i, [128, 4, 64], BF16) for i in range(2)]
        osR = [Reg("a_o%d" % i) for i in range(2)]
        rec = [sb2("a_rec%d" % i, [128, 4], F32) for i in range(2)]
        recR = [Reg("a_rec%d" % i) for i in range(2)]
        tri = sb2("a_tri", [128, 128], BF16)
        triR = Reg("a_tri")
        P.dma("sp", tri[:], tri_d, w=[triR])
        oR = Reg("a_od")
        E.outRs = getattr(E, "outRs", []) + [oR]
        for i in range(2):
            P.op("pool", (lambda e, i=i: e.memset(Vb[i][:, :, 64:65], 1.0)), pw=[VR[i]])
        Dt = Et = None
        if moba is not None:
            for i in range(2):
                P.dma("sp", Kb[i][64:80, :], moba["onehot_d"], pw=[KR[i]])
            cst = sb2("a_cst", [128, 3, 512], F32)
            cstR = Reg("a_cst")
            P.dma("sp", cst[:, 0, :], moba["eligadd_d"], pw=[cstR])
            P.dma("sp", cst[:, 1, :], moba["elig01_d"], pw=[cstR])
            P.dma("sp", cst[:, 2, :], moba["own01_d"], pw=[cstR])
            ident = sb2("a_id", [128, 128], BF16)
            identR = Reg("a_id")
            P.dma("sp", ident[:], moba["ident_d"], w=[identR])
            nb31 = sb2("a_nb31", [128, 8], F32)
            nbR = Reg("a_nb31")
            P.dma("sp", nb31[:], moba["b31_d"], w=[nbR])
            P.op("dve", lambda e: e.tensor_scalar(nb31[:], nb31[:], -1.0, None, ALU.mult), r=[nbR], w=[nbR])
            rawt = [sb2("a_raw%d" % i, [128, 128], F32) for i in range(2)]
            rawR = [Reg("a_raw%d" % i) for i in range(2)]
            Dt = sb2("a_D", [128, 8, 128], BF16)
            Et = sb2("a_E", [128, 8, 128], BF16)
            DR, ER = Reg("a_D"), Reg("a_E")
            for h in range(8):
                P.dma("sp", rawt[0][:], moba["rawD_d"][h], w=[rawR[0]])
                P.op("act", (lambda e, h=h: e.activation(rawt[0][:], rawt[0][:], AF.Exp, bias=nb31[:, h:h + 1], scale=1.0)),
                     r=[nbR], w=[rawR[0]])
                P.op("dve", (lambda e, h=h: e.tensor_tensor(Dt[:, h, :], rawt[0][:], tri[:], ALU.mult)),
                     r=[rawR[0], triR], pw=[DR])
                P.dma("sp", rawt[1][:], moba["rawE_d"][h], w=[rawR[1]])
                P.op("act", (lambda e, h=h: e.activation(Et[:, h, :], rawt[1][:], AF.Exp, bias=nb31[:, h:h + 1], scale=1.0)),
                     r=[rawR[1], nbR], pw=[ER])
            ks32 = [sb2("a_ks32_%d" % i, [64, 16], F32) for i in range(2)]
            ksb = [sb2("a_ksb_%d" % i, [64, 16], BF16) for i in range(2)]
            ks32R = [Reg("a_ks32_%d" % i) for i in range(2)]
            ksbR = [Reg("a_ksb_%d" % i) for i in range(2)]
            gm = sb2("a_gm", [128, 64], F32)
            sel = sb2("a_sel", [128, 64], F32)
            mx8 = sb2("a_mx8", [128, 4, 8], F32)
            gmR, selR, mxR = Reg("a_gm"), Reg("a_sel"), Reg("a_mx8")
            Z = sb2("a_Z", [128, 4, 80], BF16)
            ZR = Reg("a_Z")
            P.op("pool", lambda e: e.memset(Z[:], 0.0), w=[ZR])
        if rope is not None:
            Ct = sb2("a_C", [128, SEQ], F32)
            St = sb2("a_S", [128, SEQ], F32)
            CR, SR = Reg("a_C"), Reg("a_S")
            pr = slice(64, 96)
            P.dma("sp", Ct[pr, :], rope["C_d"], r=[Reg("ropetab_d")], w=[CR])
            P.dma("sp", St[pr, :], rope["S_d"], r=[Reg("ropetab_d")], w=[SR])
            Qs = [sb2("a_Qs%d" % i, [128, SEQ], BF16) for i in range(2)]
            Ks = [sb2("a_Ks%d" % i, [128, SEQ], BF16) for i in range(2)]
            QsR = [Reg("a_Qs%d" % i) for i in range(2)]
            KsR = [Reg("a_Ks%d" % i) for i in range(2)]
            rt1 = sb2("a_rt1", [128, 2048], F32)
            rt2 = sb2("a_rt2", [128, 2048], F32)
            rt1R, rt2R = Reg("a_rt1"), Reg("a_rt2")

        LOOK = NB - 1
        side = side_factory(sb2) if side_factory else None

        def head_prologue(h):
            hb = h % 2
            K, Q, V = Kb[hb], Qb[hb], Vb[hb]
            ld["K"](h, K, KR[hb])
            ld["Q"](h, Q, QR[hb])
            ld["V"](h, V, VR[hb])
            if rope is not None:
                pr = slice(64, 96)
                ld["Qs"](h, Qs[hb], QsR[hb])
                ld["Ks"](h, Ks[hb], KsR[hb])
                for (X, XR, Xs, XsR) in ((Q, QR[hb], Qs[hb], QsR[hb]), (K, KR[hb], Ks[hb], KsR[hb])):
                    for hf in range(2):
                        cs_ = slice(hf * 2048, (hf + 1) * 2048)
                        P.op("dve", (lambda e, X=X, cs_=cs_: e.tensor_tensor(rt1[pr, :], X[pr, cs_], Ct[pr, cs_], ALU.mult)),
                             r=[XR, CR], w=[rt1R])
                        P.op("pool", (lambda e, Xs=Xs, cs_=cs_: e.tensor_tensor(rt2[pr, :], Xs[pr, cs_], St[pr, cs_], ALU.mult)),
                             r=[XsR, SR], w=[rt2R])
                        P.op("dve", (lambda e, X=X, cs_=cs_: e.tensor_tensor(X[pr, cs_], rt1[pr, :], rt2[pr, :], ALU.add)),
                             r=[rt1R, rt2R, XR], pw=[XR])
            if moba is not None:
                P.op("dve", (lambda e, K=K, hb=hb: e.tensor_reduce(
                    ks32[hb][:], K[0:64, :].rearrange("p (n t) -> p n t", t=256), AX.X, ALU.add)), r=[KR[hb]], w=[ks32R[hb]])
                P.op("act", (lambda e, hb=hb: e.activation(ksb[hb][:], ks32[hb][:], AF.Copy)), r=[ks32R[hb]], w=[ksbR[hb]])

        def gate_prologue(h, j):
            hb = h % 2
            Q = Qb[hb]
            qs0 = j * 512
            pg, pgR = E.ps[6], E.psR[6]
            for ip in range(4):
                P.op("pe", (lambda e, ip=ip, Q=Q, qs0=qs0, hb=hb: e.matmul(
                    pg[:, ip * 16:(ip + 1) * 16], Q[0:64, qs0 + ip * 128:qs0 + (ip + 1) * 128], ksb[hb][:],
                    start=True, stop=True)), r=[QR[hb], ksbR[hb]], pw=[pgR])
            cs = slice(j * 64, (j + 1) * 64)
            P.op("dve", (lambda e, cs=cs: e.tensor_tensor(gm[:], pg[:, 0:64], cst[:, 0, cs], ALU.add)),
                 r=[pgR, cstR], w=[gmR])
            for ip in range(4):
                P.op("dve", (lambda e, ip=ip: e.max(mx8[:, ip, :], gm[:, ip * 16:(ip + 1) * 16])), r=[gmR], pw=[mxR])
            for ip in range(4):
                P.op("dve", (lambda e, ip=ip: e.tensor_scalar(
                    sel[:, ip * 16:(ip + 1) * 16], gm[:, ip * 16:(ip + 1) * 16], mx8[:, ip, 2:3], None, ALU.is_ge)),
                    r=[gmR, mxR], pw=[selR])
            P.op("dve", (lambda e, cs=cs: e.tensor_tensor(sel[:], sel[:], cst[:, 1, cs], ALU.mult)), r=[selR, cstR], w=[selR])
            P.op("dve", (lambda e, cs=cs: e.tensor_tensor(sel[:], sel[:], cst[:, 2, cs], ALU.add)), r=[selR, cstR], w=[selR])
            P.op("dve", lambda e: e.tensor_scalar(
                Z[:, :, 64:80], sel[:].rearrange("p (i n) -> p i n", n=16), -1.0, BIGM, ALU.add, ALU.mult),
                r=[selR], w=[ZR])
            pz, pzR = E.ps[7], E.psR[7]
            for ip in range(4):
                P.op("pe", (lambda e, ip=ip: e.matmul(pz[0:80, ip * 128:(ip + 1) * 128], Z[:, ip, :], ident[:],
                                                     start=True, stop=True)), r=[ZR, identR], pw=[pzR])
            P.op("act", (lambda e, Q=Q, qs0=qs0: e.activation(Q[64:80, qs0:qs0 + 512], pz[64:80, :], AF.Copy)),
                 r=[pzR], w=[QmR[hb][j]])

        units = [(h, j, g) for h in range(8) for j in range(NQT) for g in range(4 * j + 4)]
        first_of_head = {}
        for idx, (h, j, g) in enumerate(units):
            if (j, g) == (0, 0):
                first_of_head[h] = idx
        NU = len(units)

        def front(idx):
            h, j, g = units[idx]
            hb = h % 2
            K, Q = Kb[hb], Qb[hb]
            qs0 = j * 512
            imin = max(0, g - 4 * j)
            c0 = imin * 128
            pb = idx % NB
            pS, pSR = E.ps[sbanks[pb]], E.psR[sbanks[pb]]
            rds = [KR[hb], QR[hb]] + ([QmR[hb][j]] if moba is not None else [])
            kk = 80 if moba is not None else dk
            P.op("pe", (lambda e: e.matmul(
                pS[:, c0:512], K[0:kk, g * 128:(g + 1) * 128], Q[0:kk, qs0 + c0:qs0 + 512],
                start=True, stop=True)), r=rds, pw=[pSR])
            P.op("act", (lambda e: e.activation(Pb[pb][:, c0:512], pS[:, c0:512], AF.Exp, scale=scale)),
                 r=[pSR], w=[PR[pb]])
            for ip in range(imin, 4):
                G = 4 * j + ip
                tab = None
                if g == G:
                    tab = (Dt[:, h, :], DR) if moba is not None else (tri[:], triR)
                elif g == G - 1 and moba is not None:
                    tab = (Et[:, h, :], ER)
                if tab is not None:
                    P.op("dve", (lambda e, ip=ip, tab=tab: e.tensor_tensor(
                        Pb[pb][:, ip * 128:(ip + 1) * 128], Pb[pb][:, ip * 128:(ip + 1) * 128], tab[0], ALU.mult)),
                        r=[tab[1], PR[pb]], pw=[PR[pb]])

        def back(idx):
            h, j, g = units[idx]
            hb = h % 2
            V = Vb[hb]
            qs0 = j * 512
            imin = max(0, g - 4 * j)
            pb = idx % NB
            ob = (h * NQT + j) % 2
            po, poR = E.ps[4 + ob], E.psR[4 + ob]
            for ip in range(imin, 4):
                G = 4 * j + ip
                P.op("pe", (lambda e, ip=ip, G=G: e.matmul(
                    po[:, ip * 65:(ip + 1) * 65], Pb[pb][:, ip * 128:(ip + 1) * 128], V[:, g, :],
                    start=(g == 0 and ip == 0), stop=(g == G), skip_group_check=True)), r=[PR[pb], VR[hb]], pw=[poR])
            if g == 4 * j + 3:
                P.op("dve", (lambda e: e.reciprocal(
                    rec[ob][:], po[:, 0:260].rearrange("p (i c) -> p i c", c=65)[:, :, 64])), r=[poR], w=[recR[ob]])
                for ip in range(4):
                    P.op("dve", (lambda e, ip=ip: e.tensor_scalar(
                        osb[ob][:, ip, :], po[:, ip * 65:ip * 65 + 64], rec[ob][:, ip:ip + 1], None, ALU.mult)),
                        r=[poR, recR[ob]], pw=[osR[ob]])
                P.dma("sp", o_d[qs0:qs0 + 512, h * 64:(h + 1) * 64].rearrange("(i p) d -> p i d", p=128), osb[ob][:],
                      r=[osR[ob]], pw=[oR])

        if pre:
            pre()
        head_prologue(0)
        if moba is not None:
            gate_prologue(0, 0)
        side_jobs = list(side) if side else []
        for idx in range(NU + LOOK):
            if idx < NU:
                h, j, g = units[idx]
                if g == 0 and moba is not None and j + 1 < NQT:
                    gate_prologue(h, j + 1)
                if side_jobs and g == 0 and j >= 2:
                    side_jobs.pop(0)()
                if h + 1 < 8 and idx == first_of_head[h] + LOOK + 1:
                    head_prologue(h + 1)
                    if moba is not None:
                        gate_prologue(h + 1, 0)
                front(idx)
            if idx - LOOK >= 0:
                back(idx - LOOK)
        while side_jobs:
            side_jobs.pop(0)()


def rope_table_jobs(E, pos_d, inv_d, C_d, S_d):
    P = E.P
    TWO_PI = float(2 * np.pi)

    def factory(sb2):
        CW = 1024
        posi = sb2("r_posi", [128, CW], I32)
        rr = sb2("r_rr", [128, CW], F32)
        ni = sb2("r_ni", [128, CW], I32)
        nf = sb2("r_nf", [128, CW], F32)
        dst = sb2("r_dst", [128, CW], F32)
        inv = sb2("r_inv", [128, 2], F32)
        posR, rrR, niR, nfR, dstR, invR = Reg("r_posi"), Reg("r_rr"), Reg("r_ni"), Reg("r_nf"), Reg("r_dst"), Reg("r_inv")
        tabR = Reg("ropetab_d")
        pr = slice(64, 96)
        P.dma("sp", inv[:], inv_d, w=[invR])
        jobs = []

        def job(ck, which):
            cs = slice(ck * CW, (ck + 1) * CW)
            shift = 0.0 if which == "S" else 0.25
            if which == "S":
                P.dma("sp", posi[pr, :], pos_d[:, cs], w=[posR])
                P.op("dve", lambda e: e.tensor_copy(rr[pr, :], posi[pr, :]), r=[posR], w=[rrR])
                P.op("dve", lambda e: e.tensor_scalar(rr[pr, :], rr[pr, :], inv[pr, 0:1], 1.0 / TWO_PI, ALU.mult, ALU.mult),
                     r=[rrR, invR], w=[rrR])
            P.op("dve", lambda e: e.tensor_scalar(nf[pr, :], rr[pr, :], shift, None, ALU.add), r=[rrR], w=[nfR])
            P.op("dve", lambda e: e.tensor_copy(ni[pr, :], nf[pr, :]), r=[nfR], w=[niR])
            P.op("dve", lambda e: e.tensor_copy(dst[pr, :], ni[pr, :]), r=[niR], w=[dstR])
            P.op("dve", lambda e: e.tensor_tensor(nf[pr, :], nf[pr, :], dst[pr, :], ALU.subtract), r=[nfR, dstR], w=[nfR])
            P.op("dve", lambda e: e.tensor_scalar(dst[pr, :], nf[pr, :], 0.5, None, ALU.is_gt), r=[nfR], w=[dstR])
            P.op("dve", lambda e: e.tensor_tensor(nf[pr, :], nf[pr, :], dst[pr, :], ALU.subtract), r=[nfR, dstR], w=[nfR])
            P.op("dve", lambda e: e.tensor_scalar(dst[pr, :], nf[pr, :], -0.5, None, ALU.is_lt), r=[nfR], w=[dstR])
            P.op("dve", lambda e: e.tensor_tensor(nf[pr, :], nf[pr, :], dst[pr, :], ALU.add), r=[nfR, dstR], w=[nfR])
            P.op("act", lambda e: e.activation(dst[pr, :], nf[pr, :], AF.Sin, scale=TWO_PI), r=[nfR], w=[dstR])
            if which == "S":
                P.op("dve", lambda e: e.tensor_scalar(dst[pr, :], dst[pr, :], inv[pr, 1:2], None, ALU.mult), r=[dstR, invR], w=[dstR])
            P.dma("sp", (S_d if which == "S" else C_d)[:, cs], dst[pr, :], r=[dstR], pw=[tabR])

        for ck in range(SEQ // CW):
            for which in ("S", "C"):
                jobs.append(lambda ck=ck, which=which: job(ck, which))
        return jobs
    return factory


import math
import ml_dtypes
from concourse.bass_utils import run_bass_kernel_spmd

BF = ml_dtypes.bfloat16


def _pl(v):
    return np.ascontiguousarray(np.asarray(v).reshape(-1, 128).T)


def _dt(nc, n, s, d=F32, k="ExternalInput"):
    return nc.dram_tensor(n, list(s), d, kind=k).ap()


def _finish(P, E):
    P.emit()


def _common_inputs(nc):
    a = {}
    a["wg"] = _dt(nc, "wg", [2, 2, 1024, 2816])
    a["wu"] = _dt(nc, "wu", [2, 2, 1024, 2816])
    a["wd"] = _dt(nc, "wd", [2, 2, 2816, 1024])
    return a


def _t5_bucket_np(n):
    n = np.maximum(np.asarray(n, np.int32), 0)
    nf = np.maximum(n, 1).astype(np.float32)
    large = 16 + (np.log(nf / np.float32(16)) / np.float32(math.log(128 / 16)) * np.float32(16)).astype(np.int32)
    large = np.minimum(large, 31)
    return np.where(n < 16, n, large)


PAIRS = [[0, 1], [2, 3], [4, 5], [6, 7]]


def build_fused():
    nc = bass.Bass("TRN2", target_bir_lowering=False)
    a = _common_inputs(nc)
    xT_d = _dt(nc, "xT", [1024, 2048]); cT = _dt(nc, "cT", [128, 8])
    ada_w = _dt(nc, "ada_w", [2, 1024, 9216]); ada_bT = _dt(nc, "ada_bT", [2, 128, 72]); norm_gT = _dt(nc, "norm_gT", [2, 128, 24])
    kv_ada_w = _dt(nc, "kv_ada_w", [1024, 2048]); kv_ada_bT = _dt(nc, "kv_ada_bT", [128, 16]); kv_norm_gT = _dt(nc, "kv_norm_gT", [128, 8])
    w_qkv = _dt(nc, "w_qkv", [1024, 3072]); mgq = _dt(nc, "mgq", [128, 1]); mgk = _dt(nc, "mgk", [128, 1])
    w_o1 = _dt(nc, "w_o1", [1024, 1024]); w_o2 = _dt(nc, "w_o2", [1024, 1024])
    w_dq = _dt(nc, "w_dq", [1024, 384]); gqa = _dt(nc, "gqa", [128, 3]); w_uq = _dt(nc, "w_uq", [384, 1536]); gq = _dt(nc, "gq", [96, 1])
    w_dkv = _dt(nc, "w_dkv", [1024, 288]); gkva = _dt(nc, "gkva", [128, 2]); wkf = _dt(nc, "wkf", [288, 1536]); gk = _dt(nc, "gk", [96, 1])
    w_uv = _dt(nc, "w_uv", [256, 1024])
    tri = _dt(nc, "tri", [128, 128], BF16); ident = _dt(nc, "ident", [128, 128], BF16)
    mo = dict(onehot_d=_dt(nc, "onehot", [16, 4096], BF16), eligadd_d=_dt(nc, "eligadd", [128, 512]),
              elig01_d=_dt(nc, "elig01", [128, 512]), own01_d=_dt(nc, "own01", [128, 512]),
              rawD_d=_dt(nc, "rawD", [8, 128, 128]), rawE_d=_dt(nc, "rawE", [8, 128, 128]),
              b31_d=_dt(nc, "b31", [128, 8]), ident_d=ident)
    pos_d = _dt(nc, "pos", [32, 4096], I32)
    inv_d = _dt(nc, "inv", [128, 2])
    C_d = nc.dram_tensor("i_ropeC", [32, 4096], F32).ap()
    S_d = nc.dram_tensor("i_ropeS", [32, 4096], F32).ap()
    ro = dict(C_d=C_d, S_d=S_d)
    out = _dt(nc, "outT", [1024, 2048], F32, "ExternalOutput")
    it = lambda n, s: nc.dram_tensor(n, list(s), BF16)
    q1, k1, v1 = it("i_q1", [1024, 2048]), it("i_k1", [1024, 2048]), it("i_v1", [2048, 1024])
    q1g, k1g, v1g = it("i_q1g", [2048, 2048]), it("i_k1g", [2048, 2048]), it("i_v1g", [4096, 1024])
    o1, o1g = it("i_o1", [4096, 512]), it("i_o1g", [8192, 512])
    q2, k2, v2 = it("i_q2", [1536, 2048]), it("i_k2", [1536, 2048]), it("i_v2", [2048, 1024])
    q2g, k2g, v2g = it("i_q2g", [3072, 2048]), it("i_k2g", [3072, 2048]), it("i_v2g", [4096, 1024])
    o2, o2g = it("i_o2", [4096, 512]), it("i_o2g", [8192, 512])

    with ExitStack() as es:
        P = Prog(nc, es)
        E = alloc_common(nc, es, P)
        setup_eps(E)
        parc = {}

        def par(e):
            k = id(e)
            if k not in parc:
                parc[k] = e.snap(e.partition_id() % 2)
            return parc[k]

        def gather(src, dst, names, nch):
            srcR, dstR = Reg(names[0]), Reg(names[1])
            rows = src.shape[0] // nch
            for j in range(nch):
                P.cc("AllGather", PAIRS, src.ap()[j * rows:(j + 1) * rows, :], dst.ap()[j * 2 * rows:(j + 1) * 2 * rows, :],
                     r=[srcR], pw=[dstR])
            return dstR

        def make_exchange(q, k, v, qg, kg, vg, dk, tag, nq, names):
            jj = 1 if dk == 64 else 2
            qs_ = nc.dram_tensor("i_qs" + tag, [2, 8 * dk, 2048], BF16)
            ks_ = nc.dram_tensor("i_ks" + tag, [2, 8 * dk, 2048], BF16)
            vs_ = nc.dram_tensor("i_vs" + tag, [4096, 512], BF16)
            qsR, ksR, vsR = Reg("qsel"), Reg("ksel"), Reg("vsel")

            gRs = {}
            hpc = 16 // nq

            def chunk_gather(src, g_, nm, nch, j):
                rows = src.shape[0] // nch
                P.cc("AllGather", PAIRS, src.ap()[j * rows:(j + 1) * rows, :], g_.ap()[j * 2 * rows:(j + 1) * 2 * rows, :],
                     r=[Reg(nm[0])], pw=[Reg(nm[1])])

            def hqk(src, g_, nm, key):
                def f(h):
                    if (h + 1) % hpc == 0:
                        chunk_gather(src, g_, nm, nq, h // hpc)
                        gRs[key] = Reg(nm[1])
                return f

            def hv(t16):
                if (t16 + 1) % 8 == 0:
                    chunk_gather(v, vg, names[2], 2, t16 // 8)
                    gRs["v"] = Reg(names[2][1])

            def finish():
                for (g_, s_, sR, key) in ((qg, qs_, qsR, "q"), (kg, ks_, ksR, "k")):
                    view = g_.ap().rearrange("(g j t r) n -> g j t r n", g=2, j=jj, t=2)
                    for j in range(jj):
                        P.dma("sp", s_.ap().rearrange("t (j r) n -> j t r n", j=jj)[j],
                              (lambda e, view=view, j=j: view[bass.ds(par(e), 1), j, :, :, :].rearrange("1 t r n -> t r n")),
                              r=[gRs[key]], pw=[sR])
                vview = vg.ap().rearrange("(j t i) (a c) -> j t i a c", j=2, t=2, a=2)
                for j in range(2):
                    P.dma("sp", vs_.ap().rearrange("(t j i) c -> j t i c", t=2, j=2)[j],
                          (lambda e, j=j: vview[j, :, :, bass.ds(par(e), 1), :].rearrange("t i 1 c -> t i c")), r=[gRs["v"]], pw=[vsR])
            hooks = dict(q=hqk(q, qg, names[0], "q"), k=hqk(k, kg, names[1], "k"), v=hv)
            return hooks, (qs_, ks_, vs_, qsR, ksR, vsR), finish

        def attn_loaders(qs_, ks_, vs_, dk, qsR, ksR, vsR, rope):
            qv = qs_.ap().rearrange("t (h d) n -> t h d n", h=8)
            kv = ks_.ap().rearrange("t (h d) n -> t h d n", h=8)
            vv = vs_.ap()

            def mk(view, gR, rows=None, prow=None):
                def f(h, tile, reg):
                    r0, r1 = rows if rows is not None else (0, dk)
                    p0 = prow if prow is not None else r0
                    P.dma("sp", tile[p0:p0 + (r1 - r0), :].rearrange("d (t n) -> d t n", t=2),
                          view[:, h, r0:r1, :].rearrange("t d n -> d t n"), r=[gR(h) if callable(gR) else gR], pw=[reg])
                return f

            def fV(h, tile, reg):
                P.dma("sp", tile[:, :, 0:64], vv[:, h * 64:(h + 1) * 64].rearrange("(g p) d -> p g d", p=128),
                      r=[vsR], pw=[reg])
            ld = dict(K=mk(kv, ksR), Q=mk(qv, qsR), V=fV)
            if rope:
                def sw(view, gR):
                    f1 = mk(view, gR, rows=(80, 96), prow=64)
                    f2 = mk(view, gR, rows=(64, 80), prow=80)
                    return lambda h, tile, reg: (f1(h, tile, reg), f2(h, tile, reg))
                ld["Qs"] = sw(qv, qsR)
                ld["Ks"] = sw(kv, ksR)
            return ld

        def otok_loader(og, ogR, tag):
            os_ = nc.dram_tensor("i_os" + tag, [2, 2048, 512], BF16)
            osR_ = Reg("osel")
            ov = og.ap().rearrange("(j g n) c -> j g n c", j=2, g=2)
            P.dma("sp", os_.ap(), (lambda e: ov[bass.ds(par(e), 1), :, :, :].rearrange("1 g n c -> g n c")), r=[ogR], w=[osR_])

            def f(t16, tile, reg):
                P.dma("sp", tile[:, :].rearrange("n (g c) -> n g c", g=2),
                      os_.ap()[:, t16 * 128:(t16 + 1) * 128, :].rearrange("g n c -> n g c"), r=[osR_], w=[reg])
            return f

        load_xT(E, xT_d)
        phase0_mods(E, cT, ada_w, ada_bT, norm_gT, kv_ada_w, kv_ada_bT, kv_norm_gT)
        ffn_phase(E, a["wg"][0, 0], a["wu"][0, 0], a["wd"][0, 0], 0, 1, 2)
        hooks, sel, fin = make_exchange(q1, k1, v1, q1g, k1g, v1g, 64, "1", 2, (("q_d", "q1g"), ("k_d", "k1g"), ("v_d", "v1g")))
        moba_prep(E, w_qkv, mgq, mgk, q1.ap().rearrange("(h d) n -> h d n", h=16), k1.ap().rearrange("(h d) n -> h d n", h=16),
                  v1.ap(), 3, 4, hooks=hooks)
        attn_phase(E, attn_loaders(sel[0], sel[1], sel[2], 64, sel[3], sel[4], sel[5], False), o1.ap(), 64, 8.0, tri, moba=mo,
                   side=rope_table_jobs(E, pos_d, inv_d, C_d, S_d), pre=fin)
        ogR = gather(o1, o1g, ("a_od", "o1g"), 2)
        wo_phase(E, w_o1, otok_loader(o1g, ogR, "1"), ident, 5)
        ffn_phase(E, a["wg"][0, 1], a["wu"][0, 1], a["wd"][0, 1], 6, 7, 8)
        ffn_phase(E, a["wg"][1, 0], a["wu"][1, 0], a["wd"][1, 0], 9, 10, 11)
        hooks, sel, fin = make_exchange(q2, k2, v2, q2g, k2g, v2g, 96, "2", 4, (("q_d2", "q2g"), ("k_d2", "k2g"), ("v_d2", "v2g")))
        mla_prep(E, w_dq, gqa, w_uq, gq, w_dkv, gkva, wkf, gk, w_uv, q2.ap().rearrange("(h d) n -> h d n", h=16),
                 k2.ap().rearrange("(h d) n -> h d n", h=16), v2.ap(), 12, 13, 18, 19, hooks=hooks)
        attn_phase(E, attn_loaders(sel[0], sel[1], sel[2], 96, sel[3], sel[4], sel[5], True), o2.ap(), 96, float(math.sqrt(96)), tri, rope=ro, pre=fin)
        ogR = gather(o2, o2g, ("a_od", "o2g"), 2)
        wo_phase(E, w_o2, otok_loader(o2g, ogR, "2"), ident, 14)
        ffn_phase(E, a["wg"][1, 1], a["wu"][1, 1], a["wd"][1, 1], 15, 16, 17)
        store_xT(E, out, Reg("outo"))
        P.emit()
    return nc


def kernel(**inp):
    inp = {k: np.asarray(v) for k, v in inp.items()}
    x = inp["x"]
    ada_bT = np.stack([_pl(inp["ada_b"][L]) for L in range(2)])
    norm_gT = np.stack([_pl(inp["norm_g"][L].reshape(-1)) for L in range(2)])
    kk = np.arange(128)[:, None]
    qq = np.arange(128)[None, :]
    tri = (kk <= qq).astype(np.float32).astype(BF)
    bD = _t5_bucket_np(qq - kk)
    bE = _t5_bucket_np(qq - kk + 128)
    rb = inp["rel_bias"]
    onehot = (np.arange(4096)[None, :] // 256 == np.arange(16)[:, None]).astype(np.float32).astype(BF)
    eligadd = np.zeros((8, 4, 16), np.float32); elig01 = np.zeros((8, 4, 16), np.float32); own01 = np.zeros((8, 4, 16), np.float32)
    for j in range(8):
        for ip in range(4):
            qb = (4 * j + ip) // 2
            n = np.arange(16)
            eligadd[j, ip] = np.where(n < qb, 0.0, -1e30)
            elig01[j, ip] = (n < qb)
            own01[j, ip] = (n == qb)
    rep = lambda t: np.ascontiguousarray(np.broadcast_to(t.reshape(1, -1), (128, 512))).astype(np.float32)
    ident = np.eye(128, dtype=np.float32).astype(BF)
    wkf = np.zeros((288, 16, 96), np.float32)
    wkf[0:256, :, 0:64] = inp["w_uk"].reshape(256, 16, 64)
    wkf[256:288, :, 64:96] = np.eye(32, dtype=np.float32)[:, None, :]
    wkf = wkf.reshape(288, 1536)
    invf = (np.float32(10000.0) ** (-np.arange(16, dtype=np.float32) / np.float32(16))).astype(np.float32)
    inv = np.zeros((128, 2), np.float32)
    inv[64:96, 0] = np.tile(invf, 2)
    inv[64:80, 1] = -1.0
    inv[80:96, 1] = 1.0
    shared = dict(wg=inp["ffn_w_gate"], wu=inp["ffn_w_up"], wd=inp["ffn_w_down"], ada_w=inp["ada_w"], ada_bT=ada_bT,
                  norm_gT=norm_gT, kv_ada_w=inp["kv_ada_w"], kv_ada_bT=_pl(inp["kv_ada_b"]), kv_norm_gT=_pl(inp["kv_norm_g"]),
                  w_qkv=inp["moba_w_qkv"][0],
                  mgq=np.tile(inp["moba_q_g"][0], 2).reshape(128, 1).astype(np.float32),
                  mgk=np.tile(inp["moba_k_g"][0], 2).reshape(128, 1).astype(np.float32),
                  w_o1=inp["moba_w_o"][0], w_o2=inp["mla_w_o"][0],
                  w_dq=inp["mla_w_dq"][0], gqa=_pl(inp["mla_q_a_norm_g"][0]), w_uq=inp["mla_w_uq"][0],
                  gq=inp["mla_q_g"][0].reshape(96, 1).astype(np.float32), w_dkv=inp["w_dkv"], gkva=_pl(inp["kv_a_norm_g"]),
                  wkf=wkf, gk=inp["mla_k_g"].reshape(96, 1).astype(np.float32), w_uv=inp["w_uv"],
                  tri=tri, ident=ident, onehot=onehot, eligadd=rep(eligadd), elig01=rep(elig01), own01=rep(own01), inv=inv)
    maps = []
    for b in range(4):
        for c in range(2):
            hs = slice(c * 8, (c + 1) * 8)
            m = dict(shared)
            m.update(xT=np.ascontiguousarray(x[b, c * 2048:(c + 1) * 2048].T), cT=_pl(inp["c"][b]),
                     rawD=np.ascontiguousarray(np.transpose(rb[bD][:, :, hs], (2, 0, 1))).astype(np.float32),
                     rawE=np.ascontiguousarray(np.transpose(rb[bE][:, :, hs], (2, 0, 1))).astype(np.float32),
                     b31=np.ascontiguousarray(np.broadcast_to(rb[31, hs].reshape(1, 8), (128, 8))).astype(np.float32),
                     pos=np.ascontiguousarray(np.broadcast_to(inp["positions"][b].reshape(1, 4096), (32, 4096))).astype(np.int32))
            maps.append(m)
    res = run_bass_kernel_spmd(build_fused(), maps, core_ids=list(range(8))).results
    out = np.zeros((4, 4096, 1024), np.float32)
    for i in range(8):
        b, c = divmod(i, 2)
        out[b, c * 2048:(c + 1) * 2048] = np.asarray(res[i]["outT"]).T
    return out
```
